# Optimizing a Trainium2 kernel written in Bass

```python
import math
import jax, jax.numpy as jnp
from jax import lax
import numpy as np

D_MODEL = 4096
BATCH = 4
SEQ = 2048
DEPTH = 1

GRID_W = 64
CTX_LEN = 256
HEAD_DIM = 128
D_ATT = D_MODEL // 2
N_Q_HEADS = D_ATT // HEAD_DIM
N_KV_HEADS = N_Q_HEADS // 4
Q_PER_KV = N_Q_HEADS // N_KV_HEADS
D_KV = N_KV_HEADS * HEAD_DIM
D_LRU = D_MODEL // 2
LRU_BLOCKS = 16
LRU_BLOCK_DIM = D_LRU // LRU_BLOCKS
CONV_WIDTH = 4
CONV_PAD = (2, 1)
LRU_C = 8.0
ROPE_THETA = 10000.0
Q_BLOCK = 128
D_MIX = D_ATT + D_LRU
D_IN = D_ATT + 2 * D_KV + D_ATT + D_LRU + D_LRU
SPLITS = (D_ATT, D_ATT + D_KV, D_ATT + 2 * D_KV, 2 * D_ATT + 2 * D_KV, 2 * D_ATT + 2 * D_KV + D_LRU)
EPS = 1e-6

kernel_name = "hymba_rglru_gqa_dit_layer"


def rmsnorm(x, w):
    xf = x.astype(jnp.float32)
    y = xf * lax.rsqrt(jnp.mean(xf * xf, axis=-1, keepdims=True) + EPS)
    return (y * w.astype(jnp.float32)).astype(x.dtype)


def rope_tables(n_tokens):
    rows = n_tokens // GRID_W
    row = jnp.repeat(jnp.arange(rows), GRID_W).astype(jnp.float32)
    col = jnp.tile(jnp.arange(GRID_W), rows).astype(jnp.float32)
    n_freq = HEAD_DIM // 4
    freqs = ROPE_THETA ** (-jnp.arange(n_freq, dtype=jnp.float32) / n_freq)
    ang = jnp.stack([row[:, None] * freqs, col[:, None] * freqs], axis=1)
    return jnp.cos(ang)[:, None], jnp.sin(ang)[:, None]


def apply_rope(x, cos, sin):
    B, S, H, _ = x.shape
    xr = x.astype(jnp.float32).reshape(B, S, H, 2, 2, HEAD_DIM // 4)
    x1, x2 = xr[..., 0, :], xr[..., 1, :]
    out = jnp.stack([x1 * cos - x2 * sin, x2 * cos + x1 * sin], axis=-2)
    return out.reshape(B, S, H, HEAD_DIM).astype(x.dtype)


def modulation(cvec, w_ada_l, b_ada_l):
    mod = jax.nn.silu(cvec) @ w_ada_l + b_ada_l
    return jnp.split(mod, 3, axis=-1)


def qkv_heads(q, k, v, qw, kw):
    B, T = q.shape[:2]
    q = rmsnorm(q.reshape(B, T, N_Q_HEADS, HEAD_DIM), qw)
    k = rmsnorm(k.reshape(B, T, N_KV_HEADS, HEAD_DIM), kw)
    v = v.reshape(B, T, N_KV_HEADS, HEAD_DIM)
    return q, k, v


def gqa_attend(q_blk, k, v):
    s = jnp.einsum('bqkgd,bskd->bkgqs', q_blk, k).astype(jnp.float32) * (HEAD_DIM ** -0.5)
    p = jax.nn.softmax(s, axis=-1).astype(v.dtype)
    return jnp.einsum('bkgqs,bskd->bqkgd', p, v)


def latent_attention(q, k_all, v_all):
    B, S = q.shape[:2]
    n_blk = S // Q_BLOCK
    qb = q.reshape(B, n_blk, Q_BLOCK, N_KV_HEADS, Q_PER_KV, HEAD_DIM).transpose(1, 0, 2, 3, 4, 5)
    out = lax.map(lambda q_blk: gqa_attend(q_blk, k_all, v_all), qb)
    return out.transpose(1, 0, 2, 3, 4, 5).reshape(B, S, D_ATT)


def short_conv(x, w, b):
    y = lax.conv_general_dilated(x, w[:, None, :].astype(x.dtype), window_strides=(1,),
                                 padding=[CONV_PAD], dimension_numbers=('NWC', 'WIO', 'NWC'),
                                 feature_group_count=x.shape[-1])
    return y + b.astype(x.dtype)


def rglru_coeffs(x, wa, ba, wx, bx, lam):
    xb = x.reshape(*x.shape[:-1], LRU_BLOCKS, LRU_BLOCK_DIM)
    r = jax.nn.sigmoid(jnp.einsum('btni,nij->btnj', xb, wa.astype(jnp.float32)).reshape(x.shape) + ba.astype(jnp.float32))
    i = jax.nn.sigmoid(jnp.einsum('btni,nij->btnj', xb, wx.astype(jnp.float32)).reshape(x.shape) + bx.astype(jnp.float32))
    log_a = -LRU_C * r * jax.nn.softplus(-lam.astype(jnp.float32))
    a = jnp.exp(log_a)
    mult = jnp.sqrt(-jnp.expm1(2.0 * log_a))
    return a, mult * (i * x)


def _lin_combine(e1, e2):
    a1, b1 = e1
    a2, b2 = e2
    return a1 * a2, a2 * b1 + b2


def linear_scan(a, b, reverse):
    return lax.associative_scan(_lin_combine, (a, b), axis=1, reverse=reverse)


def setup_inputs(seed: int = 0) -> dict:
    key = jax.random.key(seed)
    ks = jax.random.split(key, 20)
    f32 = jnp.float32
    nrm = lambda k, shape, s: jax.random.normal(k, shape, f32) * s
    a0 = jax.random.uniform(ks[16], (DEPTH, 2, D_LRU), f32, 0.9, 0.999)
    a_base = a0 ** (1.0 / LRU_C)
    lam = jnp.log(a_base) - jnp.log1p(-a_base)
    return {
        "x": nrm(ks[0], (BATCH, SEQ, D_MODEL), 1.0),
        "c": nrm(ks[1], (BATCH, D_MODEL), 1.0),
        "ctx": nrm(ks[2], (BATCH, CTX_LEN, D_MODEL), 1.0),
        "c_ctx": nrm(ks[3], (D_MODEL,), 1.0),
        "w_ada": nrm(ks[4], (DEPTH, D_MODEL, 3 * D_MODEL), D_MODEL ** -0.5),
        "b_ada": nrm(ks[5], (DEPTH, 3 * D_MODEL), 0.01),
        "norm_w": 1.0 + nrm(ks[6], (DEPTH, D_MODEL), 0.01),
        "w_in": nrm(ks[7], (DEPTH, D_MODEL, D_IN), D_MODEL ** -0.5),
        "q_norm_w": 1.0 + nrm(ks[8], (DEPTH, HEAD_DIM), 0.01),
        "k_norm_w": 1.0 + nrm(ks[9], (DEPTH, HEAD_DIM), 0.01),
        "conv_w": nrm(ks[10], (DEPTH, CONV_WIDTH, D_LRU), CONV_WIDTH ** -0.5),
        "conv_b": nrm(ks[11], (DEPTH, D_LRU), 0.01),
        "lru_wa": nrm(ks[12], (DEPTH, 2, LRU_BLOCKS, LRU_BLOCK_DIM, LRU_BLOCK_DIM), LRU_BLOCK_DIM ** -0.5),
        "lru_ba": nrm(ks[13], (DEPTH, 2, D_LRU), 0.01),
        "lru_wx": nrm(ks[14], (DEPTH, 2, LRU_BLOCKS, LRU_BLOCK_DIM, LRU_BLOCK_DIM), LRU_BLOCK_DIM ** -0.5),
        "lru_bx": nrm(ks[15], (DEPTH, 2, D_LRU), 0.01),
        "lru_lambda": lam,
        "out_norm_att": 1.0 + nrm(ks[17], (DEPTH, D_ATT), 0.01),
        "out_norm_lru": 1.0 + nrm(ks[18], (DEPTH, D_LRU), 0.01),
        "w_out": nrm(ks[19], (DEPTH, D_MIX, D_MODEL), D_MIX ** -0.5),
    }


def reference(x, c, ctx, c_ctx, w_ada, b_ada, norm_w, w_in, q_norm_w, k_norm_w, conv_w, conv_b,
              lru_wa, lru_ba, lru_wx, lru_bx, lru_lambda, out_norm_att, out_norm_lru, w_out):
    B, S, _ = x.shape
    C = ctx.shape[1]
    cos, sin = rope_tables(S)
    for l in range(DEPTH):
        last = l + 1 == DEPTH
        shift_x, scale_x, gate_x = modulation(c, w_ada[l], b_ada[l])
        shift_c, scale_c, gate_c = modulation(c_ctx, w_ada[l], b_ada[l])
        h_x = rmsnorm(x, norm_w[l]) * (1.0 + scale_x[:, None]) + shift_x[:, None]
        h_c = rmsnorm(ctx, norm_w[l]) * (1.0 + scale_c) + shift_c
        q_x, k_x, v_x, ga_x, xl_x, gl_x = jnp.split(h_x @ w_in[l], SPLITS, axis=-1)
        q_c, k_c, v_c, ga_c, xl_c, gl_c = jnp.split(h_c @ w_in[l], SPLITS, axis=-1)

        q_x, k_x, v_x = qkv_heads(q_x, k_x, v_x, q_norm_w[l], k_norm_w[l])
        q_c, k_c, v_c = qkv_heads(q_c, k_c, v_c, q_norm_w[l], k_norm_w[l])
        q_x = apply_rope(q_x, cos, sin)
        k_x = apply_rope(k_x, cos, sin)
        k_all = jnp.concatenate([k_x, k_c], axis=1)
        v_all = jnp.concatenate([v_x, v_c], axis=1)
        att_x = latent_attention(q_x, k_all, v_all)

        u_x = short_conv(xl_x, conv_w[l], conv_b[l]).astype(jnp.float32)
        u_c = short_conv(xl_c, conv_w[l], conv_b[l]).astype(jnp.float32)
        lru_x = jnp.zeros(u_x.shape, jnp.float32)
        lru_c = jnp.zeros(u_c.shape, jnp.float32)
        for d in range(2):
            rev = d == 1
            p = (lru_wa[l, d], lru_ba[l, d], lru_wx[l, d], lru_bx[l, d], lru_lambda[l, d])
            a_c, b_c = rglru_coeffs(u_c, *p)
            _, h_c_states = linear_scan(a_c, b_c, rev)
            h0 = h_c_states[:, 0] if rev else h_c_states[:, -1]
            a_l, b_l = rglru_coeffs(u_x, *p)
            a_cum, h_l = linear_scan(a_l, b_l, rev)
            lru_x = lru_x + h_l + a_cum * h0[:, None]
            lru_c = lru_c + h_c_states
        lru_x = lru_x.astype(x.dtype)

        mix_x = jnp.concatenate([rmsnorm(att_x, out_norm_att[l]) * jax.nn.silu(ga_x),
                                 rmsnorm(lru_x, out_norm_lru[l]) * jax.nn.silu(gl_x)], axis=-1)
        x_new = x + gate_x[:, None] * (mix_x @ w_out[l])

        if not last:
            qc = q_c.reshape(B, C, N_KV_HEADS, Q_PER_KV, HEAD_DIM)
            att_c = gqa_attend(qc, k_c, v_c).reshape(B, C, D_ATT)
            mix_c = jnp.concatenate([rmsnorm(att_c, out_norm_att[l]) * jax.nn.silu(ga_c),
                                     rmsnorm(lru_c.astype(ctx.dtype), out_norm_lru[l]) * jax.nn.silu(gl_c)], axis=-1)
            ctx = ctx + gate_c * (mix_c @ w_out[l])
        x = x_new
    return x
```

```python
import numpy as np
import ml_dtypes
from contextlib import ExitStack
import concourse.bass as bass
import concourse.mybir as mybir
from concourse.bass_utils import run_bass_kernel_spmd

F32 = mybir.dt.float32
BF16 = mybir.dt.bfloat16
AF = mybir.ActivationFunctionType
ALU = mybir.AluOpType

D = 4096
KC = 32
NA = 1024
NH = 128
NB = 1024
NCX = 256
T1 = NH + NB + NCX
NKEY = NA + NB + NCX
DIN = 9216
C_Q, C_K, C_V, C_GA, C_XL, C_GL = 0, 2048, 2560, 3072, 5120, 7168
EPS = 1e-6
ARENA_F32 = 53000
GRAN = 256

P_NW = 0
P_CC = 32
P_QW = 96
P_KW = 97
P_W5 = 98
P_CB = 178
P_BA = 194
P_BX = 226
P_LAM = 258
P_WNA = 290
P_WNL = 306
P_ID2 = 322
P_COS = 324
P_SIN = 388
NPRM = 452


class Buf:
    def __init__(self, ar, off, n):
        self.ar, self.off, self.n = ar, off, n

    def f32(self):
        return self.ar[:, self.off // 4:(self.off + self.n) // 4]

    def bf(self):
        return self.f32().bitcast(BF16)

    def sub(self, o, n):
        return Buf(self.ar, self.off + o, n)

    def keys(self):
        return list(range(self.off // GRAN, (self.off + self.n - 1) // GRAN + 1))


class Ring:
    def __init__(self, ar, start, end):
        self.ar, self.start, self.end, self.p = ar, start, end, start

    def alloc(self, n):
        n = (n + GRAN - 1) // GRAN * GRAN
        assert n <= self.end - self.start, (n, self.start, self.end)
        if self.p + n > self.end:
            self.p = self.start
        b = Buf(self.ar, self.p, n)
        self.p += n
        return b


class Rot:
    def __init__(self, items):
        self.items, self.i = list(items), 0

    def next(self):
        v = self.items[self.i % len(self.items)]
        self.i += 1
        return v


class TR:
    def __init__(self, nc, es, n_sp=20, n_pool=10):
        self.nc = nc
        self.E = {}
        self.sems = {}
        for n, attr in (('pe', 'tensor'), ('act', 'scalar'), ('dve', 'vector'), ('pool', 'gpsimd'), ('sp', 'sync')):
            sem = es.enter_context(nc.semaphore("c_" + n))
            self.E[n] = dict(h=getattr(nc, attr), sem=sem, key="c_" + n, count=0, known={}, pend=False)
            self.sems["c_" + n] = sem
        self.dsl = {}
        for q, n in (('sp', n_sp), ('pool', n_pool)):
            sl = []
            for i in range(n):
                key = "d_%s%d" % (q, i)
                sem = es.enter_context(nc.semaphore(key))
                self.sems[key] = sem
                sl.append(dict(key=key, sem=sem, val=0))
            self.dsl[q] = dict(slots=sl, nxt=0)
        self.lastw = {}
        self.readers = {}
        self.nwait = 0

    def _wait(self, E, reads, writes, excl=()):
        need = {}
        strict = E['key'] != 'c_pe'

        def add(k, v, same_ok):
            if k == E['key'] and same_ok:
                return
            if need.get(k, 0) < v:
                need[k] = v
        for r in reads:
            t = self.lastw.get(r)
            if t is not None:
                add(t[0], t[1], False)
        for w in writes:
            t = self.lastw.get(w)
            if t is not None:
                add(t[0], t[1], not strict)
            rd = self.readers.get(w)
            if rd:
                for k, v in rd.items():
                    add(k, v, not strict)
        for w in excl:
            t = self.lastw.get(w)
            if t is not None:
                add(t[0], t[1], True)
            rd = self.readers.get(w)
            if rd:
                for k, v in rd.items():
                    add(k, v, True)
        for k, v in need.items():
            if E['known'].get(k, 0) < v:
                E['h'].wait_ge(self.sems[k], v)
                E['known'][k] = v
                self.nwait += 1

    def _record(self, tok, reads, writes):
        for w in writes:
            self.lastw[w] = tok
            self.readers[w] = {}
        for r in reads:
            d = self.readers.setdefault(r, {})
            if d.get(tok[0], 0) < tok[1]:
                d[tok[0]] = tok[1]

    def op(self, eng, fn, reads=(), writes=(), inc=True):
        E = self.E[eng]
        excl = [k for k in reads if isinstance(k, tuple) and k not in writes]
        self._wait(E, reads, writes, excl)
        writes = list(writes) + excl
        ins = fn()
        if inc:
            E['count'] += 1
            ins.then_inc(E['sem'], 1)
            tok = (E['key'], E['count'])
            E['pend'] = False
        else:
            tok = (E['key'], E['count'] + 1)
            E['pend'] = True
        self._record(tok, reads, writes)
        return tok

    def dma(self, q, out, in_, reads=(), writes=()):
        E = self.E[q]
        self._wait(E, reads, writes)
        d = self.dsl[q]
        sl = d['slots'][d['nxt'] % len(d['slots'])]
        d['nxt'] += 1
        if sl['val'] > 0 and E['known'].get(sl['key'], 0) < sl['val']:
            E['h'].wait_ge(sl['sem'], sl['val'])
            E['known'][sl['key']] = sl['val']
        ins = E['h'].dma_start(out=out, in_=in_)
        sl['val'] += 16
        ins.then_inc(sl['sem'], 16)
        tok = (sl['key'], sl['val'])
        self._record(tok, reads, writes)
        return tok

    def barrier(self, engines=('pe', 'act', 'dve', 'pool', 'sp')):
        for e in self.E.values():
            assert not e['pend'], "pending un-inc'd op at barrier"
        tgt = {}
        for e in self.E.values():
            if e['count'] > 0:
                tgt[e['key']] = e['count']
        for d in self.dsl.values():
            for sl in d['slots']:
                if sl['val'] > 0:
                    tgt[sl['key']] = sl['val']
        for n in engines:
            E = self.E[n]
            for k, v in tgt.items():
                if k == E['key']:
                    continue
                if E['known'].get(k, 0) < v:
                    E['h'].wait_ge(self.sems[k], v)
                    E['known'][k] = v


def build_nc(stop_after=None, dbg=False):
    nc = bass.Bass("TRN2", target_bir_lowering=False)
    xs_d = nc.dram_tensor("xs", [NA + NB, D], F32, kind="ExternalInput").ap()
    cx_d = nc.dram_tensor("cx", [NCX, D], F32, kind="ExternalInput").ap()
    wada_d = nc.dram_tensor("w_ada", [D, 3 * D], F32, kind="ExternalInput").ap()
    win_d = nc.dram_tensor("w_in", [D, DIN], F32, kind="ExternalInput").ap()
    wout_d = nc.dram_tensor("w_out", [D, D], F32, kind="ExternalInput").ap()
    prm_d = nc.dram_tensor("prm", [128, NPRM], F32, kind="ExternalInput").ap()
    bada_d = nc.dram_tensor("bada2", [2, 3 * D], F32, kind="ExternalInput").ap()
    cst_d = nc.dram_tensor("cst", [128, 384], BF16, kind="ExternalInput").ap()
    cs_d = nc.dram_tensor("cs", [128, 4096], BF16, kind="ExternalInput").ap()
    idf_d = nc.dram_tensor("idf", [128, 128], F32, kind="ExternalInput").ap()
    wg_d = nc.dram_tensor("wg", [128, 16 * 512], F32, kind="ExternalInput").ap()
    out_d = nc.dram_tensor("out", [NA, D], F32, kind="ExternalOutput").ap()
    gate_d = nc.dram_tensor("gate_row", [1, D], F32, kind="Internal").ap()
    ua_d = nc.dram_tensor("ua_scr", [128, 16 * NA], BF16, kind="Internal").ap()
    dbg_outs = {}

    es = ExitStack()
    with es:
        ar = es.enter_context(nc.sbuf_tensor("arena", [128, ARENA_F32], F32))
        psum = es.enter_context(nc.psum_tensor("psum", [128, 8, 512], F32))
        T = TR(nc, es)
        V, S, G, PE = nc.vector, nc.scalar, nc.gpsimd, nc.tensor

        def PS(b):
            return psum[:, b, :]

        def PSK(b):
            return [("ps", b)]

        SM = Buf(ar, 0, 10240)
        sm_p = [0]

        def small(nbytes):
            n = (nbytes + 31) // 32 * 32
            b = SM.sub(sm_p[0], nbytes)
            sm_p[0] += n
            assert sm_p[0] <= SM.n
            return b
        B_cst = small(768)
        B_prm = small(NPRM * 4)
        B_cs = Buf(ar, 10240, 8192)
        WB = Ring(ar, 18432, 18432 + 24576)
        WG = Ring(ar, 43008, 43008 + 2048)
        BASE = 45056
        TOP = 211968
        KV0 = 175104

        cst = B_cst.bf()
        ident = cst[:, 0:128]
        ones = cst[:, 128:256]
        rmat = cst[:, 256:384]
        prm = B_prm.f32()
        cs = B_cs.bf()

        def pcol(c, n=1):
            return prm[:, c:c + n]

        T.dma('sp', cst, cst_d[:, :], writes=B_cst.keys())
        T.dma('sp', prm, prm_d[:, :], writes=B_prm.keys())
        T.dma('sp', cs, cs_d[:, :], writes=B_cs.keys())

        b_idf = small(512)
        T.dma('sp', b_idf.f32(), idf_d[:, :], writes=b_idf.keys())
        b_eps = small(4)
        b_gb = small(2048)
        b_grow = small(2048)
        b_one = small(4)
        b_mh = small(4)
        b_th = small(256)
        b_s1 = small(256)
        b_csb = small(128)
        b_modT = small(512)
        b_gx = small(128)
        b_gc = small(128)
        b_qws = small(4)
        b_c05 = small(128)
        b_ba05 = small(128)
        b_bx05 = small(128)
        b_wa05 = small(64)
        b_wl05 = small(64)
        b_stF = small(64)
        b_stB = small(64)
        b_xlB2 = small(128)
        b_ss = small(96)
        b_ms = small(96)
        b_rstd = small(96)
        b_tmpc = small(128)

        T.op('dve', lambda: V.memset(b_eps.f32(), EPS), writes=b_eps.keys())
        T.op('dve', lambda: V.memset(b_one.f32(), 1.0), writes=b_one.keys())
        T.op('dve', lambda: V.memset(b_mh.f32(), -0.5), writes=b_mh.keys())

        cc = pcol(P_CC, 64)
        T.op('act', lambda: S.activation(out=b_th.f32(), in_=cc, func=AF.Tanh, scale=0.5),
             reads=B_prm.keys(), writes=b_th.keys())
        T.op('dve', lambda: V.scalar_tensor_tensor(out=b_s1.f32(), in0=b_th.f32(), scalar=1.0, in1=cc,
                                                   op0=ALU.add, op1=ALU.mult),
             reads=b_th.keys() + B_prm.keys(), writes=b_s1.keys())
        T.op('dve', lambda: V.tensor_scalar(out=b_csb.bf(), in0=b_s1.f32(), scalar1=0.5, scalar2=None, op0=ALU.mult),
             reads=b_s1.keys(), writes=b_csb.keys())
        csb = b_csb.bf()
        T.op('dve', lambda: V.tensor_scalar(out=b_qws.f32(), in0=pcol(P_QW), scalar1=float(128 ** -0.5), scalar2=None,
                                            op0=ALU.mult), reads=B_prm.keys(), writes=b_qws.keys())
        T.op('dve', lambda: V.tensor_scalar(out=b_ba05.f32(), in0=pcol(P_BA, 32), scalar1=0.5, scalar2=None, op0=ALU.mult),
             reads=B_prm.keys(), writes=b_ba05.keys())
        T.op('dve', lambda: V.tensor_scalar(out=b_bx05.f32(), in0=pcol(P_BX, 32), scalar1=0.5, scalar2=None, op0=ALU.mult),
             reads=B_prm.keys(), writes=b_bx05.keys())
        T.op('dve', lambda: V.tensor_scalar(out=b_wa05.f32(), in0=pcol(P_WNA, 16), scalar1=0.5, scalar2=None, op0=ALU.mult),
             reads=B_prm.keys(), writes=b_wa05.keys())
        T.op('dve', lambda: V.tensor_scalar(out=b_wl05.f32(), in0=pcol(P_WNL, 16), scalar1=0.5, scalar2=None, op0=ALU.mult),
             reads=B_prm.keys(), writes=b_wl05.keys())
        T.op('act', lambda: S.activation(out=b_tmpc.f32(), in_=pcol(P_LAM, 32), func=AF.Exp, scale=-1.0),
             reads=B_prm.keys(), writes=b_tmpc.keys())
        T.op('act', lambda: S.activation(out=b_tmpc.f32(), in_=b_tmpc.f32(), func=AF.Ln, bias=b_one.f32(), scale=1.0),
             reads=b_tmpc.keys() + b_one.keys(), writes=b_tmpc.keys())
        T.op('dve', lambda: V.tensor_scalar(out=b_c05.f32(), in0=b_tmpc.f32(), scalar1=-4.0, scalar2=None, op0=ALU.mult),
             reads=b_tmpc.keys(), writes=b_c05.keys())

        def dump(name, ap, shape, dt=F32, reads=()):
            if not dbg:
                return
            d = nc.dram_tensor("dbg_" + name, list(shape), dt, kind="ExternalOutput").ap()
            dbg_outs[name] = d
            T.dma('sp', d, ap, reads=list(reads))

        def finish():
            T.barrier(engines=('sp',))

        wsched = []
        for g in range(4):
            wsched.append(('in', C_K + g * 128))
        for j in range(4):
            wsched.append(('in', C_V + j * 128))
        wsched.append(('in', C_XL))
        for n in range(16):
            if n + 1 < 16:
                wsched.append(('in', C_XL + (n + 1) * 128))
            wsched.append(('ada', 2 * D + (2 * n) * 128))
            wsched.append(('ada', 2 * D + (2 * n + 1) * 128))
        for g in range(4):
            wsched.append(('in', C_K + g * 128))
        for j in range(4):
            wsched.append(('in', C_V + j * 128))
        for h in range(16):
            wsched.append(('in', C_Q + h * 128))
            wsched.append(('in', C_GA + h * 128))
        for n in range(16):
            wsched.append(('in', C_XL + n * 128))
            wsched.append(('in', C_GL + n * 128))
        wstate = dict(issued=0, used=0, tiles={})

        def w_issue():
            i = wstate['issued']
            if i >= len(wsched):
                return
            kind, col = wsched[i]
            b = WB.alloc(8192)
            src = (win_d if kind == 'in' else wada_d)[:, col:col + 128].rearrange("(kc p) n -> p kc n", p=128)
            dst = b.bf().rearrange("p (kc n) -> p kc n", kc=KC)
            T.dma('pool', dst, src, writes=b.keys())
            wstate['tiles'][i] = b
            wstate['issued'] += 1

        def w_next(expect_col, kind='in'):
            i = wstate['used']
            assert wsched[i] == (kind, expect_col), (i, wsched[i], expect_col)
            while wstate['issued'] < min(i + 3, len(wsched)):
                w_issue()
            b = wstate['tiles'].pop(i)
            wstate['used'] += 1
            return b

        R0 = Ring(ar, BASE, BASE + 2 * 32768 + 16384)
        tp_bank = 7
        modps = PS(tp_bank)
        for cg in range(16):
            wt = R0.alloc(32768)
            wv = wt.bf().rearrange("p (kc n) -> p kc n", kc=KC)
            for q4 in range(4):
                src = wada_d[q4 * 1024:(q4 + 1) * 1024, cg * 512:(cg + 1) * 512].rearrange("(kc p) n -> p kc n", p=128)
                T.dma('pool', wv[:, q4 * 8:(q4 + 1) * 8, :], src, writes=wt.sub(q4 * 8192, 8192).keys())
            bb = R0.alloc(2048)
            T.dma('sp', bb.f32()[0:2, :], bada_d[:, cg * 512:(cg + 1) * 512], writes=bb.keys())
            bank = cg % 2
            for kc in range(KC):
                T.op('pe', lambda kc=kc: PE.matmul(PS(bank)[0:2, :], lhsT=csb[:, 2 * kc:2 * kc + 2], rhs=wv[:, kc, :],
                                                   start=(kc == 0), stop=(kc == KC - 1)),
                     reads=b_csb.keys() + wt.sub(kc * 1024, 1024).keys(), writes=PSK(bank), inc=(kc == KC - 1))
            row = R0.alloc(2048)
            T.op('dve', lambda: V.tensor_tensor(out=row.f32()[0:2, :], in0=PS(bank)[0:2, :], in1=bb.f32()[0:2, :], op=ALU.add),
                 reads=PSK(bank) + bb.keys(), writes=row.keys())
            if cg < 16:
                for i in range(4):
                    j = cg * 4 + i
                    T.op('pe', lambda i=i, j=j: PE.transpose(out=modps[:, 2 * j:2 * j + 2], in_=row.f32()[0:2, i * 128:(i + 1) * 128],
                                                             identity=prm[0:2, P_ID2:P_ID2 + 2]),
                         reads=row.keys() + B_prm.keys(), writes=PSK(tp_bank), inc=(i == 3))
            else:
                T.dma('sp', gate_d[0:1, (cg - 16) * 512:(cg - 15) * 512], row.f32()[0:1, :], reads=row.keys(), writes=["gate_row"])
        T.op('dve', lambda: V.tensor_copy(out=b_modT.f32(), in_=modps[:, 0:128]), reads=PSK(tp_bank), writes=b_modT.keys())
        modR = b_modT.f32().rearrange("p (j r) -> p r j", r=2)
        nw = pcol(P_NW, 32)
        T.op('dve', lambda: V.scalar_tensor_tensor(out=b_gx.f32(), in0=modR[:, 0:1, 32:64].rearrange("p o j -> p (o j)"), scalar=1.0, in1=nw, op0=ALU.add, op1=ALU.mult),
             reads=b_modT.keys() + B_prm.keys(), writes=b_gx.keys())
        T.op('dve', lambda: V.scalar_tensor_tensor(out=b_gc.f32(), in0=modR[:, 1:2, 32:64].rearrange("p o j -> p (o j)"), scalar=1.0, in1=nw, op0=ALU.add, op1=ALU.mult),
             reads=b_modT.keys() + B_prm.keys(), writes=b_gc.keys())

        def gsel(isctx, kc):
            g = (b_gc if isctx else b_gx).f32()[:, kc:kc + 1]
            r = 1 if isctx else 0
            s = b_modT.f32()[:, kc * 2 + r:kc * 2 + r + 1]
            return g, s
        if dbg:
            dump("modT", b_modT.f32(), [128, 128], reads=b_modT.keys())
            dump("gx", b_gx.f32(), [128, 32], reads=b_gx.keys())
            dump("c05", b_c05.f32(), [128, 32], reads=b_c05.keys())
        if stop_after == 0:
            finish()
            return nc, dbg_outs
        T.barrier()

        def prep(groups, hT, RP):
            hv = hT.bf().rearrange("p (kc t) -> p kc t", kc=KC)
            ntok = hT.n // 2 // KC
            tpb = Rot([4, 5, 6, 7])
            junk = Buf(ar, RP.end, 8192)
            for grp in groups:
                xss = []
                for (rows, isctx, si, tok0) in grp:
                    xt = RP.alloc(16384)
                    T.dma('sp', xt.f32(), rows, writes=xt.keys())
                    ssa_ = b_ss.f32()[:, si:si + 1]
                    T.op('act', lambda xt=xt, junk=junk, ssa_=ssa_: S.activation(out=junk.bf(), in_=xt.f32(), func=AF.Square, accum_out=ssa_),
                         reads=xt.keys(), writes=junk.keys() + b_ss.keys())
                    msa = b_ms.f32()[:, si:si + 1]
                    T.op('dve', lambda ssa_=ssa_, msa=msa: V.tensor_scalar(out=msa, in0=ssa_, scalar1=1.0 / D, scalar2=EPS, op0=ALU.mult, op1=ALU.add),
                         reads=b_ss.keys(), writes=b_ms.keys())
                    rsa = b_rstd.f32()[:, si:si + 1]
                    T.op('pool', lambda msa=msa, rsa=rsa: G.tensor_tensor(out=rsa, in0=msa, in1=b_mh.f32(), op=ALU.pow),
                         reads=b_ms.keys() + b_mh.keys(), writes=b_rstd.keys())
                    dg = RP.alloc(512)
                    T.op('dve', lambda dg=dg, rsa=rsa: V.tensor_scalar(out=dg.f32(), in0=b_idf.f32(), scalar1=rsa, scalar2=None, op0=ALU.mult),
                         reads=b_rstd.keys() + b_idf.keys(), writes=dg.keys())
                    xss.append((xt, dg, isctx, tok0))
                ng = len(xss)
                tok0 = xss[0][3]
                isctx = xss[0][2]
                for k2 in range(KC // 2):
                    bank = tpb.next()
                    pb = PS(bank)
                    for kk in range(2):
                        kc = k2 * 2 + kk
                        for j, (xt, dg, _, _) in enumerate(xss):
                            T.op('pe', lambda kk=kk, kc=kc, j=j, xt=xt, dg=dg, pb=pb: PE.matmul(
                                pb[:, kk * 256 + j * 128:kk * 256 + (j + 1) * 128],
                                lhsT=xt.f32()[:, kc * 128:(kc + 1) * 128], rhs=dg.f32(), start=True, stop=True),
                                reads=xt.sub(kc * 512, 512).keys() + dg.keys(), writes=PSK(bank),
                                inc=(kk == 1 and j == ng - 1))
                    for kk in range(2):
                        kc = k2 * 2 + kk
                        g, s = gsel(isctx, kc)
                        dst = hv[:, kc, tok0:tok0 + ng * 128]
                        dkeys = hT.sub((kc * ntok + tok0) * 2, ng * 256).keys()
                        src = pb[:, kk * 256:kk * 256 + ng * 128]
                        if kk % 2 == 0:
                            T.op('dve', lambda dst=dst, src=src, g=g, s=s: V.tensor_scalar(out=dst, in0=src, scalar1=g, scalar2=s,
                                                                                      op0=ALU.mult, op1=ALU.add),
                                 reads=PSK(bank) + b_gx.keys() + b_gc.keys() + b_modT.keys(), writes=dkeys)
                        else:
                            T.op('act', lambda dst=dst, src=src, g=g, s=s: S.activation(out=dst, in_=src, func=AF.Identity, scale=g, bias=s),
                                 reads=PSK(bank) + b_gx.keys() + b_gc.keys() + b_modT.keys(), writes=dkeys)

        def hkeys(hT, ntok, kc, t0, n):
            return hT.sub((kc * ntok + t0) * 2, n * 2).keys()

        def proj(wt, hT, ntok, t0, n, bank):
            hv = hT.bf().rearrange("p (kc t) -> p kc t", kc=KC)
            wv = wt.bf().rearrange("p (kc n) -> p kc n", kc=KC)
            for kc in range(KC):
                T.op('pe', lambda kc=kc: PE.matmul(PS(bank)[:, 0:n], lhsT=wv[:, kc, :], rhs=hv[:, kc, t0:t0 + n],
                                                   start=(kc == 0), stop=(kc == KC - 1)),
                     reads=wt.sub(kc * 256, 256).keys() + hkeys(hT, ntok, kc, t0, n), writes=PSK(bank), inc=(kc == KC - 1))

        def proj_tm(wt, hT, ntok, t0, bank, c0):
            hv = hT.bf().rearrange("p (kc t) -> p kc t", kc=KC)
            wv = wt.bf().rearrange("p (kc n) -> p kc n", kc=KC)
            for kc in range(KC):
                T.op('pe', lambda kc=kc: PE.matmul(PS(bank)[:, c0:c0 + 128], lhsT=hv[:, kc, t0:t0 + 128], rhs=wv[:, kc, :],
                                                   start=(kc == 0), stop=(kc == KC - 1)),
                     reads=wt.sub(kc * 256, 256).keys() + hkeys(hT, ntok, kc, t0, 128), writes=PSK(bank), inc=(kc == KC - 1))

        accb = Rot([0, 1])
        auxb = Rot([6, 7])

        def rope_mul(out_b, src_fn, src_keys, pcol0, cs0, n):
            r0, nr = cs0 // 64, n // 64
            for ph in range(2):
                lo = ph * 64
                if ph == 0:
                    tab = prm[lo:lo + 64, pcol0 + r0:pcol0 + r0 + nr].unsqueeze(2).broadcast_to([64, nr, 64])
                else:
                    tab = prm[lo:lo + 64, pcol0:pcol0 + 64].unsqueeze(1).broadcast_to([64, nr, 64])
                o3 = out_b.f32()[lo:lo + 64, 0:n].rearrange("p (r c) -> p r c", c=64)
                i3 = src_fn(lo).rearrange("p (r c) -> p r c", c=64)
                T.op('dve', lambda o3=o3, i3=i3, tab=tab: V.tensor_tensor(out=o3, in0=i3, in1=tab, op=ALU.mult),
                     reads=src_keys + B_prm.keys(), writes=out_b.keys())

        def normrope(bank, n, w_ap, w_keys, cs0, dst, dkeys, RR):
            sq = RR.alloc(n * 2)
            T.op('act', lambda: S.activation(out=sq.bf(), in_=PS(bank)[:, 0:n], func=AF.Square), reads=PSK(bank), writes=sq.keys())
            ab = auxb.next()
            T.op('pe', lambda: PE.matmul(PS(ab)[:, 0:n], lhsT=ones, rhs=sq.bf(), start=True, stop=True),
                 reads=sq.keys() + B_cst.keys(), writes=PSK(ab))
            lnb = RR.alloc(n * 4)
            T.op('act', lambda: S.activation(out=lnb.f32(), in_=PS(ab)[:, 0:n], func=AF.Ln, scale=1.0 / 128, bias=b_eps.f32()),
                 reads=PSK(ab) + b_eps.keys(), writes=lnb.keys())
            T.op('act', lambda: S.activation(out=lnb.f32(), in_=lnb.f32(), func=AF.Exp, scale=-0.5), reads=lnb.keys(), writes=lnb.keys())
            if cs0 is None:
                T.op('dve', lambda: V.scalar_tensor_tensor(out=dst, in0=PS(bank)[:, 0:n], scalar=w_ap, in1=lnb.f32(), op0=ALU.mult, op1=ALU.mult),
                     reads=PSK(bank) + lnb.keys() + w_keys, writes=dkeys)
                return
            qn = RR.alloc(n * 2)
            T.op('dve', lambda: V.scalar_tensor_tensor(out=qn.bf(), in0=PS(bank)[:, 0:n], scalar=w_ap, in1=lnb.f32(), op0=ALU.mult, op1=ALU.mult),
                 reads=PSK(bank) + lnb.keys() + w_keys, writes=qn.keys())
            ab2 = auxb.next()
            T.op('pe', lambda: PE.matmul(PS(ab2)[:, 0:n], lhsT=rmat, rhs=qn.bf(), start=True, stop=True),
                 reads=qn.keys() + B_cst.keys(), writes=PSK(ab2))
            t1 = RR.alloc(n * 4)
            rope_mul(t1, lambda lo: qn.bf()[lo:lo + 64, 0:n], qn.keys(), P_COS, cs0, n)
            t2 = RR.alloc(n * 4)
            rope_mul(t2, lambda lo: PS(ab2)[lo:lo + 64, 0:n], PSK(ab2), P_SIN, cs0, n)
            T.op('dve', lambda: V.tensor_tensor(out=dst, in0=t1.f32(), in1=t2.f32(), op=ALU.add),
                 reads=t1.keys() + t2.keys(), writes=dkeys)

        B_kT = Buf(ar, KV0, 18432)
        B_v = Buf(ar, KV0 + 18432, 18432)
        kTv = B_kT.bf().rearrange("p (g t) -> p g t", g=4)
        vv = B_v.bf().rearrange("p (t c) -> p t c", t=18)

        def kkeys(g, t0, n):
            return B_kT.sub((g * NKEY + t0) * 2, n * 2).keys()

        def vkeys(t, c0, n):
            return B_v.sub((t * 512 + c0) * 2, n * 2).keys()

        def wg_load(n):
            b = WG.alloc(1024)
            T.dma('pool', b.bf(), wg_d[:, n * 512:(n + 1) * 512], writes=b.keys())
            return b

        def gates(wgb, n, d, ub_ap, ub_keys, u_ap, u_keys, a_b, x_b, m_b, nt):
            za = accb.next()
            T.op('pe', lambda: PE.matmul(PS(za)[:, 0:nt], lhsT=wgb.bf()[:, (d * 2 + 0) * 128:(d * 2 + 1) * 128], rhs=ub_ap, start=True, stop=True),
                 reads=wgb.keys() + ub_keys, writes=PSK(za))
            zx = accb.next()
            T.op('pe', lambda: PE.matmul(PS(zx)[:, 0:nt], lhsT=wgb.bf()[:, (d * 2 + 1) * 128:(d * 2 + 2) * 128], rhs=ub_ap, start=True, stop=True),
                 reads=wgb.keys() + ub_keys, writes=PSK(zx))
            idx = d * 16 + n
            T.op('act', lambda: S.activation(out=a_b.f32(), in_=PS(za)[:, 0:nt], func=AF.Tanh, scale=0.5, bias=b_ba05.f32()[:, idx:idx + 1]),
                 reads=PSK(za) + b_ba05.keys(), writes=a_b.keys())
            T.op('act', lambda: S.activation(out=x_b.f32(), in_=PS(zx)[:, 0:nt], func=AF.Tanh, scale=0.5, bias=b_bx05.f32()[:, idx:idx + 1]),
                 reads=PSK(zx) + b_bx05.keys(), writes=x_b.keys())
            c05 = b_c05.f32()[:, idx:idx + 1]
            T.op('act', lambda: S.activation(out=a_b.f32(), in_=a_b.f32(), func=AF.Exp, scale=c05, bias=c05),
                 reads=a_b.keys() + b_c05.keys(), writes=a_b.keys())
            T.op('dve', lambda: V.tensor_tensor(out=m_b.f32(), in0=a_b.f32(), in1=a_b.f32(), op=ALU.mult), reads=a_b.keys(), writes=m_b.keys())
            T.op('act', lambda: S.activation(out=m_b.f32(), in_=m_b.f32(), func=AF.Ln, scale=-1.0, bias=b_one.f32()),
                 reads=m_b.keys() + b_one.keys(), writes=m_b.keys())
            T.op('act', lambda: S.activation(out=m_b.f32(), in_=m_b.f32(), func=AF.Exp, scale=0.5), reads=m_b.keys(), writes=m_b.keys())
            T.op('dve', lambda: V.scalar_tensor_tensor(out=x_b.f32(), in0=x_b.f32(), scalar=1.0, in1=m_b.f32(), op0=ALU.add, op1=ALU.mult),
                 reads=x_b.keys() + m_b.keys(), writes=x_b.keys())
            T.op('dve', lambda: V.scalar_tensor_tensor(out=x_b.f32(), in0=x_b.f32(), scalar=0.5, in1=u_ap, op0=ALU.mult, op1=ALU.mult),
                 reads=x_b.keys() + u_keys, writes=x_b.keys())

        accb4 = Rot([0, 1, 2, 3])

        def gate_mm_tanh(wgb, n, d, ub_ap, ub_keys, a_ap, a_keys, x_ap, x_keys, nt):
            za = accb4.next()
            T.op('pe', lambda: PE.matmul(PS(za)[:, 0:nt], lhsT=wgb.bf()[:, (d * 2 + 0) * 128:(d * 2 + 1) * 128], rhs=ub_ap, start=True, stop=True),
                 reads=wgb.keys() + ub_keys, writes=PSK(za))
            zx = accb4.next()
            T.op('pe', lambda: PE.matmul(PS(zx)[:, 0:nt], lhsT=wgb.bf()[:, (d * 2 + 1) * 128:(d * 2 + 2) * 128], rhs=ub_ap, start=True, stop=True),
                 reads=wgb.keys() + ub_keys, writes=PSK(zx))
            idx = d * 16 + n
            T.op('act', lambda: S.activation(out=a_ap, in_=PS(za)[:, 0:nt], func=AF.Tanh, scale=0.5, bias=b_ba05.f32()[:, idx:idx + 1]),
                 reads=PSK(za) + b_ba05.keys(), writes=a_keys)
            T.op('act', lambda: S.activation(out=x_ap, in_=PS(zx)[:, 0:nt], func=AF.Tanh, scale=0.5, bias=b_bx05.f32()[:, idx:idx + 1]),
                 reads=PSK(zx) + b_bx05.keys(), writes=x_keys)

        def gate_exp(n, d, a_ap, a_keys):
            idx = d * 16 + n
            c05 = b_c05.f32()[:, idx:idx + 1]
            T.op('act', lambda: S.activation(out=a_ap, in_=a_ap, func=AF.Exp, scale=c05, bias=c05),
                 reads=a_keys + b_c05.keys(), writes=a_keys)

        def gate_mult(a_b, x_b, m_b):
            T.op('dve', lambda: V.tensor_tensor(out=m_b.f32(), in0=a_b.f32(), in1=a_b.f32(), op=ALU.mult), reads=a_b.keys(), writes=m_b.keys())
            T.op('act', lambda: S.activation(out=m_b.f32(), in_=m_b.f32(), func=AF.Ln, scale=-1.0, bias=b_one.f32()),
                 reads=m_b.keys() + b_one.keys(), writes=m_b.keys())
            T.op('act', lambda: S.activation(out=m_b.f32(), in_=m_b.f32(), func=AF.Exp, scale=0.5), reads=m_b.keys(), writes=m_b.keys())
            T.op('dve', lambda: V.scalar_tensor_tensor(out=x_b.f32(), in0=x_b.f32(), scalar=1.0, in1=m_b.f32(), op0=ALU.add, op1=ALU.mult),
                 reads=x_b.keys() + m_b.keys(), writes=x_b.keys())

        def rev(ap2d):
            n = ap2d.shape[1]
            return bass.AP(ap2d.tensor, ap2d.offset + (n - 1), [list(ap2d.ap[0]), [-1, n]])

        def conv5(n, xlp, o0, u, nt):
            w5 = pcol(P_W5 + n * 5, 5)
            xf = xlp.f32()
            T.op('dve', lambda: V.tensor_scalar(out=u.f32()[:, 0:nt], in0=xf[:, o0:o0 + nt], scalar1=w5[:, 0:1], scalar2=pcol(P_CB + n),
                                                op0=ALU.mult, op1=ALU.add),
                 reads=xlp.keys() + B_prm.keys(), writes=u.keys())
            for o in range(1, 5):
                T.op('dve', lambda o=o: V.scalar_tensor_tensor(out=u.f32()[:, 0:nt], in0=xf[:, o0 + o:o0 + o + nt], scalar=w5[:, o:o + 1],
                                                               in1=u.f32()[:, 0:nt], op0=ALU.mult, op1=ALU.add),
                     reads=xlp.keys() + u.keys() + B_prm.keys(), writes=u.keys())

        hT1 = Buf(ar, BASE, T1 * KC * 2)
        RP1 = Ring(ar, BASE + hT1.n, BASE + hT1.n + 49152)
        groups = []
        xrow = lambda r0: xs_d[r0:r0 + 128, :]
        groups.append([(xrow(NA - NH), False, 0, 0), (xrow(NA), False, 1, 128)])
        for i in range(3):
            groups.append([(xrow(NA + 128 * (2 * i + 1)), False, 2 + 2 * i, 128 * (2 * i + 2)),
                           (xrow(NA + 128 * (2 * i + 2)), False, 3 + 2 * i, 128 * (2 * i + 3))])
        groups.append([(xrow(NA + 128 * 7), False, 8, 128 * 8)])
        groups.append([(cx_d[0:128, :], True, 9, 1152), (cx_d[128:256, :], True, 10, 1280)])
        w_issue()
        w_issue()
        prep(groups, hT1, RP1)
        if dbg:
            dump("hT1", hT1.bf(), [128, KC * T1], BF16, reads=hT1.keys())
        if stop_after == 1:
            finish()
            return nc, dbg_outs
        T.barrier()
        R1 = Ring(ar, BASE + hT1.n, KV0)
        for g in range(4):
            wt = w_next(C_K + g * 128)
            for (t0, n, k0, roped) in ((128, 512, NA, True), (640, 512, NA + 512, True), (1152, 256, NA + NB, False)):
                bank = accb.next()
                proj(wt, hT1, T1, t0, n, bank)
                normrope(bank, n, pcol(P_KW), B_prm.keys(), (k0 if roped else None), kTv[:, g, k0:k0 + n], kkeys(g, k0, n), R1)
        for j in range(4):
            wt = w_next(C_V + j * 128)
            for tg in range(3):
                tl = list(range(tg * 4, min(tg * 4 + 4, 10)))
                bank = accb.next()
                for i, t in enumerate(tl):
                    proj_tm(wt, hT1, T1, 128 + t * 128, bank, i * 128)
                nt = len(tl)
                kt0 = 8 + tl[0]
                dst = vv[:, kt0:kt0 + nt, j * 128:(j + 1) * 128]
                dk = []
                for t in tl:
                    dk += vkeys(8 + t, j * 128, 128)
                T.op('act', lambda dst=dst, bank=bank, nt=nt: S.activation(out=dst, in_=PS(bank)[:, 0:nt * 128].rearrange("p (t c) -> p t c", t=nt), func=AF.Copy),
                     reads=PSK(bank), writes=dk)
        if dbg:
            dump("kT", B_kT.bf(), [128, 4 * NKEY], BF16, reads=B_kT.keys())
            dump("v", B_v.bf(), [128, 18 * 512], BF16, reads=B_v.keys())
        R1B = BASE + hT1.n
        XLP1 = 5376
        xlp1 = [Buf(ar, R1B + i * XLP1, XLP1) for i in range(2)]
        ub1 = [Buf(ar, R1B + 2 * XLP1 + i * 2560, 2560) for i in range(2)]
        o_ = R1B + 2 * XLP1 + 5120
        u1 = Buf(ar, o_, 5120)
        a1 = Buf(ar, o_ + 5120, 6144)
        x1 = Buf(ar, o_ + 5120 + 6144, 6144)
        m1 = Buf(ar, o_ + 5120 + 2 * 6144, 6144)
        assert o_ + 5120 + 3 * 6144 <= KV0

        def p1_A(n):
            wt = w_next(C_XL + n * 128)
            wgb = wg_load(n)
            xlp = xlp1[n % 2]
            xf = xlp.f32()
            T.op('dve', lambda: V.memset(xf[:, 1026:1030], 0.0), writes=xlp.keys())
            T.op('dve', lambda: V.memset(xf[:, 1286:1288], 0.0), writes=xlp.keys())
            for (t0, nn) in ((0, 512), (512, 512), (1024, 384)):
                bank = accb4.next()
                proj(wt, hT1, T1, t0, nn, bank)
                if t0 == 0:
                    T.op('act', lambda bank=bank: S.activation(out=xf[:, 0:386], in_=PS(bank)[:, 126:512], func=AF.Copy),
                         reads=PSK(bank), writes=xlp.keys())
                elif t0 == 512:
                    T.op('act', lambda bank=bank: S.activation(out=xf[:, 386:898], in_=PS(bank)[:, 0:512], func=AF.Copy),
                         reads=PSK(bank), writes=xlp.keys())
                else:
                    T.op('act', lambda bank=bank: S.activation(out=xf[:, 898:1026], in_=PS(bank)[:, 0:128], func=AF.Copy),
                         reads=PSK(bank), writes=xlp.keys())
                    T.op('act', lambda bank=bank: S.activation(out=xf[:, 1030:1286], in_=PS(bank)[:, 128:384], func=AF.Copy),
                         reads=PSK(bank), writes=xlp.keys())
            T.op('dve', lambda n=n: V.tensor_copy(out=b_xlB2.f32()[:, 2 * n:2 * n + 2], in_=xf[:, 2:4]),
                 reads=xlp.keys(), writes=b_xlB2.keys())
            conv5(n, xlp, 0, u1.sub(0, 4096), 1024)
            conv5(n, xlp, 1028, u1.sub(4096, 1024), 256)
            ub = ub1[n % 2]
            T.op('act', lambda: S.activation(out=ub.bf(), in_=u1.f32(), func=AF.Copy), reads=u1.keys(), writes=ub.keys())
            return wgb, ub

        def p1_B(n, wgb, ub):
            ubB = ub.bf()[:, 0:1024]
            ubC = ub.bf()[:, 1024:1280]
            af, xf_ = a1.f32(), x1.f32()
            gate_mm_tanh(wgb, n, 0, ubC, ub.keys(), af[:, 0:256], a1.sub(0, 1024).keys(), xf_[:, 0:256], x1.sub(0, 1024).keys(), 256)
            gate_mm_tanh(wgb, n, 1, ubC, ub.keys(), af[:, 256:512], a1.sub(1024, 1024).keys(), xf_[:, 256:512], x1.sub(1024, 1024).keys(), 256)
            for c in range(2):
                lo = 512 + c * 512
                gate_mm_tanh(wgb, n, 1, ubB[:, c * 512:(c + 1) * 512], ub.keys(), af[:, lo:lo + 512], a1.sub(lo * 4, 2048).keys(),
                             xf_[:, lo:lo + 512], x1.sub(lo * 4, 2048).keys(), 512)
            gate_exp(n, 0, af[:, 0:256], a1.sub(0, 1024).keys())
            gate_exp(n, 1, af[:, 256:1536], a1.sub(1024, 5120).keys())
            gate_mult(a1, x1, m1)
            for (lo, nn, uu) in ((0, 256, ubC), (256, 256, ubC), (512, 1024, ubB)):
                T.op('dve', lambda lo=lo, nn=nn, uu=uu: V.scalar_tensor_tensor(out=xf_[:, lo:lo + nn], in0=xf_[:, lo:lo + nn], scalar=0.5, in1=uu,
                                                                          op0=ALU.mult, op1=ALU.mult),
                     reads=x1.sub(lo * 4, nn * 4).keys() + ub.keys(), writes=x1.sub(lo * 4, nn * 4).keys())
            mf = m1.f32()
            T.op('dve', lambda: V.tensor_tensor_scan(out=mf[:, 0:256], data0=af[:, 0:256], data1=xf_[:, 0:256], initial=0.0, op0=ALU.mult, op1=ALU.add),
                 reads=a1.sub(0, 1024).keys() + x1.sub(0, 1024).keys(), writes=m1.sub(0, 1024).keys())
            T.op('dve', lambda n=n: V.tensor_copy(out=b_stF.f32()[:, n:n + 1], in_=mf[:, 255:256]), reads=m1.sub(0, 1024).keys(), writes=b_stF.keys())
            T.op('dve', lambda: V.tensor_tensor_scan(out=rev(mf[:, 256:512]), data0=rev(af[:, 256:512]), data1=rev(xf_[:, 256:512]), initial=0.0,
                                                     op0=ALU.mult, op1=ALU.add),
                 reads=a1.sub(1024, 1024).keys() + x1.sub(1024, 1024).keys(), writes=m1.sub(1024, 1024).keys())
            T.op('dve', lambda: V.tensor_tensor_scan(out=rev(mf[:, 512:1536]), data0=rev(af[:, 512:1536]), data1=rev(xf_[:, 512:1536]),
                                                     initial=mf[:, 256:257], op0=ALU.mult, op1=ALU.add),
                 reads=a1.sub(2048, 4096).keys() + x1.sub(2048, 4096).keys() + m1.sub(1024, 1024).keys(), writes=m1.sub(2048, 4096).keys())
            T.op('dve', lambda n=n: V.tensor_copy(out=b_stB.f32()[:, n:n + 1], in_=mf[:, 512:513]), reads=m1.sub(2048, 4096).keys(), writes=b_stB.keys())

        GBANK = 5

        def gate_tile(j):
            wt = w_next(2 * D + j * 128, 'ada')
            wv = wt.bf().rearrange("p (kc n) -> p kc n", kc=KC)
            q = j % 4
            for kc in range(KC):
                T.op('pe', lambda kc=kc: PE.matmul(PS(GBANK)[0:2, q * 128:(q + 1) * 128], lhsT=csb[:, 2 * kc:2 * kc + 2], rhs=wv[:, kc, :],
                                                   start=(kc == 0), stop=(kc == KC - 1)),
                     reads=b_csb.keys() + wt.sub(kc * 256, 256).keys(), writes=PSK(GBANK), inc=(kc == KC - 1))
            if q == 3:
                grp = j // 4
                T.dma('sp', b_gb.f32()[0:2, :], bada_d[:, 2 * D + grp * 512:2 * D + (grp + 1) * 512], writes=b_gb.keys())
                T.op('dve', lambda: V.tensor_tensor(out=b_grow.f32()[0:2, :], in0=PS(GBANK)[0:2, :], in1=b_gb.f32()[0:2, :], op=ALU.add),
                     reads=PSK(GBANK) + b_gb.keys(), writes=b_grow.keys())
                T.dma('sp', gate_d[0:1, grp * 512:(grp + 1) * 512], b_grow.f32()[0:1, :], reads=b_grow.keys(), writes=["gate_row"])

        pend = p1_A(0)
        for n in range(16):
            nxt = p1_A(n + 1) if n + 1 < 16 else None
            p1_B(n, *pend)
            gate_tile(2 * n)
            gate_tile(2 * n + 1)
            pend = nxt
        if dbg:
            dump("stF", b_stF.f32(), [128, 16], reads=b_stF.keys())
            dump("stB", b_stB.f32(), [128, 16], reads=b_stB.keys())
        if stop_after == 2:
            finish()
            return nc, dbg_outs
        T.barrier()

        hTA = Buf(ar, BASE, NA * KC * 2)
        RPA = Ring(ar, BASE + hTA.n, BASE + hTA.n + 49152)
        groups = []
        for i in range(4):
            groups.append([(xrow(256 * i), False, 11 + 2 * i, 256 * i), (xrow(256 * i + 128), False, 12 + 2 * i, 256 * i + 128)])
        prep(groups, hTA, RPA)
        T.barrier()
        B_Ua = Buf(ar, BASE + hTA.n, 32768)
        Uav = B_Ua.bf().rearrange("p (h t) -> p h t", h=16)
        R2BASE = BASE + hTA.n + 32768
        b_ssa = Buf(ar, R2BASE, 4096)
        QTB = [Buf(ar, R2BASE + 4096 + i * 1024, 1024) for i in range(4)]
        GWB = [Buf(ar, R2BASE + 8192 + i * 1024, 1024) for i in range(4)]
        R2 = Ring(ar, R2BASE + 12288 + 6144, KV0)
        for g in range(4):
            wt = w_next(C_K + g * 128)
            for qb in range(2):
                bank = accb.next()
                proj(wt, hTA, NA, qb * 512, 512, bank)
                normrope(bank, 512, pcol(P_KW), B_prm.keys(), qb * 512, kTv[:, g, qb * 512:(qb + 1) * 512], kkeys(g, qb * 512, 512), R2)
        for j in range(4):
            wt = w_next(C_V + j * 128)
            for tg in range(2):
                bank = accb.next()
                for i in range(4):
                    proj_tm(wt, hTA, NA, (tg * 4 + i) * 128, bank, i * 128)
                dst = vv[:, tg * 4:tg * 4 + 4, j * 128:(j + 1) * 128]
                dk = []
                for t in range(tg * 4, tg * 4 + 4):
                    dk += vkeys(t, j * 128, 128)
                T.op('act', lambda dst=dst, bank=bank: S.activation(out=dst, in_=PS(bank)[:, 0:512].rearrange("p (t c) -> p t c", t=4), func=AF.Copy),
                     reads=PSK(bank), writes=dk)
        if dbg:
            dump("kT2", B_kT.bf(), [128, 4 * NKEY], BF16, reads=B_kT.keys())
            dump("v2", B_v.bf(), [128, 18 * 512], BF16, reads=B_v.keys())
        sb_rot = Rot([2, 3, 6])
        pv_rot = Rot([4])
        auxb.items = [7]
        NRL = [Buf(ar, R2BASE + 12288 + i * 3072, 3072) for i in range(2)]

        def nr_split(bank, n, w_ap, w_keys, cs0, dst_b, lnb, qn):
            sqb = dst_b

            def p1():
                T.op('act', lambda: S.activation(out=sqb.bf(), in_=PS(bank)[:, 0:n], func=AF.Square), reads=PSK(bank), writes=sqb.keys())

            def p2():
                ab = auxb.next()
                T.op('pe', lambda: PE.matmul(PS(ab)[:, 0:n], lhsT=ones, rhs=sqb.bf(), start=True, stop=True),
                     reads=sqb.keys() + B_cst.keys(), writes=PSK(ab))
                T.op('act', lambda: S.activation(out=lnb.f32(), in_=PS(ab)[:, 0:n], func=AF.Ln, scale=1.0 / 128, bias=b_eps.f32()),
                     reads=PSK(ab) + b_eps.keys(), writes=lnb.keys())
                T.op('act', lambda: S.activation(out=lnb.f32(), in_=lnb.f32(), func=AF.Exp, scale=-0.5), reads=lnb.keys(), writes=lnb.keys())
                T.op('dve', lambda: V.scalar_tensor_tensor(out=qn.bf(), in0=PS(bank)[:, 0:n], scalar=w_ap, in1=lnb.f32(), op0=ALU.mult, op1=ALU.mult),
                     reads=PSK(bank) + lnb.keys() + w_keys, writes=qn.keys())

            def p3():
                ab2 = auxb.next()
                T.op('pe', lambda: PE.matmul(PS(ab2)[:, 0:n], lhsT=rmat, rhs=qn.bf(), start=True, stop=True),
                     reads=qn.keys() + B_cst.keys(), writes=PSK(ab2))
                t1 = R2.alloc(n * 4)
                rope_mul(t1, lambda lo: qn.bf()[lo:lo + 64, 0:n], qn.keys(), P_COS, cs0, n)
                t2 = R2.alloc(n * 4)
                rope_mul(t2, lambda lo: PS(ab2)[lo:lo + 64, 0:n], PSK(ab2), P_SIN, cs0, n)
                T.op('dve', lambda: V.tensor_tensor(out=dst_b.bf(), in0=t1.f32(), in1=t2.f32(), op=ALU.add),
                     reads=t1.keys() + t2.keys(), writes=dst_b.keys())
            return p1, p2, p3

        def att_A_stages(h):
            st = {}
            qTs = [QTB[(h % 2) * 2 + qb] for qb in range(2)]
            gws = [GWB[(h % 2) * 2 + qb] for qb in range(2)]

            def a1():
                wq = w_next(C_Q + h * 128)
                st['pieces'] = []
                for qb in range(2):
                    bank = accb.next()
                    proj(wq, hTA, NA, qb * 512, 512, bank)
                    pcs = nr_split(bank, 512, b_qws.f32(), b_qws.keys(), qb * 512, qTs[qb], NRL[qb].sub(0, 2048), NRL[qb].sub(2048, 1024))
                    st['pieces'].append(pcs)
                for pcs in st['pieces']:
                    pcs[0]()

            def a2():
                for pcs in st['pieces']:
                    pcs[1]()

            def a3():
                for pcs in st['pieces']:
                    pcs[2]()

            def a4():
                wga = w_next(C_GA + h * 128)
                for qb in range(2):
                    bank = accb.next()
                    proj(wga, hTA, NA, qb * 512, 512, bank)
                    th = R2.alloc(2048)
                    T.op('act', lambda th=th, bank=bank: S.activation(out=th.f32(), in_=PS(bank)[:, :], func=AF.Tanh, scale=0.5), reads=PSK(bank), writes=th.keys())
                    gw = gws[qb]
                    T.op('dve', lambda th=th, gw=gw, bank=bank: V.scalar_tensor_tensor(out=gw.bf(), in0=th.f32(), scalar=1.0, in1=PS(bank)[:, :], op0=ALU.add, op1=ALU.mult),
                         reads=th.keys() + PSK(bank), writes=gw.keys())
            return (qTs, gws), {1: a1, 5: a2, 10: a3, 14: a4}

        def att_B(h, qTs, gws, hooks):
            g = h // 4
            for qb in range(2):
                qT = qTs[qb]
                pvb = pv_rot.next()
                smb = 5
                sbanks = [None] * 18

                def s_mm(kt):
                    sb = sb_rot.next()
                    sbanks[kt] = sb
                    T.op('pe', lambda: PE.matmul(PS(sb)[:, :], lhsT=kTv[:, g, kt * 128:(kt + 1) * 128], rhs=qT.bf(), start=True, stop=True),
                         reads=kkeys(g, kt * 128, 128) + qT.keys(), writes=PSK(sb))
                s_mm(0)
                s_mm(1)
                for kt in range(18):
                    if kt + 2 < 18:
                        s_mm(kt + 2)
                    if qb == 0 and kt in hooks:
                        hooks[kt]()
                    sb = sbanks[kt]
                    pT = R2.alloc(1024)
                    T.op('act', lambda sb=sb, pT=pT: S.activation(out=pT.bf(), in_=PS(sb)[:, :], func=AF.Exp), reads=PSK(sb), writes=pT.keys())
                    T.op('pe', lambda kt=kt, pT=pT: PE.matmul(PS(pvb)[:, :], lhsT=vv[:, kt, g * 128:(g + 1) * 128], rhs=pT.bf(),
                                                              start=(kt == 0), stop=(kt == 17)),
                         reads=vkeys(kt, g * 128, 128) + pT.keys(), writes=PSK(pvb), inc=(kt == 17))
                    T.op('pe', lambda kt=kt, pT=pT: PE.matmul(PS(smb)[:, :], lhsT=ones, rhs=pT.bf(), start=(kt == 0), stop=(kt == 17)),
                         reads=pT.keys() + B_cst.keys(), writes=PSK(smb), inc=(kt == 17))
                rs = R2.alloc(2048)
                T.op('dve', lambda rs=rs: V.reciprocal(out=rs.f32(), in_=PS(smb)[:, :]), reads=PSK(smb), writes=rs.keys())
                att = R2.alloc(2048)
                T.op('dve', lambda rs=rs, att=att: V.tensor_tensor(out=att.f32(), in0=PS(pvb)[:, :], in1=rs.f32(), op=ALU.mult),
                     reads=PSK(pvb) + rs.keys(), writes=att.keys())
                ssl_ = b_ssa.sub(qb * 2048, 2048)
                if h == 0:
                    T.op('dve', lambda att=att, ssl_=ssl_: V.tensor_tensor(out=ssl_.f32(), in0=att.f32(), in1=att.f32(), op=ALU.mult),
                         reads=att.keys(), writes=ssl_.keys())
                else:
                    sqa = R2.alloc(2048)
                    T.op('dve', lambda att=att, sqa=sqa: V.tensor_tensor(out=sqa.f32(), in0=att.f32(), in1=att.f32(), op=ALU.mult),
                         reads=att.keys(), writes=sqa.keys())
                    T.op('dve', lambda sqa=sqa, ssl_=ssl_: V.tensor_tensor(out=ssl_.f32(), in0=ssl_.f32(), in1=sqa.f32(), op=ALU.add),
                         reads=sqa.keys() + ssl_.keys(), writes=ssl_.keys())
                gw = gws[qb]
                T.op('dve', lambda att=att, gw=gw, qb=qb, h=h: V.scalar_tensor_tensor(out=Uav[:, h, qb * 512:(qb + 1) * 512], in0=att.f32(),
                                                                                    scalar=b_wa05.f32()[:, h:h + 1], in1=gw.bf(),
                                                                                    op0=ALU.mult, op1=ALU.mult),
                     reads=att.keys() + gw.keys() + b_wa05.keys(), writes=B_Ua.sub((h * NA + qb * 512) * 2, 1024).keys())

        cur, hk = att_A_stages(0)
        for k_ in (1, 5, 10, 14):
            hk[k_]()
        for h in range(16):
            if h + 1 < 16:
                nxt, hk = att_A_stages(h + 1)
            else:
                nxt, hk = None, {}
            att_B(h, cur[0], cur[1], hk)
            cur = nxt
        auxb.items = [6, 7]
        rsa = R2.alloc(4096)
        for qb in range(2):
            sqb = R2.alloc(1024)
            T.op('dve', lambda qb=qb, sqb=sqb: V.tensor_copy(out=sqb.bf(), in_=b_ssa.f32()[:, qb * 512:(qb + 1) * 512]),
                 reads=b_ssa.keys(), writes=sqb.keys())
            ab = auxb.next()
            T.op('pe', lambda sqb=sqb, ab=ab: PE.matmul(PS(ab)[:, :], lhsT=ones, rhs=sqb.bf(), start=True, stop=True),
                 reads=sqb.keys() + B_cst.keys(), writes=PSK(ab))
            T.op('act', lambda qb=qb, ab=ab: S.activation(out=rsa.f32()[:, qb * 512:(qb + 1) * 512], in_=PS(ab)[:, :], func=AF.Ln, scale=1.0 / 2048, bias=b_eps.f32()),
                 reads=PSK(ab) + b_eps.keys(), writes=rsa.keys())
        T.op('act', lambda: S.activation(out=rsa.f32(), in_=rsa.f32(), func=AF.Exp, scale=-0.5), reads=rsa.keys(), writes=rsa.keys())
        for h in range(16):
            T.op('dve', lambda h=h: V.tensor_tensor(out=Uav[:, h, :], in0=Uav[:, h, :], in1=rsa.f32(), op=ALU.mult),
                 reads=rsa.keys() + B_Ua.sub(h * 2048, 2048).keys(), writes=B_Ua.sub(h * 2048, 2048).keys())
        if dbg:
            dump("Ua", B_Ua.bf(), [128, 16 * NA], BF16, reads=B_Ua.keys())
        if stop_after == 3:
            finish()
            return nc, dbg_outs
        T.barrier()

        T.dma('sp', ua_d[:, :], B_Ua.bf(), reads=B_Ua.keys(), writes=["ua_scr"])
        T.barrier()
        B_Ul = Buf(ar, KV0, 32768)
        Ulv = B_Ul.bf().rearrange("p (h t) -> p h t", h=16)
        R2A = BASE + hTA.n
        b_ssl = Buf(ar, R2A, 4096)
        o_ = R2A + 4096
        xlpA = [Buf(ar, o_ + i * 4352, 4352) for i in range(2)]
        o_ += 8704
        uA = [Buf(ar, o_ + i * 4096, 4096) for i in range(2)]
        o_ += 8192
        ubA = [Buf(ar, o_ + i * 2048, 2048) for i in range(2)]
        o_ += 4096
        aA = Buf(ar, o_, 8192)
        xA = Buf(ar, o_ + 8192, 8192)
        mA = Buf(ar, o_ + 16384, 8192)
        o_ += 24576
        RS = Ring(ar, o_, o_ + 8192)
        GWL = [Buf(ar, o_ + 8192 + i * 1024, 1024) for i in range(4)]
        assert o_ + 8192 + 4096 <= KV0

        def p2_A(n):
            wxl = w_next(C_XL + n * 128)
            wgb = wg_load(n)
            xlp = xlpA[n % 2]
            xf = xlp.f32()
            T.op('dve', lambda: V.memset(xf[:, 0:2], 0.0), writes=xlp.keys())
            T.op('dve', lambda n=n: V.tensor_copy(out=xf[:, 1026:1028], in_=b_xlB2.f32()[:, 2 * n:2 * n + 2]), reads=b_xlB2.keys(), writes=xlp.keys())
            for qb in range(2):
                bank = accb4.next()
                proj(wxl, hTA, NA, qb * 512, 512, bank)
                T.op('act', lambda qb=qb, bank=bank: S.activation(out=xf[:, 2 + qb * 512:2 + (qb + 1) * 512], in_=PS(bank)[:, :], func=AF.Copy),
                     reads=PSK(bank), writes=xlp.keys())
            u = uA[n % 2]
            conv5(n, xlp, 0, u, 1024)
            ub = ubA[n % 2]
            T.op('act', lambda: S.activation(out=ub.bf(), in_=u.f32(), func=AF.Copy), reads=u.keys(), writes=ub.keys())
            wgl = w_next(C_GL + n * 128)
            gws = []
            for qb in range(2):
                bank = accb4.next()
                proj(wgl, hTA, NA, qb * 512, 512, bank)
                th = RS.alloc(2048)
                T.op('act', lambda th=th, bank=bank: S.activation(out=th.f32(), in_=PS(bank)[:, :], func=AF.Tanh, scale=0.5), reads=PSK(bank), writes=th.keys())
                gw = GWL[(n % 2) * 2 + qb]
                T.op('dve', lambda th=th, gw=gw, bank=bank: V.scalar_tensor_tensor(out=gw.bf(), in0=th.f32(), scalar=1.0, in1=PS(bank)[:, :], op0=ALU.add, op1=ALU.mult),
                     reads=th.keys() + PSK(bank), writes=gw.keys())
                gws.append(gw)
            return wgb, u, ub, gws

        def p2_B(n, wgb, u, ub, gws):
            af, xf_, mf = aA.f32(), xA.f32(), mA.f32()
            for d in range(2):
                for c in range(2):
                    lo = d * 1024 + c * 512
                    gate_mm_tanh(wgb, n, d, ub.bf()[:, c * 512:(c + 1) * 512], ub.keys(), af[:, lo:lo + 512], aA.sub(lo * 4, 2048).keys(),
                                 xf_[:, lo:lo + 512], xA.sub(lo * 4, 2048).keys(), 512)
            for d in range(2):
                gate_exp(n, d, af[:, d * 1024:(d + 1) * 1024], aA.sub(d * 4096, 4096).keys())
            gate_mult(aA, xA, mA)
            for d in range(2):
                T.op('dve', lambda d=d: V.scalar_tensor_tensor(out=xf_[:, d * 1024:(d + 1) * 1024], in0=xf_[:, d * 1024:(d + 1) * 1024], scalar=0.5,
                                                            in1=u.f32(), op0=ALU.mult, op1=ALU.mult),
                     reads=xA.sub(d * 4096, 4096).keys() + u.keys(), writes=xA.sub(d * 4096, 4096).keys())
            T.op('dve', lambda: V.tensor_tensor_scan(out=mf[:, 0:1024], data0=af[:, 0:1024], data1=xf_[:, 0:1024], initial=b_stF.f32()[:, n:n + 1],
                                                     op0=ALU.mult, op1=ALU.add),
                 reads=aA.sub(0, 4096).keys() + xA.sub(0, 4096).keys() + b_stF.keys(), writes=mA.sub(0, 4096).keys())
            T.op('dve', lambda: V.tensor_tensor_scan(out=rev(mf[:, 1024:2048]), data0=rev(af[:, 1024:2048]), data1=rev(xf_[:, 1024:2048]),
                                                     initial=b_stB.f32()[:, n:n + 1], op0=ALU.mult, op1=ALU.add),
                 reads=aA.sub(4096, 4096).keys() + xA.sub(4096, 4096).keys() + b_stB.keys(), writes=mA.sub(4096, 4096).keys())
            lru = mA.sub(0, 4096)
            T.op('dve', lambda: V.tensor_tensor(out=mf[:, 0:1024], in0=mf[:, 0:1024], in1=mf[:, 1024:2048], op=ALU.add),
                 reads=mA.keys(), writes=lru.keys())
            for qb in range(2):
                lr = lru.sub(qb * 2048, 2048)
                ssl_ = b_ssl.sub(qb * 2048, 2048)
                if n == 0:
                    T.op('dve', lambda: V.tensor_tensor(out=ssl_.f32(), in0=lr.f32(), in1=lr.f32(), op=ALU.mult), reads=lr.keys(), writes=ssl_.keys())
                else:
                    sql = RS.alloc(2048)
                    T.op('dve', lambda: V.tensor_tensor(out=sql.f32(), in0=lr.f32(), in1=lr.f32(), op=ALU.mult), reads=lr.keys(), writes=sql.keys())
                    T.op('dve', lambda: V.tensor_tensor(out=ssl_.f32(), in0=ssl_.f32(), in1=sql.f32(), op=ALU.add),
                         reads=sql.keys() + ssl_.keys(), writes=ssl_.keys())
                gw = gws[qb]
                T.op('dve', lambda: V.scalar_tensor_tensor(out=Ulv[:, n, qb * 512:(qb + 1) * 512], in0=lr.f32(), scalar=b_wl05.f32()[:, n:n + 1],
                                                           in1=gw.bf(), op0=ALU.mult, op1=ALU.mult),
                     reads=lr.keys() + gw.keys() + b_wl05.keys(), writes=B_Ul.sub((n * NA + qb * 512) * 2, 1024).keys())

        pend = p2_A(0)
        for n in range(16):
            nxt = p2_A(n + 1) if n + 1 < 16 else None
            p2_B(n, *pend)
            pend = nxt
        rsl = Buf(ar, o_ - 24576, 4096)
        for qb in range(2):
            sqb = RS.alloc(1024)
            T.op('dve', lambda qb=qb, sqb=sqb: V.tensor_copy(out=sqb.bf(), in_=b_ssl.f32()[:, qb * 512:(qb + 1) * 512]),
                 reads=b_ssl.keys(), writes=sqb.keys())
            ab = auxb.next()
            T.op('pe', lambda sqb=sqb, ab=ab: PE.matmul(PS(ab)[:, :], lhsT=ones, rhs=sqb.bf(), start=True, stop=True),
                 reads=sqb.keys() + B_cst.keys(), writes=PSK(ab))
            T.op('act', lambda qb=qb, ab=ab: S.activation(out=rsl.f32()[:, qb * 512:(qb + 1) * 512], in_=PS(ab)[:, :], func=AF.Ln, scale=1.0 / 2048, bias=b_eps.f32()),
                 reads=PSK(ab) + b_eps.keys(), writes=rsl.keys())
        T.op('act', lambda: S.activation(out=rsl.f32(), in_=rsl.f32(), func=AF.Exp, scale=-0.5), reads=rsl.keys(), writes=rsl.keys())
        for n in range(16):
            T.op('dve', lambda n=n: V.tensor_tensor(out=Ulv[:, n, :], in0=Ulv[:, n, :], in1=rsl.f32(), op=ALU.mult),
                 reads=rsl.keys() + B_Ul.sub(n * 2048, 2048).keys(), writes=B_Ul.sub(n * 2048, 2048).keys())
        if dbg:
            dump("Ul", B_Ul.bf(), [128, 16 * NA], BF16, reads=B_Ul.keys())
        if stop_after == 4:
            finish()
            return nc, dbg_outs
        T.barrier()

        WO = Ring(ar, BASE, BASE + 65536)
        B_gate = Buf(ar, BASE + 65536 + 32768, 16384)
        R3 = Ring(ar, B_gate.off + B_gate.n, KV0)
        T.dma('sp', B_Ua.bf(), ua_d[:, :], reads=["ua_scr"], writes=B_Ua.keys())
        T.dma('sp', B_gate.f32(), gate_d[0:1, :].partition_broadcast(128), reads=["gate_row"], writes=B_gate.keys())
        out_toks = []

        def wo_load(cg):
            wt = WO.alloc(32768)
            wv = wt.bf().rearrange("p (kc n) -> p kc n", kc=KC)
            for q4 in range(4):
                src = wout_d[q4 * 1024:(q4 + 1) * 1024, cg * 512:(cg + 1) * 512].rearrange("(kc p) n -> p kc n", p=128)
                T.dma('pool', wv[:, q4 * 8:(q4 + 1) * 8, :], src, writes=wt.sub(q4 * 8192, 8192).keys())
            return wt
        wos = {0: wo_load(0)}
        acc3 = Rot([0, 1, 2, 3])
        for cg in range(8):
            if cg + 1 < 8:
                wos[cg + 1] = wo_load(cg + 1)
            wt = wos.pop(cg)
            wv = wt.bf().rearrange("p (kc n) -> p kc n", kc=KC)
            for tt in range(8):
                xr = R3.alloc(2048)
                T.dma('sp', xr.f32(), xs_d[tt * 128:(tt + 1) * 128, cg * 512:(cg + 1) * 512], writes=xr.keys())
                bank = acc3.next()
                for kc in range(KC):
                    if kc < 16:
                        lhs = Uav[:, kc, tt * 128:(tt + 1) * 128]
                        lk = B_Ua.sub((kc * NA + tt * 128) * 2, 256).keys()
                    else:
                        lhs = Ulv[:, kc - 16, tt * 128:(tt + 1) * 128]
                        lk = B_Ul.sub(((kc - 16) * NA + tt * 128) * 2, 256).keys()
                    T.op('pe', lambda kc=kc, lhs=lhs: PE.matmul(PS(bank)[:, :], lhsT=lhs, rhs=wv[:, kc, :], start=(kc == 0), stop=(kc == KC - 1)),
                         reads=lk + wt.sub(kc * 1024, 1024).keys(), writes=PSK(bank), inc=(kc == KC - 1))
                t_ = R3.alloc(2048)
                T.op('dve', lambda: V.tensor_tensor(out=t_.f32(), in0=PS(bank)[:, :], in1=B_gate.f32()[:, cg * 512:(cg + 1) * 512], op=ALU.mult),
                     reads=PSK(bank) + B_gate.sub(cg * 2048, 2048).keys(), writes=t_.keys())
                T.op('pool', lambda: G.tensor_tensor(out=t_.f32(), in0=t_.f32(), in1=xr.f32(), op=ALU.add),
                     reads=t_.keys() + xr.keys(), writes=t_.keys())
                T.dma('sp', out_d[tt * 128:(tt + 1) * 128, cg * 512:(cg + 1) * 512], t_.f32(), reads=t_.keys(), writes=["out"])
        finish()
    return nc, dbg_outs


def _rope_tables(h):
    pos = np.arange(2048)
    if h == 1:
        pos = 2047 - pos
    row = (pos // 64).astype(np.float32)
    col = (pos % 64).astype(np.float32)
    n_freq = 32
    freqs = (np.float32(10000.0) ** (-np.arange(n_freq, dtype=np.float32) / np.float32(n_freq))).astype(np.float32)
    cos = np.zeros((128, 2048), np.float32)
    sin = np.zeros((128, 2048), np.float32)
    for d in range(128):
        axis = d // 64
        j = d % 32
        ang = (row if axis == 0 else col) * freqs[j]
        cos[d] = np.cos(ang)
        sin[d] = np.sin(ang)
    return cos, sin


def _rope_small(h):
    n_freq = 32
    freqs = (np.float32(10000.0) ** (-np.arange(n_freq, dtype=np.float32) / np.float32(n_freq))).astype(np.float32)
    ctab = np.zeros((128, 64), np.float32)
    stab = np.zeros((128, 64), np.float32)
    for d in range(128):
        j = d % 32
        if d < 64:
            idx = np.arange(32, dtype=np.float32)
            val = idx if h == 0 else (31 - idx)
            ang = (val * freqs[j]).astype(np.float32)
            ctab[d, 0:32] = np.cos(ang)
            stab[d, 0:32] = np.sin(ang)
        else:
            idx = np.arange(64, dtype=np.float32)
            val = idx if h == 0 else (63 - idx)
            ang = (val * freqs[j]).astype(np.float32)
            ctab[d] = np.cos(ang)
            stab[d] = np.sin(ang)
    return ctab, stab


def _consts():
    ident = np.eye(128, dtype=np.float32)
    ones = np.ones((128, 128), np.float32)
    rm = np.zeros((128, 128), np.float32)
    for m in range(128):
        half = (m % 64) // 32
        if half == 0:
            rm[m + 32, m] = -1.0
        else:
            rm[m - 32, m] = 1.0
    return np.concatenate([ident, ones, rm], axis=1).astype(ml_dtypes.bfloat16)


def make_in_maps(x, c, ctx, c_ctx, w_ada, b_ada, norm_w, w_in, q_norm_w, k_norm_w, conv_w, conv_b,
                 lru_wa, lru_ba, lru_wx, lru_bx, lru_lambda, out_norm_att, out_norm_lru, w_out):
    f = np.float32
    w_ada0 = np.ascontiguousarray(np.asarray(w_ada, f)[0])
    w_in0 = np.ascontiguousarray(np.asarray(w_in, f)[0])
    w_out0 = np.ascontiguousarray(np.asarray(w_out, f)[0])
    bada2 = np.ascontiguousarray(np.broadcast_to(np.asarray(b_ada, f)[0][None, :], (2, 3 * D)))
    cst = _consts()
    x = np.asarray(x, f)
    ctx = np.asarray(ctx, f)
    c = np.asarray(c, f)
    c_ctx = np.asarray(c_ctx, f)

    def pl(v, n):
        return np.asarray(v, f).reshape(n, 128).T
    in_maps = []
    for core in range(8):
        b, h = core // 2, core % 2
        xs = x[b]
        cx = ctx[b]
        if h == 1:
            xs = xs[::-1]
            cx = cx[::-1]
        prm = np.zeros((128, NPRM), f)
        prm[:, P_NW:P_NW + 32] = pl(norm_w[0], 32)
        cc = np.zeros((128, 32, 2), f)
        cc[:, :, 0] = pl(c[b], 32)
        cc[:, :, 1] = pl(c_ctx, 32)
        prm[:, P_CC:P_CC + 64] = cc.reshape(128, 64)
        prm[:, P_QW] = np.asarray(q_norm_w, f)[0]
        prm[:, P_KW] = np.asarray(k_norm_w, f)[0]
        cw = np.asarray(conv_w, f)[0]
        w5 = np.zeros((5, 2048), f)
        if h == 0:
            w5[0:4] = cw
        else:
            w5[1:5] = cw[::-1]
        prm[:, P_W5:P_W5 + 80] = w5.reshape(5, 16, 128).transpose(2, 1, 0).reshape(128, 80)
        prm[:, P_CB:P_CB + 16] = pl(np.asarray(conv_b, f)[0], 16)
        dirs = (0, 1) if h == 0 else (1, 0)
        for dd, d in enumerate(dirs):
            prm[:, P_BA + dd * 16:P_BA + dd * 16 + 16] = pl(np.asarray(lru_ba, f)[0, d], 16)
            prm[:, P_BX + dd * 16:P_BX + dd * 16 + 16] = pl(np.asarray(lru_bx, f)[0, d], 16)
            prm[:, P_LAM + dd * 16:P_LAM + dd * 16 + 16] = pl(np.asarray(lru_lambda, f)[0, d], 16)
        prm[:, P_WNA:P_WNA + 16] = pl(np.asarray(out_norm_att, f)[0], 16)
        prm[:, P_WNL:P_WNL + 16] = pl(np.asarray(out_norm_lru, f)[0], 16)
        ctab, stab = _rope_small(h)
        prm[:, P_COS:P_COS + 64] = ctab
        prm[:, P_SIN:P_SIN + 64] = stab
        prm[0, P_ID2] = 1.0
        prm[1, P_ID2 + 1] = 1.0
        wa = np.asarray(lru_wa, f)[0]
        wx = np.asarray(lru_wx, f)[0]
        wg = np.zeros((128, 16, 2, 2, 128), f)
        for dd, d in enumerate(dirs):
            wg[:, :, dd, 0, :] = wa[d].transpose(1, 0, 2)
            wg[:, :, dd, 1, :] = wx[d].transpose(1, 0, 2)
        cos, sin = _rope_tables(h)
        cs = np.concatenate([cos, sin], axis=1).astype(ml_dtypes.bfloat16)
        in_maps.append(dict(xs=np.ascontiguousarray(xs), cx=np.ascontiguousarray(cx), w_ada=w_ada0, w_in=w_in0, w_out=w_out0,
                            prm=prm, bada2=bada2, cst=cst, cs=cs, idf=np.eye(128, dtype=np.float32), wg=np.ascontiguousarray(wg.reshape(128, 16 * 512))))
    return in_maps


_NC_CACHE = {}


def kernel(x, c, ctx, c_ctx, w_ada, b_ada, norm_w, w_in, q_norm_w, k_norm_w, conv_w, conv_b,
           lru_wa, lru_ba, lru_wx, lru_bx, lru_lambda, out_norm_att, out_norm_lru, w_out):
    in_maps = make_in_maps(x, c, ctx, c_ctx, w_ada, b_ada, norm_w, w_in, q_norm_w, k_norm_w, conv_w, conv_b,
                           lru_wa, lru_ba, lru_wx, lru_bx, lru_lambda, out_norm_att, out_norm_lru, w_out)
    if 'nc' not in _NC_CACHE:
        _NC_CACHE['nc'] = build_nc()[0]
    nc = _NC_CACHE['nc']
    res = run_bass_kernel_spmd(nc, in_maps, core_ids=list(range(8)))
    out = np.zeros((4, 2048, D), np.float32)
    for core in range(8):
        b, h = core // 2, core % 2
        o = np.asarray(res.results[core]["out"], np.float32)
        if h == 0:
            out[b, 0:1024] = o
        else:
            out[b, 1024:2048] = o[::-1]
    return out
```

```python
import numpy as np
import ml_dtypes
from contextlib import ExitStack
import concourse.bass as bass
import concourse.mybir as mybir
from concourse.bass_utils import run_bass_kernel_spmd

F32 = mybir.dt.float32
BF16 = mybir.dt.bfloat16
AF = mybir.ActivationFunctionType
ALU = mybir.AluOpType

D = 4096
KC = 32
NA = 1024
NH = 128
NB = 1024
NCX = 256
T1 = NH + NB + NCX
NKEY = NA + NB + NCX
DIN = 9216
C_Q, C_K, C_V, C_GA, C_XL, C_GL = 0, 2048, 2560, 3072, 5120, 7168
EPS = 1e-6
ARENA_F32 = 53000
GRAN = 256

P_NW = 0
P_CC = 32
P_QW = 96
P_KW = 97
P_W5 = 98
P_CB = 178
P_BA = 194
P_BX = 226
P_LAM = 258
P_WNA = 290
P_WNL = 306
P_ID2 = 322
P_COS = 324
P_SIN = 388
NPRM = 452


class Buf:
    def __init__(self, ar, off, n):
        self.ar, self.off, self.n = ar, off, n

    def f32(self):
        return self.ar[:, self.off // 4:(self.off + self.n) // 4]

    def bf(self):
        return self.f32().bitcast(BF16)

    def sub(self, o, n):
        return Buf(self.ar, self.off + o, n)

    def keys(self):
        return list(range(self.off // GRAN, (self.off + self.n - 1) // GRAN + 1))


class Ring:
    def __init__(self, ar, start, end):
        self.ar, self.start, self.end, self.p = ar, start, end, start

    def alloc(self, n):
        n = (n + GRAN - 1) // GRAN * GRAN
        assert n <= self.end - self.start, (n, self.start, self.end)
        if self.p + n > self.end:
            self.p = self.start
        b = Buf(self.ar, self.p, n)
        self.p += n
        return b


class Rot:
    def __init__(self, items):
        self.items, self.i = list(items), 0

    def next(self):
        v = self.items[self.i % len(self.items)]
        self.i += 1
        return v


class TR:
    def __init__(self, nc, es, n_sp=20, n_pool=10):
        self.nc = nc
        self.E = {}
        self.sems = {}
        for n, attr in (('pe', 'tensor'), ('act', 'scalar'), ('dve', 'vector'), ('pool', 'gpsimd'), ('sp', 'sync')):
            sem = es.enter_context(nc.semaphore("c_" + n))
            self.E[n] = dict(h=getattr(nc, attr), sem=sem, key="c_" + n, count=0, known={}, pend=False)
            self.sems["c_" + n] = sem
        self.dsl = {}
        for q, n in (('sp', n_sp), ('pool', n_pool)):
            sl = []
            for i in range(n):
                key = "d_%s%d" % (q, i)
                sem = es.enter_context(nc.semaphore(key))
                self.sems[key] = sem
                sl.append(dict(key=key, sem=sem, val=0))
            self.dsl[q] = dict(slots=sl, nxt=0)
        self.lastw = {}
        self.readers = {}
        self.nwait = 0

    def _wait(self, E, reads, writes, excl=()):
        need = {}
        strict = E['key'] != 'c_pe'

        def add(k, v, same_ok):
            if k == E['key'] and same_ok:
                return
            if need.get(k, 0) < v:
                need[k] = v
        for r in reads:
            t = self.lastw.get(r)
            if t is not None:
                add(t[0], t[1], False)
        for w in writes:
            t = self.lastw.get(w)
            if t is not None:
                add(t[0], t[1], not strict)
            rd = self.readers.get(w)
            if rd:
                for k, v in rd.items():
                    add(k, v, not strict)
        for w in excl:
            t = self.lastw.get(w)
            if t is not None:
                add(t[0], t[1], True)
            rd = self.readers.get(w)
            if rd:
                for k, v in rd.items():
                    add(k, v, True)
        for k, v in need.items():
            if E['known'].get(k, 0) < v:
                E['h'].wait_ge(self.sems[k], v)
                E['known'][k] = v
                self.nwait += 1

    def _record(self, tok, reads, writes):
        for w in writes:
            self.lastw[w] = tok
            self.readers[w] = {}
        for r in reads:
            d = self.readers.setdefault(r, {})
            if d.get(tok[0], 0) < tok[1]:
                d[tok[0]] = tok[1]

    def op(self, eng, fn, reads=(), writes=(), inc=True):
        E = self.E[eng]
        excl = [k for k in reads if isinstance(k, tuple) and k not in writes]
        self._wait(E, reads, writes, excl)
        writes = list(writes) + excl
        ins = fn()
        if inc:
            E['count'] += 1
            ins.then_inc(E['sem'], 1)
            tok = (E['key'], E['count'])
            E['pend'] = False
        else:
            tok = (E['key'], E['count'] + 1)
            E['pend'] = True
        self._record(tok, reads, writes)
        return tok

    def dma(self, q, out, in_, reads=(), writes=()):
        E = self.E[q]
        self._wait(E, reads, writes)
        d = self.dsl[q]
        sl = d['slots'][d['nxt'] % len(d['slots'])]
        d['nxt'] += 1
        if sl['val'] > 0 and E['known'].get(sl['key'], 0) < sl['val']:
            E['h'].wait_ge(sl['sem'], sl['val'])
            E['known'][sl['key']] = sl['val']
        ins = E['h'].dma_start(out=out, in_=in_)
        sl['val'] += 16
        ins.then_inc(sl['sem'], 16)
        tok = (sl['key'], sl['val'])
        self._record(tok, reads, writes)
        return tok

    def barrier(self, engines=('pe', 'act', 'dve', 'pool', 'sp')):
        for e in self.E.values():
            assert not e['pend'], "pending un-inc'd op at barrier"
        tgt = {}
        for e in self.E.values():
            if e['count'] > 0:
                tgt[e['key']] = e['count']
        for d in self.dsl.values():
            for sl in d['slots']:
                if sl['val'] > 0:
                    tgt[sl['key']] = sl['val']
        for n in engines:
            E = self.E[n]
            for k, v in tgt.items():
                if k == E['key']:
                    continue
                if E['known'].get(k, 0) < v:
                    E['h'].wait_ge(self.sems[k], v)
                    E['known'][k] = v


def build_nc(stop_after=None, dbg=False):
    nc = bass.Bass("TRN2", target_bir_lowering=False)
    xs_d = nc.dram_tensor("xs", [NA + NB, D], F32, kind="ExternalInput").ap()
    cx_d = nc.dram_tensor("cx", [NCX, D], F32, kind="ExternalInput").ap()
    wada_d = nc.dram_tensor("w_ada", [D, 3 * D], F32, kind="ExternalInput").ap()
    win_d = nc.dram_tensor("w_in", [D, DIN], F32, kind="ExternalInput").ap()
    wout_d = nc.dram_tensor("w_out", [D, D], F32, kind="ExternalInput").ap()
    prm_d = nc.dram_tensor("prm", [128, NPRM], F32, kind="ExternalInput").ap()
    bada_d = nc.dram_tensor("bada2", [2, 3 * D], F32, kind="ExternalInput").ap()
    cst_d = nc.dram_tensor("cst", [128, 384], BF16, kind="ExternalInput").ap()
    cs_d = nc.dram_tensor("cs", [128, 4096], BF16, kind="ExternalInput").ap()
    idf_d = nc.dram_tensor("idf", [128, 128], F32, kind="ExternalInput").ap()
    wg_d = nc.dram_tensor("wg", [128, 16 * 512], F32, kind="ExternalInput").ap()
    out_d = nc.dram_tensor("out", [NA, D], F32, kind="ExternalOutput").ap()
    gate_d = nc.dram_tensor("gate_row", [1, D], F32, kind="Internal").ap()
    ua_d = nc.dram_tensor("ua_scr", [128, 16 * NA], BF16, kind="Internal").ap()
    dbg_outs = {}

    es = ExitStack()
    with es:
        ar = es.enter_context(nc.sbuf_tensor("arena", [128, ARENA_F32], F32))
        psum = es.enter_context(nc.psum_tensor("psum", [128, 8, 512], F32))
        T = TR(nc, es)
        V, S, G, PE = nc.vector, nc.scalar, nc.gpsimd, nc.tensor

        def PS(b):
            return psum[:, b, :]

        def PSK(b):
            return [("ps", b)]

        SM = Buf(ar, 0, 10240)
        sm_p = [0]

        def small(nbytes):
            n = (nbytes + 31) // 32 * 32
            b = SM.sub(sm_p[0], nbytes)
            sm_p[0] += n
            assert sm_p[0] <= SM.n
            return b
        B_cst = small(768)
        B_prm = small(NPRM * 4)
        B_cs = Buf(ar, 10240, 8192)
        WB = Ring(ar, 18432, 18432 + 24576)
        WG = Ring(ar, 43008, 43008 + 2048)
        BASE = 45056
        TOP = 211968
        KV0 = 175104

        cst = B_cst.bf()
        ident = cst[:, 0:128]
        ones = cst[:, 128:256]
        rmat = cst[:, 256:384]
        prm = B_prm.f32()
        cs = B_cs.bf()

        def pcol(c, n=1):
            return prm[:, c:c + n]

        T.dma('sp', cst, cst_d[:, :], writes=B_cst.keys())
        T.dma('sp', prm, prm_d[:, :], writes=B_prm.keys())
        T.dma('sp', cs, cs_d[:, :], writes=B_cs.keys())

        b_idf = small(512)
        T.dma('sp', b_idf.f32(), idf_d[:, :], writes=b_idf.keys())
        b_eps = small(4)
        b_gb = small(2048)
        b_grow = small(2048)
        b_one = small(4)
        b_mh = small(4)
        b_th = small(256)
        b_s1 = small(256)
        b_csb = small(128)
        b_modT = small(512)
        b_gx = small(128)
        b_gc = small(128)
        b_qws = small(4)
        b_c05 = small(128)
        b_ba05 = small(128)
        b_bx05 = small(128)
        b_wa05 = small(64)
        b_wl05 = small(64)
        b_stF = small(64)
        b_stB = small(64)
        b_xlB2 = small(128)
        b_ss = small(96)
        b_ms = small(96)
        b_rstd = small(96)
        b_tmpc = small(128)

        T.op('dve', lambda: V.memset(b_eps.f32(), EPS), writes=b_eps.keys())
        T.op('dve', lambda: V.memset(b_one.f32(), 1.0), writes=b_one.keys())
        T.op('dve', lambda: V.memset(b_mh.f32(), -0.5), writes=b_mh.keys())

        cc = pcol(P_CC, 64)
        T.op('act', lambda: S.activation(out=b_th.f32(), in_=cc, func=AF.Tanh, scale=0.5),
             reads=B_prm.keys(), writes=b_th.keys())
        T.op('dve', lambda: V.scalar_tensor_tensor(out=b_s1.f32(), in0=b_th.f32(), scalar=1.0, in1=cc,
                                                   op0=ALU.add, op1=ALU.mult),
             reads=b_th.keys() + B_prm.keys(), writes=b_s1.keys())
        T.op('dve', lambda: V.tensor_scalar(out=b_csb.bf(), in0=b_s1.f32(), scalar1=0.5, scalar2=None, op0=ALU.mult),
             reads=b_s1.keys(), writes=b_csb.keys())
        csb = b_csb.bf()
        T.op('dve', lambda: V.tensor_scalar(out=b_qws.f32(), in0=pcol(P_QW), scalar1=float(128 ** -0.5), scalar2=None,
                                            op0=ALU.mult), reads=B_prm.keys(), writes=b_qws.keys())
        T.op('dve', lambda: V.tensor_scalar(out=b_ba05.f32(), in0=pcol(P_BA, 32), scalar1=0.5, scalar2=None, op0=ALU.mult),
             reads=B_prm.keys(), writes=b_ba05.keys())
        T.op('dve', lambda: V.tensor_scalar(out=b_bx05.f32(), in0=pcol(P_BX, 32), scalar1=0.5, scalar2=None, op0=ALU.mult),
             reads=B_prm.keys(), writes=b_bx05.keys())
        T.op('dve', lambda: V.tensor_scalar(out=b_wa05.f32(), in0=pcol(P_WNA, 16), scalar1=0.5, scalar2=None, op0=ALU.mult),
             reads=B_prm.keys(), writes=b_wa05.keys())
        T.op('dve', lambda: V.tensor_scalar(out=b_wl05.f32(), in0=pcol(P_WNL, 16), scalar1=0.5, scalar2=None, op0=ALU.mult),
             reads=B_prm.keys(), writes=b_wl05.keys())
        T.op('act', lambda: S.activation(out=b_tmpc.f32(), in_=pcol(P_LAM, 32), func=AF.Exp, scale=-1.0),
             reads=B_prm.keys(), writes=b_tmpc.keys())
        T.op('act', lambda: S.activation(out=b_tmpc.f32(), in_=b_tmpc.f32(), func=AF.Ln, bias=b_one.f32(), scale=1.0),
             reads=b_tmpc.keys() + b_one.keys(), writes=b_tmpc.keys())
        T.op('dve', lambda: V.tensor_scalar(out=b_c05.f32(), in0=b_tmpc.f32(), scalar1=-4.0, scalar2=None, op0=ALU.mult),
             reads=b_tmpc.keys(), writes=b_c05.keys())

        def dump(name, ap, shape, dt=F32, reads=()):
            if not dbg:
                return
            d = nc.dram_tensor("dbg_" + name, list(shape), dt, kind="ExternalOutput").ap()
            dbg_outs[name] = d
            T.dma('sp', d, ap, reads=list(reads))

        def finish():
            T.barrier(engines=('sp',))

        wsched = []
        for g in range(4):
            wsched.append(('in', C_K + g * 128))
        for j in range(4):
            wsched.append(('in', C_V + j * 128))
        wsched.append(('in', C_XL))
        for n in range(16):
            if n + 1 < 16:
                wsched.append(('in', C_XL + (n + 1) * 128))
            wsched.append(('ada', 2 * D + (2 * n) * 128))
            wsched.append(('ada', 2 * D + (2 * n + 1) * 128))
        for g in range(4):
            wsched.append(('in', C_K + g * 128))
        for j in range(4):
            wsched.append(('in', C_V + j * 128))
        for h in range(16):
            wsched.append(('in', C_Q + h * 128))
            wsched.append(('in', C_GA + h * 128))
        for n in range(16):
            wsched.append(('in', C_XL + n * 128))
            wsched.append(('in', C_GL + n * 128))
        wstate = dict(issued=0, used=0, tiles={})

        def w_issue():
            i = wstate['issued']
            if i >= len(wsched):
                return
            kind, col = wsched[i]
            b = WB.alloc(8192)
            src = (win_d if kind == 'in' else wada_d)[:, col:col + 128].rearrange("(kc p) n -> p kc n", p=128)
            dst = b.bf().rearrange("p (kc n) -> p kc n", kc=KC)
            T.dma('pool', dst, src, writes=b.keys())
            wstate['tiles'][i] = b
            wstate['issued'] += 1

        def w_next(expect_col, kind='in'):
            i = wstate['used']
            assert wsched[i] == (kind, expect_col), (i, wsched[i], expect_col)
            while wstate['issued'] < min(i + 3, len(wsched)):
                w_issue()
            b = wstate['tiles'].pop(i)
            wstate['used'] += 1
            return b

        R0 = Ring(ar, BASE, BASE + 2 * 32768 + 16384)
        tp_bank = 7
        modps = PS(tp_bank)
        for cg in range(16):
            wt = R0.alloc(32768)
            wv = wt.bf().rearrange("p (kc n) -> p kc n", kc=KC)
            for q4 in range(4):
                src = wada_d[q4 * 1024:(q4 + 1) * 1024, cg * 512:(cg + 1) * 512].rearrange("(kc p) n -> p kc n", p=128)
                T.dma('pool', wv[:, q4 * 8:(q4 + 1) * 8, :], src, writes=wt.sub(q4 * 8192, 8192).keys())
            bb = R0.alloc(2048)
            T.dma('sp', bb.f32()[0:2, :], bada_d[:, cg * 512:(cg + 1) * 512], writes=bb.keys())
            bank = cg % 2
            for kc in range(KC):
                T.op('pe', lambda kc=kc: PE.matmul(PS(bank)[0:2, :], lhsT=csb[:, 2 * kc:2 * kc + 2], rhs=wv[:, kc, :],
                                                   start=(kc == 0), stop=(kc == KC - 1)),
                     reads=b_csb.keys() + wt.sub(kc * 1024, 1024).keys(), writes=PSK(bank), inc=(kc == KC - 1))
            row = R0.alloc(2048)
            T.op('dve', lambda: V.tensor_tensor(out=row.f32()[0:2, :], in0=PS(bank)[0:2, :], in1=bb.f32()[0:2, :], op=ALU.add),
                 reads=PSK(bank) + bb.keys(), writes=row.keys())
            if cg < 16:
                for i in range(4):
                    j = cg * 4 + i
                    T.op('pe', lambda i=i, j=j: PE.transpose(out=modps[:, 2 * j:2 * j + 2], in_=row.f32()[0:2, i * 128:(i + 1) * 128],
                                                             identity=prm[0:2, P_ID2:P_ID2 + 2]),
                         reads=row.keys() + B_prm.keys(), writes=PSK(tp_bank), inc=(i == 3))
            else:
                T.dma('sp', gate_d[0:1, (cg - 16) * 512:(cg - 15) * 512], row.f32()[0:1, :], reads=row.keys(), writes=["gate_row"])
        T.op('dve', lambda: V.tensor_copy(out=b_modT.f32(), in_=modps[:, 0:128]), reads=PSK(tp_bank), writes=b_modT.keys())
        modR = b_modT.f32().rearrange("p (j r) -> p r j", r=2)
        nw = pcol(P_NW, 32)
        T.op('dve', lambda: V.scalar_tensor_tensor(out=b_gx.f32(), in0=modR[:, 0:1, 32:64].rearrange("p o j -> p (o j)"), scalar=1.0, in1=nw, op0=ALU.add, op1=ALU.mult),
             reads=b_modT.keys() + B_prm.keys(), writes=b_gx.keys())
        T.op('dve', lambda: V.scalar_tensor_tensor(out=b_gc.f32(), in0=modR[:, 1:2, 32:64].rearrange("p o j -> p (o j)"), scalar=1.0, in1=nw, op0=ALU.add, op1=ALU.mult),
             reads=b_modT.keys() + B_prm.keys(), writes=b_gc.keys())

        def gsel(isctx, kc):
            g = (b_gc if isctx else b_gx).f32()[:, kc:kc + 1]
            r = 1 if isctx else 0
            s = b_modT.f32()[:, kc * 2 + r:kc * 2 + r + 1]
            return g, s
        if dbg:
            dump("modT", b_modT.f32(), [128, 128], reads=b_modT.keys())
            dump("gx", b_gx.f32(), [128, 32], reads=b_gx.keys())
            dump("c05", b_c05.f32(), [128, 32], reads=b_c05.keys())
        if stop_after == 0:
            finish()
            return nc, dbg_outs
        T.barrier()

        def prep(groups, hT, RP):
            hv = hT.bf().rearrange("p (kc t) -> p kc t", kc=KC)
            ntok = hT.n // 2 // KC
            tpb = Rot([4, 5, 6, 7])
            junk = Buf(ar, RP.end, 8192)
            for grp in groups:
                xss = []
                for (rows, isctx, si, tok0) in grp:
                    xt = RP.alloc(16384)
                    T.dma('sp', xt.f32(), rows, writes=xt.keys())
                    ssa_ = b_ss.f32()[:, si:si + 1]
                    T.op('act', lambda xt=xt, junk=junk, ssa_=ssa_: S.activation(out=junk.bf(), in_=xt.f32(), func=AF.Square, accum_out=ssa_),
                         reads=xt.keys(), writes=junk.keys() + b_ss.keys())
                    msa = b_ms.f32()[:, si:si + 1]
                    T.op('dve', lambda ssa_=ssa_, msa=msa: V.tensor_scalar(out=msa, in0=ssa_, scalar1=1.0 / D, scalar2=EPS, op0=ALU.mult, op1=ALU.add),
                         reads=b_ss.keys(), writes=b_ms.keys())
                    rsa = b_rstd.f32()[:, si:si + 1]
                    T.op('pool', lambda msa=msa, rsa=rsa: G.tensor_tensor(out=rsa, in0=msa, in1=b_mh.f32(), op=ALU.pow),
                         reads=b_ms.keys() + b_mh.keys(), writes=b_rstd.keys())
                    dg = RP.alloc(512)
                    T.op('dve', lambda dg=dg, rsa=rsa: V.tensor_scalar(out=dg.f32(), in0=b_idf.f32(), scalar1=rsa, scalar2=None, op0=ALU.mult),
                         reads=b_rstd.keys() + b_idf.keys(), writes=dg.keys())
                    xss.append((xt, dg, isctx, tok0))
                ng = len(xss)
                tok0 = xss[0][3]
                isctx = xss[0][2]
                for k2 in range(KC // 2):
                    bank = tpb.next()
                    pb = PS(bank)
                    for kk in range(2):
                        kc = k2 * 2 + kk
                        for j, (xt, dg, _, _) in enumerate(xss):
                            T.op('pe', lambda kk=kk, kc=kc, j=j, xt=xt, dg=dg, pb=pb: PE.matmul(
                                pb[:, kk * 256 + j * 128:kk * 256 + (j + 1) * 128],
                                lhsT=xt.f32()[:, kc * 128:(kc + 1) * 128], rhs=dg.f32(), start=True, stop=True),
                                reads=xt.sub(kc * 512, 512).keys() + dg.keys(), writes=PSK(bank),
                                inc=(kk == 1 and j == ng - 1))
                    for kk in range(2):
                        kc = k2 * 2 + kk
                        g, s = gsel(isctx, kc)
                        dst = hv[:, kc, tok0:tok0 + ng * 128]
                        dkeys = hT.sub((kc * ntok + tok0) * 2, ng * 256).keys()
                        src = pb[:, kk * 256:kk * 256 + ng * 128]
                        if kk % 2 == 0:
                            T.op('dve', lambda dst=dst, src=src, g=g, s=s: V.tensor_scalar(out=dst, in0=src, scalar1=g, scalar2=s,
                                                                                      op0=ALU.mult, op1=ALU.add),
                                 reads=PSK(bank) + b_gx.keys() + b_gc.keys() + b_modT.keys(), writes=dkeys)
                        else:
                            T.op('act', lambda dst=dst, src=src, g=g, s=s: S.activation(out=dst, in_=src, func=AF.Identity, scale=g, bias=s),
                                 reads=PSK(bank) + b_gx.keys() + b_gc.keys() + b_modT.keys(), writes=dkeys)

        def hkeys(hT, ntok, kc, t0, n):
            return hT.sub((kc * ntok + t0) * 2, n * 2).keys()

        def proj(wt, hT, ntok, t0, n, bank):
            hv = hT.bf().rearrange("p (kc t) -> p kc t", kc=KC)
            wv = wt.bf().rearrange("p (kc n) -> p kc n", kc=KC)
            for kc in range(KC):
                T.op('pe', lambda kc=kc: PE.matmul(PS(bank)[:, 0:n], lhsT=wv[:, kc, :], rhs=hv[:, kc, t0:t0 + n],
                                                   start=(kc == 0), stop=(kc == KC - 1)),
                     reads=wt.sub(kc * 256, 256).keys() + hkeys(hT, ntok, kc, t0, n), writes=PSK(bank), inc=(kc == KC - 1))

        def proj_tm(wt, hT, ntok, t0, bank, c0):
            hv = hT.bf().rearrange("p (kc t) -> p kc t", kc=KC)
            wv = wt.bf().rearrange("p (kc n) -> p kc n", kc=KC)
            for kc in range(KC):
                T.op('pe', lambda kc=kc: PE.matmul(PS(bank)[:, c0:c0 + 128], lhsT=hv[:, kc, t0:t0 + 128], rhs=wv[:, kc, :],
                                                   start=(kc == 0), stop=(kc == KC - 1)),
                     reads=wt.sub(kc * 256, 256).keys() + hkeys(hT, ntok, kc, t0, 128), writes=PSK(bank), inc=(kc == KC - 1))

        accb = Rot([0, 1])
        auxb = Rot([6, 7])

        def rope_mul(out_b, src_fn, src_keys, pcol0, cs0, n):
            r0, nr = cs0 // 64, n // 64
            for ph in range(2):
                lo = ph * 64
                if ph == 0:
                    tab = prm[lo:lo + 64, pcol0 + r0:pcol0 + r0 + nr].unsqueeze(2).broadcast_to([64, nr, 64])
                else:
                    tab = prm[lo:lo + 64, pcol0:pcol0 + 64].unsqueeze(1).broadcast_to([64, nr, 64])
                o3 = out_b.f32()[lo:lo + 64, 0:n].rearrange("p (r c) -> p r c", c=64)
                i3 = src_fn(lo).rearrange("p (r c) -> p r c", c=64)
                T.op('dve', lambda o3=o3, i3=i3, tab=tab: V.tensor_tensor(out=o3, in0=i3, in1=tab, op=ALU.mult),
                     reads=src_keys + B_prm.keys(), writes=out_b.keys())

        def normrope(bank, n, w_ap, w_keys, cs0, dst, dkeys, RR):
            sq = RR.alloc(n * 2)
            T.op('act', lambda: S.activation(out=sq.bf(), in_=PS(bank)[:, 0:n], func=AF.Square), reads=PSK(bank), writes=sq.keys())
            ab = auxb.next()
            T.op('pe', lambda: PE.matmul(PS(ab)[:, 0:n], lhsT=ones, rhs=sq.bf(), start=True, stop=True),
                 reads=sq.keys() + B_cst.keys(), writes=PSK(ab))
            lnb = RR.alloc(n * 4)
            T.op('act', lambda: S.activation(out=lnb.f32(), in_=PS(ab)[:, 0:n], func=AF.Ln, scale=1.0 / 128, bias=b_eps.f32()),
                 reads=PSK(ab) + b_eps.keys(), writes=lnb.keys())
            T.op('act', lambda: S.activation(out=lnb.f32(), in_=lnb.f32(), func=AF.Exp, scale=-0.5), reads=lnb.keys(), writes=lnb.keys())
            if cs0 is None:
                T.op('dve', lambda: V.scalar_tensor_tensor(out=dst, in0=PS(bank)[:, 0:n], scalar=w_ap, in1=lnb.f32(), op0=ALU.mult, op1=ALU.mult),
                     reads=PSK(bank) + lnb.keys() + w_keys, writes=dkeys)
                return
            qn = RR.alloc(n * 2)
            T.op('dve', lambda: V.scalar_tensor_tensor(out=qn.bf(), in0=PS(bank)[:, 0:n], scalar=w_ap, in1=lnb.f32(), op0=ALU.mult, op1=ALU.mult),
                 reads=PSK(bank) + lnb.keys() + w_keys, writes=qn.keys())
            ab2 = auxb.next()
            T.op('pe', lambda: PE.matmul(PS(ab2)[:, 0:n], lhsT=rmat, rhs=qn.bf(), start=True, stop=True),
                 reads=qn.keys() + B_cst.keys(), writes=PSK(ab2))
            t1 = RR.alloc(n * 4)
            rope_mul(t1, lambda lo: qn.bf()[lo:lo + 64, 0:n], qn.keys(), P_COS, cs0, n)
            t2 = RR.alloc(n * 4)
            rope_mul(t2, lambda lo: PS(ab2)[lo:lo + 64, 0:n], PSK(ab2), P_SIN, cs0, n)
            T.op('dve', lambda: V.tensor_tensor(out=dst, in0=t1.f32(), in1=t2.f32(), op=ALU.add),
                 reads=t1.keys() + t2.keys(), writes=dkeys)

        B_kT = Buf(ar, KV0, 18432)
        B_v = Buf(ar, KV0 + 18432, 18432)
        kTv = B_kT.bf().rearrange("p (g t) -> p g t", g=4)
        vv = B_v.bf().rearrange("p (t c) -> p t c", t=18)

        def kkeys(g, t0, n):
            return B_kT.sub((g * NKEY + t0) * 2, n * 2).keys()

        def vkeys(t, c0, n):
            return B_v.sub((t * 512 + c0) * 2, n * 2).keys()

        def wg_load(n):
            b = WG.alloc(1024)
            T.dma('pool', b.bf(), wg_d[:, n * 512:(n + 1) * 512], writes=b.keys())
            return b

        def gates(wgb, n, d, ub_ap, ub_keys, u_ap, u_keys, a_b, x_b, m_b, nt):
            za = accb.next()
            T.op('pe', lambda: PE.matmul(PS(za)[:, 0:nt], lhsT=wgb.bf()[:, (d * 2 + 0) * 128:(d * 2 + 1) * 128], rhs=ub_ap, start=True, stop=True),
                 reads=wgb.keys() + ub_keys, writes=PSK(za))
            zx = accb.next()
            T.op('pe', lambda: PE.matmul(PS(zx)[:, 0:nt], lhsT=wgb.bf()[:, (d * 2 + 1) * 128:(d * 2 + 2) * 128], rhs=ub_ap, start=True, stop=True),
                 reads=wgb.keys() + ub_keys, writes=PSK(zx))
            idx = d * 16 + n
            T.op('act', lambda: S.activation(out=a_b.f32(), in_=PS(za)[:, 0:nt], func=AF.Tanh, scale=0.5, bias=b_ba05.f32()[:, idx:idx + 1]),
                 reads=PSK(za) + b_ba05.keys(), writes=a_b.keys())
            T.op('act', lambda: S.activation(out=x_b.f32(), in_=PS(zx)[:, 0:nt], func=AF.Tanh, scale=0.5, bias=b_bx05.f32()[:, idx:idx + 1]),
                 reads=PSK(zx) + b_bx05.keys(), writes=x_b.keys())
            c05 = b_c05.f32()[:, idx:idx + 1]
            T.op('act', lambda: S.activation(out=a_b.f32(), in_=a_b.f32(), func=AF.Exp, scale=c05, bias=c05),
                 reads=a_b.keys() + b_c05.keys(), writes=a_b.keys())
            T.op('dve', lambda: V.tensor_tensor(out=m_b.f32(), in0=a_b.f32(), in1=a_b.f32(), op=ALU.mult), reads=a_b.keys(), writes=m_b.keys())
            T.op('act', lambda: S.activation(out=m_b.f32(), in_=m_b.f32(), func=AF.Ln, scale=-1.0, bias=b_one.f32()),
                 reads=m_b.keys() + b_one.keys(), writes=m_b.keys())
            T.op('act', lambda: S.activation(out=m_b.f32(), in_=m_b.f32(), func=AF.Exp, scale=0.5), reads=m_b.keys(), writes=m_b.keys())
            T.op('dve', lambda: V.scalar_tensor_tensor(out=x_b.f32(), in0=x_b.f32(), scalar=1.0, in1=m_b.f32(), op0=ALU.add, op1=ALU.mult),
                 reads=x_b.keys() + m_b.keys(), writes=x_b.keys())
            T.op('dve', lambda: V.scalar_tensor_tensor(out=x_b.f32(), in0=x_b.f32(), scalar=0.5, in1=u_ap, op0=ALU.mult, op1=ALU.mult),
                 reads=x_b.keys() + u_keys, writes=x_b.keys())

        accb4 = Rot([0, 1, 2, 3])

        def gate_mm_tanh(wgb, n, d, ub_ap, ub_keys, a_ap, a_keys, x_ap, x_keys, nt):
            za = accb4.next()
            T.op('pe', lambda: PE.matmul(PS(za)[:, 0:nt], lhsT=wgb.bf()[:, (d * 2 + 0) * 128:(d * 2 + 1) * 128], rhs=ub_ap, start=True, stop=True),
                 reads=wgb.keys() + ub_keys, writes=PSK(za))
            zx = accb4.next()
            T.op('pe', lambda: PE.matmul(PS(zx)[:, 0:nt], lhsT=wgb.bf()[:, (d * 2 + 1) * 128:(d * 2 + 2) * 128], rhs=ub_ap, start=True, stop=True),
                 reads=wgb.keys() + ub_keys, writes=PSK(zx))
            idx = d * 16 + n
            T.op('act', lambda: S.activation(out=a_ap, in_=PS(za)[:, 0:nt], func=AF.Tanh, scale=0.5, bias=b_ba05.f32()[:, idx:idx + 1]),
                 reads=PSK(za) + b_ba05.keys(), writes=a_keys)
            T.op('act', lambda: S.activation(out=x_ap, in_=PS(zx)[:, 0:nt], func=AF.Tanh, scale=0.5, bias=b_bx05.f32()[:, idx:idx + 1]),
                 reads=PSK(zx) + b_bx05.keys(), writes=x_keys)

        def gate_exp(n, d, a_ap, a_keys):
            idx = d * 16 + n
            c05 = b_c05.f32()[:, idx:idx + 1]
            T.op('act', lambda: S.activation(out=a_ap, in_=a_ap, func=AF.Exp, scale=c05, bias=c05),
                 reads=a_keys + b_c05.keys(), writes=a_keys)

        def gate_mult(a_b, x_b, m_b):
            T.op('dve', lambda: V.tensor_tensor(out=m_b.f32(), in0=a_b.f32(), in1=a_b.f32(), op=ALU.mult), reads=a_b.keys(), writes=m_b.keys())
            T.op('act', lambda: S.activation(out=m_b.f32(), in_=m_b.f32(), func=AF.Ln, scale=-1.0, bias=b_one.f32()),
                 reads=m_b.keys() + b_one.keys(), writes=m_b.keys())
            T.op('act', lambda: S.activation(out=m_b.f32(), in_=m_b.f32(), func=AF.Exp, scale=0.5), reads=m_b.keys(), writes=m_b.keys())
            T.op('dve', lambda: V.scalar_tensor_tensor(out=x_b.f32(), in0=x_b.f32(), scalar=1.0, in1=m_b.f32(), op0=ALU.add, op1=ALU.mult),
                 reads=x_b.keys() + m_b.keys(), writes=x_b.keys())

        def rev(ap2d):
            n = ap2d.shape[1]
            return bass.AP(ap2d.tensor, ap2d.offset + (n - 1), [list(ap2d.ap[0]), [-1, n]])

        def conv5(n, xlp, o0, u, nt):
            w5 = pcol(P_W5 + n * 5, 5)
            xf = xlp.f32()
            T.op('dve', lambda: V.tensor_scalar(out=u.f32()[:, 0:nt], in0=xf[:, o0:o0 + nt], scalar1=w5[:, 0:1], scalar2=pcol(P_CB + n),
                                                op0=ALU.mult, op1=ALU.add),
                 reads=xlp.keys() + B_prm.keys(), writes=u.keys())
            for o in range(1, 5):
                T.op('dve', lambda o=o: V.scalar_tensor_tensor(out=u.f32()[:, 0:nt], in0=xf[:, o0 + o:o0 + o + nt], scalar=w5[:, o:o + 1],
                                                               in1=u.f32()[:, 0:nt], op0=ALU.mult, op1=ALU.add),
                     reads=xlp.keys() + u.keys() + B_prm.keys(), writes=u.keys())

        hT1 = Buf(ar, BASE, T1 * KC * 2)
        RP1 = Ring(ar, BASE + hT1.n, BASE + hT1.n + 49152)
        groups = []
        xrow = lambda r0: xs_d[r0:r0 + 128, :]
        groups.append([(xrow(NA - NH), False, 0, 0), (xrow(NA), False, 1, 128)])
        for i in range(3):
            groups.append([(xrow(NA + 128 * (2 * i + 1)), False, 2 + 2 * i, 128 * (2 * i + 2)),
                           (xrow(NA + 128 * (2 * i + 2)), False, 3 + 2 * i, 128 * (2 * i + 3))])
        groups.append([(xrow(NA + 128 * 7), False, 8, 128 * 8)])
        groups.append([(cx_d[0:128, :], True, 9, 1152), (cx_d[128:256, :], True, 10, 1280)])
        w_issue()
        w_issue()
        prep(groups, hT1, RP1)
        if dbg:
            dump("hT1", hT1.bf(), [128, KC * T1], BF16, reads=hT1.keys())
        if stop_after == 1:
            finish()
            return nc, dbg_outs
        T.barrier()
        R1 = Ring(ar, BASE + hT1.n, KV0)
        def pipelined(units):
            prev = None
            for (pf, qf) in units:
                pf()
                if prev is not None:
                    prev()
                prev = qf
            if prev is not None:
                prev()

        kunits = []
        for g in range(4):
            wtb = {}
            for ui, (t0, n, k0, roped) in enumerate(((128, 512, NA, True), (640, 512, NA + 512, True), (1152, 256, NA + NB, False))):
                stt = {}

                def pf(g=g, ui=ui, t0=t0, n=n, wtb=wtb, stt=stt):
                    if ui == 0:
                        wtb['wt'] = w_next(C_K + g * 128)
                    stt['bank'] = accb.next()
                    proj(wtb['wt'], hT1, T1, t0, n, stt['bank'])

                def qf(g=g, n=n, k0=k0, roped=roped, stt=stt):
                    normrope(stt['bank'], n, pcol(P_KW), B_prm.keys(), (k0 if roped else None), kTv[:, g, k0:k0 + n], kkeys(g, k0, n), R1)
                kunits.append((pf, qf))
        pipelined(kunits)
        for j in range(4):
            wt = w_next(C_V + j * 128)
            for tg in range(3):
                tl = list(range(tg * 4, min(tg * 4 + 4, 10)))
                bank = accb.next()
                for i, t in enumerate(tl):
                    proj_tm(wt, hT1, T1, 128 + t * 128, bank, i * 128)
                nt = len(tl)
                kt0 = 8 + tl[0]
                dst = vv[:, kt0:kt0 + nt, j * 128:(j + 1) * 128]
                dk = []
                for t in tl:
                    dk += vkeys(8 + t, j * 128, 128)
                T.op('act', lambda dst=dst, bank=bank, nt=nt: S.activation(out=dst, in_=PS(bank)[:, 0:nt * 128].rearrange("p (t c) -> p t c", t=nt), func=AF.Copy),
                     reads=PSK(bank), writes=dk)
        if dbg:
            dump("kT", B_kT.bf(), [128, 4 * NKEY], BF16, reads=B_kT.keys())
            dump("v", B_v.bf(), [128, 18 * 512], BF16, reads=B_v.keys())
        R1B = BASE + hT1.n
        XLP1 = 5376
        xlp1 = [Buf(ar, R1B + i * XLP1, XLP1) for i in range(2)]
        ub1 = [Buf(ar, R1B + 2 * XLP1 + i * 2560, 2560) for i in range(2)]
        o_ = R1B + 2 * XLP1 + 5120
        u1 = Buf(ar, o_, 5120)
        a1 = Buf(ar, o_ + 5120, 6144)
        x1 = Buf(ar, o_ + 5120 + 6144, 6144)
        m1 = Buf(ar, o_ + 5120 + 2 * 6144, 6144)
        assert o_ + 5120 + 3 * 6144 <= KV0

        def p1_A(n):
            wt = w_next(C_XL + n * 128)
            wgb = wg_load(n)
            xlp = xlp1[n % 2]
            xf = xlp.f32()
            T.op('dve', lambda: V.memset(xf[:, 1026:1030], 0.0), writes=xlp.keys())
            T.op('dve', lambda: V.memset(xf[:, 1286:1288], 0.0), writes=xlp.keys())
            for (t0, nn) in ((0, 512), (512, 512), (1024, 384)):
                bank = accb4.next()
                proj(wt, hT1, T1, t0, nn, bank)
                if t0 == 0:
                    T.op('act', lambda bank=bank: S.activation(out=xf[:, 0:386], in_=PS(bank)[:, 126:512], func=AF.Copy),
                         reads=PSK(bank), writes=xlp.keys())
                elif t0 == 512:
                    T.op('act', lambda bank=bank: S.activation(out=xf[:, 386:898], in_=PS(bank)[:, 0:512], func=AF.Copy),
                         reads=PSK(bank), writes=xlp.keys())
                else:
                    T.op('act', lambda bank=bank: S.activation(out=xf[:, 898:1026], in_=PS(bank)[:, 0:128], func=AF.Copy),
                         reads=PSK(bank), writes=xlp.keys())
                    T.op('act', lambda bank=bank: S.activation(out=xf[:, 1030:1286], in_=PS(bank)[:, 128:384], func=AF.Copy),
                         reads=PSK(bank), writes=xlp.keys())
            T.op('dve', lambda n=n: V.tensor_copy(out=b_xlB2.f32()[:, 2 * n:2 * n + 2], in_=xf[:, 2:4]),
                 reads=xlp.keys(), writes=b_xlB2.keys())
            conv5(n, xlp, 0, u1.sub(0, 4096), 1024)
            conv5(n, xlp, 1028, u1.sub(4096, 1024), 256)
            ub = ub1[n % 2]
            T.op('act', lambda: S.activation(out=ub.bf(), in_=u1.f32(), func=AF.Copy), reads=u1.keys(), writes=ub.keys())
            return wgb, ub

        def p1_B(n, wgb, ub):
            ubB = ub.bf()[:, 0:1024]
            ubC = ub.bf()[:, 1024:1280]
            af, xf_ = a1.f32(), x1.f32()
            gate_mm_tanh(wgb, n, 0, ubC, ub.keys(), af[:, 0:256], a1.sub(0, 1024).keys(), xf_[:, 0:256], x1.sub(0, 1024).keys(), 256)
            gate_mm_tanh(wgb, n, 1, ubC, ub.keys(), af[:, 256:512], a1.sub(1024, 1024).keys(), xf_[:, 256:512], x1.sub(1024, 1024).keys(), 256)
            for c in range(2):
                lo = 512 + c * 512
                gate_mm_tanh(wgb, n, 1, ubB[:, c * 512:(c + 1) * 512], ub.keys(), af[:, lo:lo + 512], a1.sub(lo * 4, 2048).keys(),
                             xf_[:, lo:lo + 512], x1.sub(lo * 4, 2048).keys(), 512)
            gate_exp(n, 0, af[:, 0:256], a1.sub(0, 1024).keys())
            gate_exp(n, 1, af[:, 256:1536], a1.sub(1024, 5120).keys())
            gate_mult(a1, x1, m1)
            for (lo, nn, uu) in ((0, 256, ubC), (256, 256, ubC), (512, 1024, ubB)):
                T.op('dve', lambda lo=lo, nn=nn, uu=uu: V.scalar_tensor_tensor(out=xf_[:, lo:lo + nn], in0=xf_[:, lo:lo + nn], scalar=0.5, in1=uu,
                                                                          op0=ALU.mult, op1=ALU.mult),
                     reads=x1.sub(lo * 4, nn * 4).keys() + ub.keys(), writes=x1.sub(lo * 4, nn * 4).keys())
            mf = m1.f32()
            T.op('dve', lambda: V.tensor_tensor_scan(out=mf[:, 0:256], data0=af[:, 0:256], data1=xf_[:, 0:256], initial=0.0, op0=ALU.mult, op1=ALU.add),
                 reads=a1.sub(0, 1024).keys() + x1.sub(0, 1024).keys(), writes=m1.sub(0, 1024).keys())
            T.op('dve', lambda n=n: V.tensor_copy(out=b_stF.f32()[:, n:n + 1], in_=mf[:, 255:256]), reads=m1.sub(0, 1024).keys(), writes=b_stF.keys())
            T.op('dve', lambda: V.tensor_tensor_scan(out=rev(mf[:, 256:512]), data0=rev(af[:, 256:512]), data1=rev(xf_[:, 256:512]), initial=0.0,
                                                     op0=ALU.mult, op1=ALU.add),
                 reads=a1.sub(1024, 1024).keys() + x1.sub(1024, 1024).keys(), writes=m1.sub(1024, 1024).keys())
            T.op('dve', lambda: V.tensor_tensor_scan(out=rev(mf[:, 512:1536]), data0=rev(af[:, 512:1536]), data1=rev(xf_[:, 512:1536]),
                                                     initial=mf[:, 256:257], op0=ALU.mult, op1=ALU.add),
                 reads=a1.sub(2048, 4096).keys() + x1.sub(2048, 4096).keys() + m1.sub(1024, 1024).keys(), writes=m1.sub(2048, 4096).keys())
            T.op('dve', lambda n=n: V.tensor_copy(out=b_stB.f32()[:, n:n + 1], in_=mf[:, 512:513]), reads=m1.sub(2048, 4096).keys(), writes=b_stB.keys())

        GBANK = 5

        def gate_tile(j):
            wt = w_next(2 * D + j * 128, 'ada')
            wv = wt.bf().rearrange("p (kc n) -> p kc n", kc=KC)
            q = j % 4
            for kc in range(KC):
                T.op('pe', lambda kc=kc: PE.matmul(PS(GBANK)[0:2, q * 128:(q + 1) * 128], lhsT=csb[:, 2 * kc:2 * kc + 2], rhs=wv[:, kc, :],
                                                   start=(kc == 0), stop=(kc == KC - 1)),
                     reads=b_csb.keys() + wt.sub(kc * 256, 256).keys(), writes=PSK(GBANK), inc=(kc == KC - 1))
            if q == 3:
                grp = j // 4
                T.dma('sp', b_gb.f32()[0:2, :], bada_d[:, 2 * D + grp * 512:2 * D + (grp + 1) * 512], writes=b_gb.keys())
                T.op('dve', lambda: V.tensor_tensor(out=b_grow.f32()[0:2, :], in0=PS(GBANK)[0:2, :], in1=b_gb.f32()[0:2, :], op=ALU.add),
                     reads=PSK(GBANK) + b_gb.keys(), writes=b_grow.keys())
                T.dma('sp', gate_d[0:1, grp * 512:(grp + 1) * 512], b_grow.f32()[0:1, :], reads=b_grow.keys(), writes=["gate_row"])

        pend = p1_A(0)
        for n in range(16):
            nxt = p1_A(n + 1) if n + 1 < 16 else None
            p1_B(n, *pend)
            gate_tile(2 * n)
            gate_tile(2 * n + 1)
            pend = nxt
        if dbg:
            dump("stF", b_stF.f32(), [128, 16], reads=b_stF.keys())
            dump("stB", b_stB.f32(), [128, 16], reads=b_stB.keys())
        if stop_after == 2:
            finish()
            return nc, dbg_outs
        T.barrier()

        hTA = Buf(ar, BASE, NA * KC * 2)
        RPA = Ring(ar, BASE + hTA.n, BASE + hTA.n + 49152)
        groups = []
        for i in range(4):
            groups.append([(xrow(256 * i), False, 11 + 2 * i, 256 * i), (xrow(256 * i + 128), False, 12 + 2 * i, 256 * i + 128)])
        prep(groups, hTA, RPA)
        T.barrier()
        B_Ua = Buf(ar, BASE + hTA.n, 32768)
        Uav = B_Ua.bf().rearrange("p (h t) -> p h t", h=16)
        R2BASE = BASE + hTA.n + 32768
        b_ssa = Buf(ar, R2BASE, 4096)
        QTB = [Buf(ar, R2BASE + 4096 + i * 1024, 1024) for i in range(4)]
        GWB = [Buf(ar, R2BASE + 8192 + i * 1024, 1024) for i in range(4)]
        R2 = Ring(ar, R2BASE + 12288 + 6144, KV0)
        kunits = []
        for g in range(4):
            wtb = {}
            for qb in range(2):
                stt = {}

                def pf(g=g, qb=qb, wtb=wtb, stt=stt):
                    if qb == 0:
                        wtb['wt'] = w_next(C_K + g * 128)
                    stt['bank'] = accb.next()
                    proj(wtb['wt'], hTA, NA, qb * 512, 512, stt['bank'])

                def qf(g=g, qb=qb, stt=stt):
                    normrope(stt['bank'], 512, pcol(P_KW), B_prm.keys(), qb * 512, kTv[:, g, qb * 512:(qb + 1) * 512], kkeys(g, qb * 512, 512), R2)
                kunits.append((pf, qf))
        pipelined(kunits)
        for j in range(4):
            wt = w_next(C_V + j * 128)
            for tg in range(2):
                bank = accb.next()
                for i in range(4):
                    proj_tm(wt, hTA, NA, (tg * 4 + i) * 128, bank, i * 128)
                dst = vv[:, tg * 4:tg * 4 + 4, j * 128:(j + 1) * 128]
                dk = []
                for t in range(tg * 4, tg * 4 + 4):
                    dk += vkeys(t, j * 128, 128)
                T.op('act', lambda dst=dst, bank=bank: S.activation(out=dst, in_=PS(bank)[:, 0:512].rearrange("p (t c) -> p t c", t=4), func=AF.Copy),
                     reads=PSK(bank), writes=dk)
        if dbg:
            dump("kT2", B_kT.bf(), [128, 4 * NKEY], BF16, reads=B_kT.keys())
            dump("v2", B_v.bf(), [128, 18 * 512], BF16, reads=B_v.keys())
        sb_rot = Rot([2, 3, 6])
        pv_rot = Rot([4])
        auxb.items = [7]
        NRL = [Buf(ar, R2BASE + 12288 + i * 3072, 3072) for i in range(2)]

        def nr_split(bank, n, w_ap, w_keys, cs0, dst_b, lnb, qn):
            sqb = dst_b

            def p1():
                T.op('act', lambda: S.activation(out=sqb.bf(), in_=PS(bank)[:, 0:n], func=AF.Square), reads=PSK(bank), writes=sqb.keys())

            def p2():
                ab = auxb.next()
                T.op('pe', lambda: PE.matmul(PS(ab)[:, 0:n], lhsT=ones, rhs=sqb.bf(), start=True, stop=True),
                     reads=sqb.keys() + B_cst.keys(), writes=PSK(ab))
                T.op('act', lambda: S.activation(out=lnb.f32(), in_=PS(ab)[:, 0:n], func=AF.Ln, scale=1.0 / 128, bias=b_eps.f32()),
                     reads=PSK(ab) + b_eps.keys(), writes=lnb.keys())
                T.op('act', lambda: S.activation(out=lnb.f32(), in_=lnb.f32(), func=AF.Exp, scale=-0.5), reads=lnb.keys(), writes=lnb.keys())
                T.op('dve', lambda: V.scalar_tensor_tensor(out=qn.bf(), in0=PS(bank)[:, 0:n], scalar=w_ap, in1=lnb.f32(), op0=ALU.mult, op1=ALU.mult),
                     reads=PSK(bank) + lnb.keys() + w_keys, writes=qn.keys())

            def p3():
                ab2 = auxb.next()
                T.op('pe', lambda: PE.matmul(PS(ab2)[:, 0:n], lhsT=rmat, rhs=qn.bf(), start=True, stop=True),
                     reads=qn.keys() + B_cst.keys(), writes=PSK(ab2))
                t1 = R2.alloc(n * 4)
                rope_mul(t1, lambda lo: qn.bf()[lo:lo + 64, 0:n], qn.keys(), P_COS, cs0, n)
                t2 = R2.alloc(n * 4)
                rope_mul(t2, lambda lo: PS(ab2)[lo:lo + 64, 0:n], PSK(ab2), P_SIN, cs0, n)
                T.op('dve', lambda: V.tensor_tensor(out=dst_b.bf(), in0=t1.f32(), in1=t2.f32(), op=ALU.add),
                     reads=t1.keys() + t2.keys(), writes=dst_b.keys())
            return p1, p2, p3

        def att_A_stages(h):
            st = {}
            qTs = [QTB[(h % 2) * 2 + qb] for qb in range(2)]
            gws = [GWB[(h % 2) * 2 + qb] for qb in range(2)]

            def a1():
                wq = w_next(C_Q + h * 128)
                st['pieces'] = []
                for qb in range(2):
                    bank = accb.next()
                    proj(wq, hTA, NA, qb * 512, 512, bank)
                    pcs = nr_split(bank, 512, b_qws.f32(), b_qws.keys(), qb * 512, qTs[qb], NRL[qb].sub(0, 2048), NRL[qb].sub(2048, 1024))
                    st['pieces'].append(pcs)
                for pcs in st['pieces']:
                    pcs[0]()

            def a2():
                for pcs in st['pieces']:
                    pcs[1]()

            def a3():
                for pcs in st['pieces']:
                    pcs[2]()

            def a4():
                wga = w_next(C_GA + h * 128)
                for qb in range(2):
                    bank = accb.next()
                    proj(wga, hTA, NA, qb * 512, 512, bank)
                    th = R2.alloc(2048)
                    T.op('act', lambda th=th, bank=bank: S.activation(out=th.f32(), in_=PS(bank)[:, :], func=AF.Tanh, scale=0.5), reads=PSK(bank), writes=th.keys())
                    gw = gws[qb]
                    T.op('dve', lambda th=th, gw=gw, bank=bank: V.scalar_tensor_tensor(out=gw.bf(), in0=th.f32(), scalar=1.0, in1=PS(bank)[:, :], op0=ALU.add, op1=ALU.mult),
                         reads=th.keys() + PSK(bank), writes=gw.keys())
            return (qTs, gws), {1: a1, 5: a2, 10: a3, 14: a4}

        def att_B(h, qTs, gws, hooks):
            g = h // 4
            for qb in range(2):
                qT = qTs[qb]
                pvb = pv_rot.next()
                smb = 5
                sbanks = [None] * 18

                def s_mm(kt):
                    sb = sb_rot.next()
                    sbanks[kt] = sb
                    T.op('pe', lambda: PE.matmul(PS(sb)[:, :], lhsT=kTv[:, g, kt * 128:(kt + 1) * 128], rhs=qT.bf(), start=True, stop=True),
                         reads=kkeys(g, kt * 128, 128) + qT.keys(), writes=PSK(sb))
                s_mm(0)
                s_mm(1)
                for kt in range(18):
                    if kt + 2 < 18:
                        s_mm(kt + 2)
                    if qb == 0 and kt in hooks:
                        hooks[kt]()
                    sb = sbanks[kt]
                    pT = R2.alloc(1024)
                    T.op('act', lambda sb=sb, pT=pT: S.activation(out=pT.bf(), in_=PS(sb)[:, :], func=AF.Exp), reads=PSK(sb), writes=pT.keys())
                    T.op('pe', lambda kt=kt, pT=pT: PE.matmul(PS(pvb)[:, :], lhsT=vv[:, kt, g * 128:(g + 1) * 128], rhs=pT.bf(),
                                                              start=(kt == 0), stop=(kt == 17)),
                         reads=vkeys(kt, g * 128, 128) + pT.keys(), writes=PSK(pvb), inc=(kt == 17))
                    T.op('pe', lambda kt=kt, pT=pT: PE.matmul(PS(smb)[:, :], lhsT=ones, rhs=pT.bf(), start=(kt == 0), stop=(kt == 17)),
                         reads=pT.keys() + B_cst.keys(), writes=PSK(smb), inc=(kt == 17))
                rs = R2.alloc(2048)
                T.op('dve', lambda rs=rs: V.reciprocal(out=rs.f32(), in_=PS(smb)[:, :]), reads=PSK(smb), writes=rs.keys())
                att = R2.alloc(2048)
                T.op('dve', lambda rs=rs, att=att: V.tensor_tensor(out=att.f32(), in0=PS(pvb)[:, :], in1=rs.f32(), op=ALU.mult),
                     reads=PSK(pvb) + rs.keys(), writes=att.keys())
                ssl_ = b_ssa.sub(qb * 2048, 2048)
                if h == 0:
                    T.op('dve', lambda att=att, ssl_=ssl_: V.tensor_tensor(out=ssl_.f32(), in0=att.f32(), in1=att.f32(), op=ALU.mult),
                         reads=att.keys(), writes=ssl_.keys())
                else:
                    sqa = R2.alloc(2048)
                    T.op('dve', lambda att=att, sqa=sqa: V.tensor_tensor(out=sqa.f32(), in0=att.f32(), in1=att.f32(), op=ALU.mult),
                         reads=att.keys(), writes=sqa.keys())
                    T.op('dve', lambda sqa=sqa, ssl_=ssl_: V.tensor_tensor(out=ssl_.f32(), in0=ssl_.f32(), in1=sqa.f32(), op=ALU.add),
                         reads=sqa.keys() + ssl_.keys(), writes=ssl_.keys())
                gw = gws[qb]
                T.op('dve', lambda att=att, gw=gw, qb=qb, h=h: V.scalar_tensor_tensor(out=Uav[:, h, qb * 512:(qb + 1) * 512], in0=att.f32(),
                                                                                    scalar=b_wa05.f32()[:, h:h + 1], in1=gw.bf(),
                                                                                    op0=ALU.mult, op1=ALU.mult),
                     reads=att.keys() + gw.keys() + b_wa05.keys(), writes=B_Ua.sub((h * NA + qb * 512) * 2, 1024).keys())

        cur, hk = att_A_stages(0)
        for k_ in (1, 5, 10, 14):
            hk[k_]()
        for h in range(16):
            if h + 1 < 16:
                nxt, hk = att_A_stages(h + 1)
            else:
                nxt, hk = None, {}
            att_B(h, cur[0], cur[1], hk)
            cur = nxt
        auxb.items = [6, 7]
        rsa = R2.alloc(4096)
        for qb in range(2):
            sqb = R2.alloc(1024)
            T.op('dve', lambda qb=qb, sqb=sqb: V.tensor_copy(out=sqb.bf(), in_=b_ssa.f32()[:, qb * 512:(qb + 1) * 512]),
                 reads=b_ssa.keys(), writes=sqb.keys())
            ab = auxb.next()
            T.op('pe', lambda sqb=sqb, ab=ab: PE.matmul(PS(ab)[:, :], lhsT=ones, rhs=sqb.bf(), start=True, stop=True),
                 reads=sqb.keys() + B_cst.keys(), writes=PSK(ab))
            T.op('act', lambda qb=qb, ab=ab: S.activation(out=rsa.f32()[:, qb * 512:(qb + 1) * 512], in_=PS(ab)[:, :], func=AF.Ln, scale=1.0 / 2048, bias=b_eps.f32()),
                 reads=PSK(ab) + b_eps.keys(), writes=rsa.keys())
        T.op('act', lambda: S.activation(out=rsa.f32(), in_=rsa.f32(), func=AF.Exp, scale=-0.5), reads=rsa.keys(), writes=rsa.keys())
        for h in range(16):
            T.op('dve', lambda h=h: V.tensor_tensor(out=Uav[:, h, :], in0=Uav[:, h, :], in1=rsa.f32(), op=ALU.mult),
                 reads=rsa.keys() + B_Ua.sub(h * 2048, 2048).keys(), writes=B_Ua.sub(h * 2048, 2048).keys())
        if dbg:
            dump("Ua", B_Ua.bf(), [128, 16 * NA], BF16, reads=B_Ua.keys())
        if stop_after == 3:
            finish()
            return nc, dbg_outs
        T.barrier()

        T.dma('sp', ua_d[:, :], B_Ua.bf(), reads=B_Ua.keys(), writes=["ua_scr"])
        T.barrier()
        B_Ul = Buf(ar, KV0, 32768)
        Ulv = B_Ul.bf().rearrange("p (h t) -> p h t", h=16)
        R2A = BASE + hTA.n
        b_ssl = Buf(ar, R2A, 4096)
        o_ = R2A + 4096
        xlpA = [Buf(ar, o_ + i * 4352, 4352) for i in range(2)]
        o_ += 8704
        uA = [Buf(ar, o_ + i * 4096, 4096) for i in range(2)]
        o_ += 8192
        ubA = [Buf(ar, o_ + i * 2048, 2048) for i in range(2)]
        o_ += 4096
        aA = Buf(ar, o_, 8192)
        xA = Buf(ar, o_ + 8192, 8192)
        mA = Buf(ar, o_ + 16384, 8192)
        o_ += 24576
        RS = Ring(ar, o_, o_ + 8192)
        GWL = [Buf(ar, o_ + 8192 + i * 1024, 1024) for i in range(4)]
        assert o_ + 8192 + 4096 <= KV0

        def p2_A(n):
            wxl = w_next(C_XL + n * 128)
            wgb = wg_load(n)
            xlp = xlpA[n % 2]
            xf = xlp.f32()
            T.op('dve', lambda: V.memset(xf[:, 0:2], 0.0), writes=xlp.keys())
            T.op('dve', lambda n=n: V.tensor_copy(out=xf[:, 1026:1028], in_=b_xlB2.f32()[:, 2 * n:2 * n + 2]), reads=b_xlB2.keys(), writes=xlp.keys())
            for qb in range(2):
                bank = accb4.next()
                proj(wxl, hTA, NA, qb * 512, 512, bank)
                T.op('act', lambda qb=qb, bank=bank: S.activation(out=xf[:, 2 + qb * 512:2 + (qb + 1) * 512], in_=PS(bank)[:, :], func=AF.Copy),
                     reads=PSK(bank), writes=xlp.keys())
            u = uA[n % 2]
            conv5(n, xlp, 0, u, 1024)
            ub = ubA[n % 2]
            T.op('act', lambda: S.activation(out=ub.bf(), in_=u.f32(), func=AF.Copy), reads=u.keys(), writes=ub.keys())
            wgl = w_next(C_GL + n * 128)
            gws = []
            for qb in range(2):
                bank = accb4.next()
                proj(wgl, hTA, NA, qb * 512, 512, bank)
                th = RS.alloc(2048)
                T.op('act', lambda th=th, bank=bank: S.activation(out=th.f32(), in_=PS(bank)[:, :], func=AF.Tanh, scale=0.5), reads=PSK(bank), writes=th.keys())
                gw = GWL[(n % 2) * 2 + qb]
                T.op('dve', lambda th=th, gw=gw, bank=bank: V.scalar_tensor_tensor(out=gw.bf(), in0=th.f32(), scalar=1.0, in1=PS(bank)[:, :], op0=ALU.add, op1=ALU.mult),
                     reads=th.keys() + PSK(bank), writes=gw.keys())
                gws.append(gw)
            return wgb, u, ub, gws

        def p2_B(n, wgb, u, ub, gws):
            af, xf_, mf = aA.f32(), xA.f32(), mA.f32()
            for d in range(2):
                for c in range(2):
                    lo = d * 1024 + c * 512
                    gate_mm_tanh(wgb, n, d, ub.bf()[:, c * 512:(c + 1) * 512], ub.keys(), af[:, lo:lo + 512], aA.sub(lo * 4, 2048).keys(),
                                 xf_[:, lo:lo + 512], xA.sub(lo * 4, 2048).keys(), 512)
            for d in range(2):
                gate_exp(n, d, af[:, d * 1024:(d + 1) * 1024], aA.sub(d * 4096, 4096).keys())
            gate_mult(aA, xA, mA)
            for d in range(2):
                T.op('dve', lambda d=d: V.scalar_tensor_tensor(out=xf_[:, d * 1024:(d + 1) * 1024], in0=xf_[:, d * 1024:(d + 1) * 1024], scalar=0.5,
                                                            in1=u.f32(), op0=ALU.mult, op1=ALU.mult),
                     reads=xA.sub(d * 4096, 4096).keys() + u.keys(), writes=xA.sub(d * 4096, 4096).keys())
            T.op('dve', lambda: V.tensor_tensor_scan(out=mf[:, 0:1024], data0=af[:, 0:1024], data1=xf_[:, 0:1024], initial=b_stF.f32()[:, n:n + 1],
                                                     op0=ALU.mult, op1=ALU.add),
                 reads=aA.sub(0, 4096).keys() + xA.sub(0, 4096).keys() + b_stF.keys(), writes=mA.sub(0, 4096).keys())
            T.op('dve', lambda: V.tensor_tensor_scan(out=rev(mf[:, 1024:2048]), data0=rev(af[:, 1024:2048]), data1=rev(xf_[:, 1024:2048]),
                                                     initial=b_stB.f32()[:, n:n + 1], op0=ALU.mult, op1=ALU.add),
                 reads=aA.sub(4096, 4096).keys() + xA.sub(4096, 4096).keys() + b_stB.keys(), writes=mA.sub(4096, 4096).keys())
            lru = mA.sub(0, 4096)
            T.op('dve', lambda: V.tensor_tensor(out=mf[:, 0:1024], in0=mf[:, 0:1024], in1=mf[:, 1024:2048], op=ALU.add),
                 reads=mA.keys(), writes=lru.keys())
            for qb in range(2):
                lr = lru.sub(qb * 2048, 2048)
                ssl_ = b_ssl.sub(qb * 2048, 2048)
                if n == 0:
                    T.op('dve', lambda: V.tensor_tensor(out=ssl_.f32(), in0=lr.f32(), in1=lr.f32(), op=ALU.mult), reads=lr.keys(), writes=ssl_.keys())
                else:
                    sql = RS.alloc(2048)
                    T.op('dve', lambda: V.tensor_tensor(out=sql.f32(), in0=lr.f32(), in1=lr.f32(), op=ALU.mult), reads=lr.keys(), writes=sql.keys())
                    T.op('dve', lambda: V.tensor_tensor(out=ssl_.f32(), in0=ssl_.f32(), in1=sql.f32(), op=ALU.add),
                         reads=sql.keys() + ssl_.keys(), writes=ssl_.keys())
                gw = gws[qb]
                T.op('dve', lambda: V.scalar_tensor_tensor(out=Ulv[:, n, qb * 512:(qb + 1) * 512], in0=lr.f32(), scalar=b_wl05.f32()[:, n:n + 1],
                                                           in1=gw.bf(), op0=ALU.mult, op1=ALU.mult),
                     reads=lr.keys() + gw.keys() + b_wl05.keys(), writes=B_Ul.sub((n * NA + qb * 512) * 2, 1024).keys())

        pend = p2_A(0)
        for n in range(16):
            nxt = p2_A(n + 1) if n + 1 < 16 else None
            p2_B(n, *pend)
            pend = nxt
        rsl = Buf(ar, o_ - 24576, 4096)
        for qb in range(2):
            sqb = RS.alloc(1024)
            T.op('dve', lambda qb=qb, sqb=sqb: V.tensor_copy(out=sqb.bf(), in_=b_ssl.f32()[:, qb * 512:(qb + 1) * 512]),
                 reads=b_ssl.keys(), writes=sqb.keys())
            ab = auxb.next()
            T.op('pe', lambda sqb=sqb, ab=ab: PE.matmul(PS(ab)[:, :], lhsT=ones, rhs=sqb.bf(), start=True, stop=True),
                 reads=sqb.keys() + B_cst.keys(), writes=PSK(ab))
            T.op('act', lambda qb=qb, ab=ab: S.activation(out=rsl.f32()[:, qb * 512:(qb + 1) * 512], in_=PS(ab)[:, :], func=AF.Ln, scale=1.0 / 2048, bias=b_eps.f32()),
                 reads=PSK(ab) + b_eps.keys(), writes=rsl.keys())
        T.op('act', lambda: S.activation(out=rsl.f32(), in_=rsl.f32(), func=AF.Exp, scale=-0.5), reads=rsl.keys(), writes=rsl.keys())
        for n in range(16):
            T.op('dve', lambda n=n: V.tensor_tensor(out=Ulv[:, n, :], in0=Ulv[:, n, :], in1=rsl.f32(), op=ALU.mult),
                 reads=rsl.keys() + B_Ul.sub(n * 2048, 2048).keys(), writes=B_Ul.sub(n * 2048, 2048).keys())
        if dbg:
            dump("Ul", B_Ul.bf(), [128, 16 * NA], BF16, reads=B_Ul.keys())
        if stop_after == 4:
            finish()
            return nc, dbg_outs
        T.barrier()

        WO = Ring(ar, BASE, BASE + 65536)
        B_gate = Buf(ar, BASE + 65536 + 32768, 16384)
        R3 = Ring(ar, B_gate.off + B_gate.n, KV0)
        T.dma('sp', B_Ua.bf(), ua_d[:, :], reads=["ua_scr"], writes=B_Ua.keys())
        T.dma('sp', B_gate.f32(), gate_d[0:1, :].partition_broadcast(128), reads=["gate_row"], writes=B_gate.keys())
        out_toks = []

        def wo_load(cg):
            wt = WO.alloc(32768)
            wv = wt.bf().rearrange("p (kc n) -> p kc n", kc=KC)
            for q4 in range(4):
                src = wout_d[q4 * 1024:(q4 + 1) * 1024, cg * 512:(cg + 1) * 512].rearrange("(kc p) n -> p kc n", p=128)
                T.dma('pool', wv[:, q4 * 8:(q4 + 1) * 8, :], src, writes=wt.sub(q4 * 8192, 8192).keys())
            return wt
        wos = {0: wo_load(0)}
        acc3 = Rot([0, 1, 2, 3])
        for cg in range(8):
            if cg + 1 < 8:
                wos[cg + 1] = wo_load(cg + 1)
            wt = wos.pop(cg)
            wv = wt.bf().rearrange("p (kc n) -> p kc n", kc=KC)
            for tt in range(8):
                xr = R3.alloc(2048)
                T.dma('sp', xr.f32(), xs_d[tt * 128:(tt + 1) * 128, cg * 512:(cg + 1) * 512], writes=xr.keys())
                bank = acc3.next()
                for kc in range(KC):
                    if kc < 16:
                        lhs = Uav[:, kc, tt * 128:(tt + 1) * 128]
                        lk = B_Ua.sub((kc * NA + tt * 128) * 2, 256).keys()
                    else:
                        lhs = Ulv[:, kc - 16, tt * 128:(tt + 1) * 128]
                        lk = B_Ul.sub(((kc - 16) * NA + tt * 128) * 2, 256).keys()
                    T.op('pe', lambda kc=kc, lhs=lhs: PE.matmul(PS(bank)[:, :], lhsT=lhs, rhs=wv[:, kc, :], start=(kc == 0), stop=(kc == KC - 1)),
                         reads=lk + wt.sub(kc * 1024, 1024).keys(), writes=PSK(bank), inc=(kc == KC - 1))
                t_ = R3.alloc(2048)
                T.op('dve', lambda: V.tensor_tensor(out=t_.f32(), in0=PS(bank)[:, :], in1=B_gate.f32()[:, cg * 512:(cg + 1) * 512], op=ALU.mult),
                     reads=PSK(bank) + B_gate.sub(cg * 2048, 2048).keys(), writes=t_.keys())
                T.op('pool', lambda: G.tensor_tensor(out=t_.f32(), in0=t_.f32(), in1=xr.f32(), op=ALU.add),
                     reads=t_.keys() + xr.keys(), writes=t_.keys())
                T.dma('sp', out_d[tt * 128:(tt + 1) * 128, cg * 512:(cg + 1) * 512], t_.f32(), reads=t_.keys(), writes=["out"])
        finish()
    return nc, dbg_outs


def _rope_tables(h):
    pos = np.arange(2048)
    if h == 1:
        pos = 2047 - pos
    row = (pos // 64).astype(np.float32)
    col = (pos % 64).astype(np.float32)
    n_freq = 32
    freqs = (np.float32(10000.0) ** (-np.arange(n_freq, dtype=np.float32) / np.float32(n_freq))).astype(np.float32)
    cos = np.zeros((128, 2048), np.float32)
    sin = np.zeros((128, 2048), np.float32)
    for d in range(128):
        axis = d // 64
        j = d % 32
        ang = (row if axis == 0 else col) * freqs[j]
        cos[d] = np.cos(ang)
        sin[d] = np.sin(ang)
    return cos, sin


def _rope_small(h):
    n_freq = 32
    freqs = (np.float32(10000.0) ** (-np.arange(n_freq, dtype=np.float32) / np.float32(n_freq))).astype(np.float32)
    ctab = np.zeros((128, 64), np.float32)
    stab = np.zeros((128, 64), np.float32)
    for d in range(128):
        j = d % 32
        if d < 64:
            idx = np.arange(32, dtype=np.float32)
            val = idx if h == 0 else (31 - idx)
            ang = (val * freqs[j]).astype(np.float32)
            ctab[d, 0:32] = np.cos(ang)
            stab[d, 0:32] = np.sin(ang)
        else:
            idx = np.arange(64, dtype=np.float32)
            val = idx if h == 0 else (63 - idx)
            ang = (val * freqs[j]).astype(np.float32)
            ctab[d] = np.cos(ang)
            stab[d] = np.sin(ang)
    return ctab, stab


def _consts():
    ident = np.eye(128, dtype=np.float32)
    ones = np.ones((128, 128), np.float32)
    rm = np.zeros((128, 128), np.float32)
    for m in range(128):
        half = (m % 64) // 32
        if half == 0:
            rm[m + 32, m] = -1.0
        else:
            rm[m - 32, m] = 1.0
    return np.concatenate([ident, ones, rm], axis=1).astype(ml_dtypes.bfloat16)


def make_in_maps(x, c, ctx, c_ctx, w_ada, b_ada, norm_w, w_in, q_norm_w, k_norm_w, conv_w, conv_b,
                 lru_wa, lru_ba, lru_wx, lru_bx, lru_lambda, out_norm_att, out_norm_lru, w_out):
    f = np.float32
    w_ada0 = np.ascontiguousarray(np.asarray(w_ada, f)[0])
    w_in0 = np.ascontiguousarray(np.asarray(w_in, f)[0])
    w_out0 = np.ascontiguousarray(np.asarray(w_out, f)[0])
    bada2 = np.ascontiguousarray(np.broadcast_to(np.asarray(b_ada, f)[0][None, :], (2, 3 * D)))
    cst = _consts()
    x = np.asarray(x, f)
    ctx = np.asarray(ctx, f)
    c = np.asarray(c, f)
    c_ctx = np.asarray(c_ctx, f)

    def pl(v, n):
        return np.asarray(v, f).reshape(n, 128).T
    in_maps = []
    for core in range(8):
        b, h = core // 2, core % 2
        xs = x[b]
        cx = ctx[b]
        if h == 1:
            xs = xs[::-1]
            cx = cx[::-1]
        prm = np.zeros((128, NPRM), f)
        prm[:, P_NW:P_NW + 32] = pl(norm_w[0], 32)
        cc = np.zeros((128, 32, 2), f)
        cc[:, :, 0] = pl(c[b], 32)
        cc[:, :, 1] = pl(c_ctx, 32)
        prm[:, P_CC:P_CC + 64] = cc.reshape(128, 64)
        prm[:, P_QW] = np.asarray(q_norm_w, f)[0]
        prm[:, P_KW] = np.asarray(k_norm_w, f)[0]
        cw = np.asarray(conv_w, f)[0]
        w5 = np.zeros((5, 2048), f)
        if h == 0:
            w5[0:4] = cw
        else:
            w5[1:5] = cw[::-1]
        prm[:, P_W5:P_W5 + 80] = w5.reshape(5, 16, 128).transpose(2, 1, 0).reshape(128, 80)
        prm[:, P_CB:P_CB + 16] = pl(np.asarray(conv_b, f)[0], 16)
        dirs = (0, 1) if h == 0 else (1, 0)
        for dd, d in enumerate(dirs):
            prm[:, P_BA + dd * 16:P_BA + dd * 16 + 16] = pl(np.asarray(lru_ba, f)[0, d], 16)
            prm[:, P_BX + dd * 16:P_BX + dd * 16 + 16] = pl(np.asarray(lru_bx, f)[0, d], 16)
            prm[:, P_LAM + dd * 16:P_LAM + dd * 16 + 16] = pl(np.asarray(lru_lambda, f)[0, d], 16)
        prm[:, P_WNA:P_WNA + 16] = pl(np.asarray(out_norm_att, f)[0], 16)
        prm[:, P_WNL:P_WNL + 16] = pl(np.asarray(out_norm_lru, f)[0], 16)
        ctab, stab = _rope_small(h)
        prm[:, P_COS:P_COS + 64] = ctab
        prm[:, P_SIN:P_SIN + 64] = stab
        prm[0, P_ID2] = 1.0
        prm[1, P_ID2 + 1] = 1.0
        wa = np.asarray(lru_wa, f)[0]
        wx = np.asarray(lru_wx, f)[0]
        wg = np.zeros((128, 16, 2, 2, 128), f)
        for dd, d in enumerate(dirs):
            wg[:, :, dd, 0, :] = wa[d].transpose(1, 0, 2)
            wg[:, :, dd, 1, :] = wx[d].transpose(1, 0, 2)
        cos, sin = _rope_tables(h)
        cs = np.concatenate([cos, sin], axis=1).astype(ml_dtypes.bfloat16)
        in_maps.append(dict(xs=np.ascontiguousarray(xs), cx=np.ascontiguousarray(cx), w_ada=w_ada0, w_in=w_in0, w_out=w_out0,
                            prm=prm, bada2=bada2, cst=cst, cs=cs, idf=np.eye(128, dtype=np.float32), wg=np.ascontiguousarray(wg.reshape(128, 16 * 512))))
    return in_maps


_NC_CACHE = {}


def kernel(x, c, ctx, c_ctx, w_ada, b_ada, norm_w, w_in, q_norm_w, k_norm_w, conv_w, conv_b,
           lru_wa, lru_ba, lru_wx, lru_bx, lru_lambda, out_norm_att, out_norm_lru, w_out):
    in_maps = make_in_maps(x, c, ctx, c_ctx, w_ada, b_ada, norm_w, w_in, q_norm_w, k_norm_w, conv_w, conv_b,
                           lru_wa, lru_ba, lru_wx, lru_bx, lru_lambda, out_norm_att, out_norm_lru, w_out)
    if 'nc' not in _NC_CACHE:
        _NC_CACHE['nc'] = build_nc()[0]
    nc = _NC_CACHE['nc']
    res = run_bass_kernel_spmd(nc, in_maps, core_ids=list(range(8)))
    out = np.zeros((4, 2048, D), np.float32)
    for core in range(8):
        b, h = core // 2, core % 2
        o = np.asarray(res.results[core]["out"], np.float32)
        if h == 0:
            out[b, 0:1024] = o
        else:
            out[b, 1024:2048] = o[::-1]
    return out
```

```python
import numpy as np
import ml_dtypes
from contextlib import ExitStack
import concourse.bass as bass
import concourse.mybir as mybir
from concourse.bass_utils import run_bass_kernel_spmd

F32 = mybir.dt.float32
BF16 = mybir.dt.bfloat16
AF = mybir.ActivationFunctionType
ALU = mybir.AluOpType

D = 4096
KC = 32
NA = 1024
NH = 128
NB = 1024
NCX = 256
T1 = NH + NB + NCX
NKEY = NA + NB + NCX
DIN = 9216
C_Q, C_K, C_V, C_GA, C_XL, C_GL = 0, 2048, 2560, 3072, 5120, 7168
EPS = 1e-6
ARENA_F32 = 53000
GRAN = 256

P_NW = 0
P_CC = 32
P_QW = 96
P_KW = 97
P_W5 = 98
P_CB = 178
P_BA = 194
P_BX = 226
P_LAM = 258
P_WNA = 290
P_WNL = 306
P_ID2 = 322
P_COS = 324
P_SIN = 388
NPRM = 452


class Buf:
    def __init__(self, ar, off, n):
        self.ar, self.off, self.n = ar, off, n

    def f32(self):
        return self.ar[:, self.off // 4:(self.off + self.n) // 4]

    def bf(self):
        return self.f32().bitcast(BF16)

    def sub(self, o, n):
        return Buf(self.ar, self.off + o, n)

    def keys(self):
        return list(range(self.off // GRAN, (self.off + self.n - 1) // GRAN + 1))


class Ring:
    def __init__(self, ar, start, end):
        self.ar, self.start, self.end, self.p = ar, start, end, start

    def alloc(self, n):
        n = (n + GRAN - 1) // GRAN * GRAN
        assert n <= self.end - self.start, (n, self.start, self.end)
        if self.p + n > self.end:
            self.p = self.start
        b = Buf(self.ar, self.p, n)
        self.p += n
        return b


class Rot:
    def __init__(self, items):
        self.items, self.i = list(items), 0

    def next(self):
        v = self.items[self.i % len(self.items)]
        self.i += 1
        return v


class TR:
    def __init__(self, nc, es, n_sp=20, n_pool=10):
        self.nc = nc
        self.E = {}
        self.sems = {}
        for n, attr in (('pe', 'tensor'), ('act', 'scalar'), ('dve', 'vector'), ('pool', 'gpsimd'), ('sp', 'sync')):
            sem = es.enter_context(nc.semaphore("c_" + n))
            self.E[n] = dict(h=getattr(nc, attr), sem=sem, key="c_" + n, count=0, known={}, pend=False)
            self.sems["c_" + n] = sem
        self.dsl = {}
        for q, n in (('sp', n_sp), ('pool', n_pool)):
            sl = []
            for i in range(n):
                key = "d_%s%d" % (q, i)
                sem = es.enter_context(nc.semaphore(key))
                self.sems[key] = sem
                sl.append(dict(key=key, sem=sem, val=0))
            self.dsl[q] = dict(slots=sl, nxt=0)
        self.lastw = {}
        self.readers = {}
        self.nwait = 0

    def _wait(self, E, reads, writes, excl=()):
        need = {}
        strict = E['key'] != 'c_pe'

        def add(k, v, same_ok):
            if k == E['key'] and same_ok:
                return
            if need.get(k, 0) < v:
                need[k] = v
        for r in reads:
            t = self.lastw.get(r)
            if t is not None:
                add(t[0], t[1], False)
        for w in writes:
            t = self.lastw.get(w)
            if t is not None:
                add(t[0], t[1], not strict)
            rd = self.readers.get(w)
            if rd:
                for k, v in rd.items():
                    add(k, v, not strict)
        for w in excl:
            t = self.lastw.get(w)
            if t is not None:
                add(t[0], t[1], True)
            rd = self.readers.get(w)
            if rd:
                for k, v in rd.items():
                    add(k, v, True)
        for k, v in need.items():
            if E['known'].get(k, 0) < v:
                E['h'].wait_ge(self.sems[k], v)
                E['known'][k] = v
                self.nwait += 1

    def _record(self, tok, reads, writes):
        for w in writes:
            self.lastw[w] = tok
            self.readers[w] = {}
        for r in reads:
            d = self.readers.setdefault(r, {})
            if d.get(tok[0], 0) < tok[1]:
                d[tok[0]] = tok[1]

    def op(self, eng, fn, reads=(), writes=(), inc=True):
        E = self.E[eng]
        excl = [k for k in reads if isinstance(k, tuple) and k not in writes]
        self._wait(E, reads, writes, excl)
        writes = list(writes) + excl
        ins = fn()
        if inc:
            E['count'] += 1
            ins.then_inc(E['sem'], 1)
            tok = (E['key'], E['count'])
            E['pend'] = False
        else:
            tok = (E['key'], E['count'] + 1)
            E['pend'] = True
        self._record(tok, reads, writes)
        return tok

    def dma(self, q, out, in_, reads=(), writes=()):
        E = self.E[q]
        self._wait(E, reads, writes)
        d = self.dsl[q]
        sl = d['slots'][d['nxt'] % len(d['slots'])]
        d['nxt'] += 1
        if sl['val'] > 0 and E['known'].get(sl['key'], 0) < sl['val']:
            E['h'].wait_ge(sl['sem'], sl['val'])
            E['known'][sl['key']] = sl['val']
        ins = E['h'].dma_start(out=out, in_=in_)
        sl['val'] += 16
        ins.then_inc(sl['sem'], 16)
        tok = (sl['key'], sl['val'])
        self._record(tok, reads, writes)
        return tok

    def barrier(self, engines=('pe', 'act', 'dve', 'pool', 'sp')):
        for e in self.E.values():
            assert not e['pend'], "pending un-inc'd op at barrier"
        tgt = {}
        for e in self.E.values():
            if e['count'] > 0:
                tgt[e['key']] = e['count']
        for d in self.dsl.values():
            for sl in d['slots']:
                if sl['val'] > 0:
                    tgt[sl['key']] = sl['val']
        for n in engines:
            E = self.E[n]
            for k, v in tgt.items():
                if k == E['key']:
                    continue
                if E['known'].get(k, 0) < v:
                    E['h'].wait_ge(self.sems[k], v)
                    E['known'][k] = v


def build_nc(stop_after=None, dbg=False):
    nc = bass.Bass("TRN2", target_bir_lowering=False)
    xs_d = nc.dram_tensor("xs", [NA + NB, D], F32, kind="ExternalInput").ap()
    cx_d = nc.dram_tensor("cx", [NCX, D], F32, kind="ExternalInput").ap()
    wada_d = nc.dram_tensor("w_ada", [D, 3 * D], F32, kind="ExternalInput").ap()
    win_d = nc.dram_tensor("w_in", [D, DIN], F32, kind="ExternalInput").ap()
    wout_d = nc.dram_tensor("w_out", [D, D], F32, kind="ExternalInput").ap()
    prm_d = nc.dram_tensor("prm", [128, NPRM], F32, kind="ExternalInput").ap()
    bada_d = nc.dram_tensor("bada2", [2, 3 * D], F32, kind="ExternalInput").ap()
    cst_d = nc.dram_tensor("cst", [128, 384], BF16, kind="ExternalInput").ap()
    cs_d = nc.dram_tensor("cs", [128, 4096], BF16, kind="ExternalInput").ap()
    idf_d = nc.dram_tensor("idf", [128, 128], F32, kind="ExternalInput").ap()
    wg_d = nc.dram_tensor("wg", [128, 16 * 512], F32, kind="ExternalInput").ap()
    out_d = nc.dram_tensor("out", [NA, D], F32, kind="ExternalOutput").ap()
    gate_d = nc.dram_tensor("gate_row", [1, D], F32, kind="Internal").ap()
    ua_d = nc.dram_tensor("ua_scr", [128, 16 * NA], BF16, kind="Internal").ap()
    dbg_outs = {}

    es = ExitStack()
    with es:
        ar = es.enter_context(nc.sbuf_tensor("arena", [128, ARENA_F32], F32))
        psum = es.enter_context(nc.psum_tensor("psum", [128, 8, 512], F32))
        T = TR(nc, es)
        V, S, G, PE = nc.vector, nc.scalar, nc.gpsimd, nc.tensor

        def PS(b):
            return psum[:, b, :]

        def PSK(b):
            return [("ps", b)]

        SM = Buf(ar, 0, 10240)
        sm_p = [0]

        def small(nbytes):
            n = (nbytes + 31) // 32 * 32
            b = SM.sub(sm_p[0], nbytes)
            sm_p[0] += n
            assert sm_p[0] <= SM.n
            return b
        B_cst = small(768)
        B_prm = small(NPRM * 4)
        B_cs = Buf(ar, 10240, 8192)
        WB = Ring(ar, 18432, 18432 + 24576)
        WG = Ring(ar, 43008, 43008 + 2048)
        BASE = 45056
        TOP = 211968
        KV0 = 175104

        cst = B_cst.bf()
        ident = cst[:, 0:128]
        ones = cst[:, 128:256]
        rmat = cst[:, 256:384]
        prm = B_prm.f32()
        cs = B_cs.bf()

        def pcol(c, n=1):
            return prm[:, c:c + n]

        T.dma('sp', cst, cst_d[:, :], writes=B_cst.keys())
        T.dma('sp', prm, prm_d[:, :], writes=B_prm.keys())
        T.dma('sp', cs, cs_d[:, :], writes=B_cs.keys())

        b_idf = small(512)
        T.dma('sp', b_idf.f32(), idf_d[:, :], writes=b_idf.keys())
        b_eps = small(4)
        b_gb = small(2048)
        b_grow = small(2048)
        b_one = small(4)
        b_mh = small(4)
        b_th = small(256)
        b_s1 = small(256)
        b_csb = small(128)
        b_modT = small(512)
        b_gx = small(128)
        b_gc = small(128)
        b_qws = small(4)
        b_c05 = small(128)
        b_ba05 = small(128)
        b_bx05 = small(128)
        b_wa05 = small(64)
        b_wl05 = small(64)
        b_stF = small(64)
        b_stB = small(64)
        b_xlB2 = small(128)
        b_ss = small(96)
        b_ms = small(96)
        b_rstd = small(96)
        b_tmpc = small(128)

        T.op('dve', lambda: V.memset(b_eps.f32(), EPS), writes=b_eps.keys())
        T.op('dve', lambda: V.memset(b_one.f32(), 1.0), writes=b_one.keys())
        T.op('dve', lambda: V.memset(b_mh.f32(), -0.5), writes=b_mh.keys())

        cc = pcol(P_CC, 64)
        T.op('act', lambda: S.activation(out=b_th.f32(), in_=cc, func=AF.Tanh, scale=0.5),
             reads=B_prm.keys(), writes=b_th.keys())
        T.op('dve', lambda: V.scalar_tensor_tensor(out=b_s1.f32(), in0=b_th.f32(), scalar=1.0, in1=cc,
                                                   op0=ALU.add, op1=ALU.mult),
             reads=b_th.keys() + B_prm.keys(), writes=b_s1.keys())
        T.op('dve', lambda: V.tensor_scalar(out=b_csb.bf(), in0=b_s1.f32(), scalar1=0.5, scalar2=None, op0=ALU.mult),
             reads=b_s1.keys(), writes=b_csb.keys())
        csb = b_csb.bf()
        T.op('dve', lambda: V.tensor_scalar(out=b_qws.f32(), in0=pcol(P_QW), scalar1=float(128 ** -0.5), scalar2=None,
                                            op0=ALU.mult), reads=B_prm.keys(), writes=b_qws.keys())
        T.op('dve', lambda: V.tensor_scalar(out=b_ba05.f32(), in0=pcol(P_BA, 32), scalar1=0.5, scalar2=None, op0=ALU.mult),
             reads=B_prm.keys(), writes=b_ba05.keys())
        T.op('dve', lambda: V.tensor_scalar(out=b_bx05.f32(), in0=pcol(P_BX, 32), scalar1=0.5, scalar2=None, op0=ALU.mult),
             reads=B_prm.keys(), writes=b_bx05.keys())
        T.op('dve', lambda: V.tensor_scalar(out=b_wa05.f32(), in0=pcol(P_WNA, 16), scalar1=0.5, scalar2=None, op0=ALU.mult),
             reads=B_prm.keys(), writes=b_wa05.keys())
        T.op('dve', lambda: V.tensor_scalar(out=b_wl05.f32(), in0=pcol(P_WNL, 16), scalar1=0.5, scalar2=None, op0=ALU.mult),
             reads=B_prm.keys(), writes=b_wl05.keys())
        T.op('act', lambda: S.activation(out=b_tmpc.f32(), in_=pcol(P_LAM, 32), func=AF.Exp, scale=-1.0),
             reads=B_prm.keys(), writes=b_tmpc.keys())
        T.op('act', lambda: S.activation(out=b_tmpc.f32(), in_=b_tmpc.f32(), func=AF.Ln, bias=b_one.f32(), scale=1.0),
             reads=b_tmpc.keys() + b_one.keys(), writes=b_tmpc.keys())
        T.op('dve', lambda: V.tensor_scalar(out=b_c05.f32(), in0=b_tmpc.f32(), scalar1=-4.0, scalar2=None, op0=ALU.mult),
             reads=b_tmpc.keys(), writes=b_c05.keys())

        def dump(name, ap, shape, dt=F32, reads=()):
            if not dbg:
                return
            d = nc.dram_tensor("dbg_" + name, list(shape), dt, kind="ExternalOutput").ap()
            dbg_outs[name] = d
            T.dma('sp', d, ap, reads=list(reads))

        def finish():
            T.barrier(engines=('sp',))

        wsched = []
        for g in range(4):
            wsched.append(('in', C_K + g * 128))
        for j in range(4):
            wsched.append(('in', C_V + j * 128))
        wsched.append(('in', C_XL))
        for n in range(16):
            if n + 1 < 16:
                wsched.append(('in', C_XL + (n + 1) * 128))
            wsched.append(('ada', 2 * D + (2 * n) * 128))
            wsched.append(('ada', 2 * D + (2 * n + 1) * 128))
        for g in range(4):
            wsched.append(('in', C_K + g * 128))
        for j in range(4):
            wsched.append(('in', C_V + j * 128))
        for h in range(16):
            wsched.append(('in', C_Q + h * 128))
            wsched.append(('in', C_GA + h * 128))
        for n in range(16):
            wsched.append(('in', C_XL + n * 128))
            wsched.append(('in', C_GL + n * 128))
        wstate = dict(issued=0, used=0, tiles={})

        def w_issue():
            i = wstate['issued']
            if i >= len(wsched):
                return
            kind, col = wsched[i]
            b = WB.alloc(8192)
            src = (win_d if kind == 'in' else wada_d)[:, col:col + 128].rearrange("(kc p) n -> p kc n", p=128)
            dst = b.bf().rearrange("p (kc n) -> p kc n", kc=KC)
            T.dma('pool', dst, src, writes=b.keys())
            wstate['tiles'][i] = b
            wstate['issued'] += 1

        def w_next(expect_col, kind='in'):
            i = wstate['used']
            assert wsched[i] == (kind, expect_col), (i, wsched[i], expect_col)
            while wstate['issued'] < min(i + 3, len(wsched)):
                w_issue()
            b = wstate['tiles'].pop(i)
            wstate['used'] += 1
            return b

        R0 = Ring(ar, BASE, BASE + 2 * 32768 + 16384)
        tp_bank = 7
        modps = PS(tp_bank)
        for cg in range(16):
            wt = R0.alloc(32768)
            wv = wt.bf().rearrange("p (kc n) -> p kc n", kc=KC)
            for q4 in range(4):
                src = wada_d[q4 * 1024:(q4 + 1) * 1024, cg * 512:(cg + 1) * 512].rearrange("(kc p) n -> p kc n", p=128)
                T.dma('pool', wv[:, q4 * 8:(q4 + 1) * 8, :], src, writes=wt.sub(q4 * 8192, 8192).keys())
            bb = R0.alloc(2048)
            T.dma('sp', bb.f32()[0:2, :], bada_d[:, cg * 512:(cg + 1) * 512], writes=bb.keys())
            bank = cg % 2
            for kc in range(KC):
                T.op('pe', lambda kc=kc: PE.matmul(PS(bank)[0:2, :], lhsT=csb[:, 2 * kc:2 * kc + 2], rhs=wv[:, kc, :],
                                                   start=(kc == 0), stop=(kc == KC - 1)),
                     reads=b_csb.keys() + wt.sub(kc * 1024, 1024).keys(), writes=PSK(bank), inc=(kc == KC - 1))
            row = R0.alloc(2048)
            T.op('dve', lambda: V.tensor_tensor(out=row.f32()[0:2, :], in0=PS(bank)[0:2, :], in1=bb.f32()[0:2, :], op=ALU.add),
                 reads=PSK(bank) + bb.keys(), writes=row.keys())
            if cg < 16:
                for i in range(4):
                    j = cg * 4 + i
                    T.op('pe', lambda i=i, j=j: PE.transpose(out=modps[:, 2 * j:2 * j + 2], in_=row.f32()[0:2, i * 128:(i + 1) * 128],
                                                             identity=prm[0:2, P_ID2:P_ID2 + 2]),
                         reads=row.keys() + B_prm.keys(), writes=PSK(tp_bank), inc=(i == 3))
            else:
                T.dma('sp', gate_d[0:1, (cg - 16) * 512:(cg - 15) * 512], row.f32()[0:1, :], reads=row.keys(), writes=["gate_row"])
        T.op('dve', lambda: V.tensor_copy(out=b_modT.f32(), in_=modps[:, 0:128]), reads=PSK(tp_bank), writes=b_modT.keys())
        modR = b_modT.f32().rearrange("p (j r) -> p r j", r=2)
        nw = pcol(P_NW, 32)
        T.op('dve', lambda: V.scalar_tensor_tensor(out=b_gx.f32(), in0=modR[:, 0:1, 32:64].rearrange("p o j -> p (o j)"), scalar=1.0, in1=nw, op0=ALU.add, op1=ALU.mult),
             reads=b_modT.keys() + B_prm.keys(), writes=b_gx.keys())
        T.op('dve', lambda: V.scalar_tensor_tensor(out=b_gc.f32(), in0=modR[:, 1:2, 32:64].rearrange("p o j -> p (o j)"), scalar=1.0, in1=nw, op0=ALU.add, op1=ALU.mult),
             reads=b_modT.keys() + B_prm.keys(), writes=b_gc.keys())

        def gsel(isctx, kc):
            g = (b_gc if isctx else b_gx).f32()[:, kc:kc + 1]
            r = 1 if isctx else 0
            s = b_modT.f32()[:, kc * 2 + r:kc * 2 + r + 1]
            return g, s
        if dbg:
            dump("modT", b_modT.f32(), [128, 128], reads=b_modT.keys())
            dump("gx", b_gx.f32(), [128, 32], reads=b_gx.keys())
            dump("c05", b_c05.f32(), [128, 32], reads=b_c05.keys())
        if stop_after == 0:
            finish()
            return nc, dbg_outs
        T.barrier()

        def prep(groups, hT, RP):
            hv = hT.bf().rearrange("p (kc t) -> p kc t", kc=KC)
            ntok = hT.n // 2 // KC
            tpb = Rot([4, 5, 6, 7])
            junk = Buf(ar, RP.end, 8192)
            for grp in groups:
                xss = []
                for (rows, isctx, si, tok0) in grp:
                    xt = RP.alloc(16384)
                    T.dma('sp', xt.f32(), rows, writes=xt.keys())
                    ssa_ = b_ss.f32()[:, si:si + 1]
                    T.op('act', lambda xt=xt, junk=junk, ssa_=ssa_: S.activation(out=junk.bf(), in_=xt.f32(), func=AF.Square, accum_out=ssa_),
                         reads=xt.keys(), writes=junk.keys() + b_ss.keys())
                    msa = b_ms.f32()[:, si:si + 1]
                    T.op('dve', lambda ssa_=ssa_, msa=msa: V.tensor_scalar(out=msa, in0=ssa_, scalar1=1.0 / D, scalar2=EPS, op0=ALU.mult, op1=ALU.add),
                         reads=b_ss.keys(), writes=b_ms.keys())
                    rsa = b_rstd.f32()[:, si:si + 1]
                    T.op('pool', lambda msa=msa, rsa=rsa: G.tensor_tensor(out=rsa, in0=msa, in1=b_mh.f32(), op=ALU.pow),
                         reads=b_ms.keys() + b_mh.keys(), writes=b_rstd.keys())
                    dg = RP.alloc(512)
                    T.op('dve', lambda dg=dg, rsa=rsa: V.tensor_scalar(out=dg.f32(), in0=b_idf.f32(), scalar1=rsa, scalar2=None, op0=ALU.mult),
                         reads=b_rstd.keys() + b_idf.keys(), writes=dg.keys())
                    xss.append((xt, dg, isctx, tok0))
                ng = len(xss)
                tok0 = xss[0][3]
                isctx = xss[0][2]
                for k2 in range(KC // 2):
                    bank = tpb.next()
                    pb = PS(bank)
                    for kk in range(2):
                        kc = k2 * 2 + kk
                        for j, (xt, dg, _, _) in enumerate(xss):
                            T.op('pe', lambda kk=kk, kc=kc, j=j, xt=xt, dg=dg, pb=pb: PE.matmul(
                                pb[:, kk * 256 + j * 128:kk * 256 + (j + 1) * 128],
                                lhsT=xt.f32()[:, kc * 128:(kc + 1) * 128], rhs=dg.f32(), start=True, stop=True),
                                reads=xt.sub(kc * 512, 512).keys() + dg.keys(), writes=PSK(bank),
                                inc=(kk == 1 and j == ng - 1))
                    for kk in range(2):
                        kc = k2 * 2 + kk
                        g, s = gsel(isctx, kc)
                        dst = hv[:, kc, tok0:tok0 + ng * 128]
                        dkeys = hT.sub((kc * ntok + tok0) * 2, ng * 256).keys()
                        src = pb[:, kk * 256:kk * 256 + ng * 128]
                        if kk % 2 == 0:
                            T.op('dve', lambda dst=dst, src=src, g=g, s=s: V.tensor_scalar(out=dst, in0=src, scalar1=g, scalar2=s,
                                                                                      op0=ALU.mult, op1=ALU.add),
                                 reads=PSK(bank) + b_gx.keys() + b_gc.keys() + b_modT.keys(), writes=dkeys)
                        else:
                            T.op('act', lambda dst=dst, src=src, g=g, s=s: S.activation(out=dst, in_=src, func=AF.Identity, scale=g, bias=s),
                                 reads=PSK(bank) + b_gx.keys() + b_gc.keys() + b_modT.keys(), writes=dkeys)

        def hkeys(hT, ntok, kc, t0, n):
            return hT.sub((kc * ntok + t0) * 2, n * 2).keys()

        def proj(wt, hT, ntok, t0, n, bank):
            hv = hT.bf().rearrange("p (kc t) -> p kc t", kc=KC)
            wv = wt.bf().rearrange("p (kc n) -> p kc n", kc=KC)
            for kc in range(KC):
                T.op('pe', lambda kc=kc: PE.matmul(PS(bank)[:, 0:n], lhsT=wv[:, kc, :], rhs=hv[:, kc, t0:t0 + n],
                                                   start=(kc == 0), stop=(kc == KC - 1)),
                     reads=wt.sub(kc * 256, 256).keys() + hkeys(hT, ntok, kc, t0, n), writes=PSK(bank), inc=(kc == KC - 1))

        def proj_tm(wt, hT, ntok, t0, bank, c0):
            hv = hT.bf().rearrange("p (kc t) -> p kc t", kc=KC)
            wv = wt.bf().rearrange("p (kc n) -> p kc n", kc=KC)
            for kc in range(KC):
                T.op('pe', lambda kc=kc: PE.matmul(PS(bank)[:, c0:c0 + 128], lhsT=hv[:, kc, t0:t0 + 128], rhs=wv[:, kc, :],
                                                   start=(kc == 0), stop=(kc == KC - 1)),
                     reads=wt.sub(kc * 256, 256).keys() + hkeys(hT, ntok, kc, t0, 128), writes=PSK(bank), inc=(kc == KC - 1))

        accb = Rot([0, 1])
        auxb = Rot([6, 7])

        def rope_mul(out_b, src_fn, src_keys, pcol0, cs0, n):
            r0, nr = cs0 // 64, n // 64
            for ph in range(2):
                lo = ph * 64
                if ph == 0:
                    tab = prm[lo:lo + 64, pcol0 + r0:pcol0 + r0 + nr].unsqueeze(2).broadcast_to([64, nr, 64])
                else:
                    tab = prm[lo:lo + 64, pcol0:pcol0 + 64].unsqueeze(1).broadcast_to([64, nr, 64])
                o3 = out_b.f32()[lo:lo + 64, 0:n].rearrange("p (r c) -> p r c", c=64)
                i3 = src_fn(lo).rearrange("p (r c) -> p r c", c=64)
                T.op('dve', lambda o3=o3, i3=i3, tab=tab: V.tensor_tensor(out=o3, in0=i3, in1=tab, op=ALU.mult),
                     reads=src_keys + B_prm.keys(), writes=out_b.keys())

        def normrope(bank, n, w_ap, w_keys, cs0, dst, dkeys, RR):
            sq = RR.alloc(n * 2)
            T.op('act', lambda: S.activation(out=sq.bf(), in_=PS(bank)[:, 0:n], func=AF.Square), reads=PSK(bank), writes=sq.keys())
            ab = auxb.next()
            T.op('pe', lambda: PE.matmul(PS(ab)[:, 0:n], lhsT=ones, rhs=sq.bf(), start=True, stop=True),
                 reads=sq.keys() + B_cst.keys(), writes=PSK(ab))
            lnb = RR.alloc(n * 4)
            T.op('act', lambda: S.activation(out=lnb.f32(), in_=PS(ab)[:, 0:n], func=AF.Ln, scale=1.0 / 128, bias=b_eps.f32()),
                 reads=PSK(ab) + b_eps.keys(), writes=lnb.keys())
            T.op('act', lambda: S.activation(out=lnb.f32(), in_=lnb.f32(), func=AF.Exp, scale=-0.5), reads=lnb.keys(), writes=lnb.keys())
            if cs0 is None:
                T.op('dve', lambda: V.scalar_tensor_tensor(out=dst, in0=PS(bank)[:, 0:n], scalar=w_ap, in1=lnb.f32(), op0=ALU.mult, op1=ALU.mult),
                     reads=PSK(bank) + lnb.keys() + w_keys, writes=dkeys)
                return
            qn = RR.alloc(n * 2)
            T.op('dve', lambda: V.scalar_tensor_tensor(out=qn.bf(), in0=PS(bank)[:, 0:n], scalar=w_ap, in1=lnb.f32(), op0=ALU.mult, op1=ALU.mult),
                 reads=PSK(bank) + lnb.keys() + w_keys, writes=qn.keys())
            ab2 = auxb.next()
            T.op('pe', lambda: PE.matmul(PS(ab2)[:, 0:n], lhsT=rmat, rhs=qn.bf(), start=True, stop=True),
                 reads=qn.keys() + B_cst.keys(), writes=PSK(ab2))
            t1 = RR.alloc(n * 4)
            rope_mul(t1, lambda lo: qn.bf()[lo:lo + 64, 0:n], qn.keys(), P_COS, cs0, n)
            t2 = RR.alloc(n * 4)
            rope_mul(t2, lambda lo: PS(ab2)[lo:lo + 64, 0:n], PSK(ab2), P_SIN, cs0, n)
            T.op('dve', lambda: V.tensor_tensor(out=dst, in0=t1.f32(), in1=t2.f32(), op=ALU.add),
                 reads=t1.keys() + t2.keys(), writes=dkeys)

        B_kT = Buf(ar, KV0, 18432)
        B_v = Buf(ar, KV0 + 18432, 18432)
        kTv = B_kT.bf().rearrange("p (g t) -> p g t", g=4)
        vv = B_v.bf().rearrange("p (t c) -> p t c", t=18)

        def kkeys(g, t0, n):
            return B_kT.sub((g * NKEY + t0) * 2, n * 2).keys()

        def vkeys(t, c0, n):
            return B_v.sub((t * 512 + c0) * 2, n * 2).keys()

        def wg_load(n):
            b = WG.alloc(1024)
            T.dma('pool', b.bf(), wg_d[:, n * 512:(n + 1) * 512], writes=b.keys())
            return b

        def gates(wgb, n, d, ub_ap, ub_keys, u_ap, u_keys, a_b, x_b, m_b, nt):
            za = accb.next()
            T.op('pe', lambda: PE.matmul(PS(za)[:, 0:nt], lhsT=wgb.bf()[:, (d * 2 + 0) * 128:(d * 2 + 1) * 128], rhs=ub_ap, start=True, stop=True),
                 reads=wgb.keys() + ub_keys, writes=PSK(za))
            zx = accb.next()
            T.op('pe', lambda: PE.matmul(PS(zx)[:, 0:nt], lhsT=wgb.bf()[:, (d * 2 + 1) * 128:(d * 2 + 2) * 128], rhs=ub_ap, start=True, stop=True),
                 reads=wgb.keys() + ub_keys, writes=PSK(zx))
            idx = d * 16 + n
            T.op('act', lambda: S.activation(out=a_b.f32(), in_=PS(za)[:, 0:nt], func=AF.Tanh, scale=0.5, bias=b_ba05.f32()[:, idx:idx + 1]),
                 reads=PSK(za) + b_ba05.keys(), writes=a_b.keys())
            T.op('act', lambda: S.activation(out=x_b.f32(), in_=PS(zx)[:, 0:nt], func=AF.Tanh, scale=0.5, bias=b_bx05.f32()[:, idx:idx + 1]),
                 reads=PSK(zx) + b_bx05.keys(), writes=x_b.keys())
            c05 = b_c05.f32()[:, idx:idx + 1]
            T.op('act', lambda: S.activation(out=a_b.f32(), in_=a_b.f32(), func=AF.Exp, scale=c05, bias=c05),
                 reads=a_b.keys() + b_c05.keys(), writes=a_b.keys())
            T.op('dve', lambda: V.tensor_tensor(out=m_b.f32(), in0=a_b.f32(), in1=a_b.f32(), op=ALU.mult), reads=a_b.keys(), writes=m_b.keys())
            T.op('act', lambda: S.activation(out=m_b.f32(), in_=m_b.f32(), func=AF.Ln, scale=-1.0, bias=b_one.f32()),
                 reads=m_b.keys() + b_one.keys(), writes=m_b.keys())
            T.op('act', lambda: S.activation(out=m_b.f32(), in_=m_b.f32(), func=AF.Exp, scale=0.5), reads=m_b.keys(), writes=m_b.keys())
            T.op('dve', lambda: V.scalar_tensor_tensor(out=x_b.f32(), in0=x_b.f32(), scalar=1.0, in1=m_b.f32(), op0=ALU.add, op1=ALU.mult),
                 reads=x_b.keys() + m_b.keys(), writes=x_b.keys())
            T.op('dve', lambda: V.scalar_tensor_tensor(out=x_b.f32(), in0=x_b.f32(), scalar=0.5, in1=u_ap, op0=ALU.mult, op1=ALU.mult),
                 reads=x_b.keys() + u_keys, writes=x_b.keys())

        accb4 = Rot([0, 1, 2, 3])

        def gate_mm_tanh(wgb, n, d, ub_ap, ub_keys, a_ap, a_keys, x_ap, x_keys, nt):
            za = accb4.next()
            T.op('pe', lambda: PE.matmul(PS(za)[:, 0:nt], lhsT=wgb.bf()[:, (d * 2 + 0) * 128:(d * 2 + 1) * 128], rhs=ub_ap, start=True, stop=True),
                 reads=wgb.keys() + ub_keys, writes=PSK(za))
            zx = accb4.next()
            T.op('pe', lambda: PE.matmul(PS(zx)[:, 0:nt], lhsT=wgb.bf()[:, (d * 2 + 1) * 128:(d * 2 + 2) * 128], rhs=ub_ap, start=True, stop=True),
                 reads=wgb.keys() + ub_keys, writes=PSK(zx))
            idx = d * 16 + n
            T.op('act', lambda: S.activation(out=a_ap, in_=PS(za)[:, 0:nt], func=AF.Tanh, scale=0.5, bias=b_ba05.f32()[:, idx:idx + 1]),
                 reads=PSK(za) + b_ba05.keys(), writes=a_keys)
            T.op('act', lambda: S.activation(out=x_ap, in_=PS(zx)[:, 0:nt], func=AF.Tanh, scale=0.5, bias=b_bx05.f32()[:, idx:idx + 1]),
                 reads=PSK(zx) + b_bx05.keys(), writes=x_keys)

        def gate_exp(n, d, a_ap, a_keys):
            idx = d * 16 + n
            c05 = b_c05.f32()[:, idx:idx + 1]
            T.op('act', lambda: S.activation(out=a_ap, in_=a_ap, func=AF.Exp, scale=c05, bias=c05),
                 reads=a_keys + b_c05.keys(), writes=a_keys)

        def gate_mult(a_b, x_b, m_b):
            T.op('dve', lambda: V.tensor_tensor(out=m_b.f32(), in0=a_b.f32(), in1=a_b.f32(), op=ALU.mult), reads=a_b.keys(), writes=m_b.keys())
            T.op('act', lambda: S.activation(out=m_b.f32(), in_=m_b.f32(), func=AF.Ln, scale=-1.0, bias=b_one.f32()),
                 reads=m_b.keys() + b_one.keys(), writes=m_b.keys())
            T.op('act', lambda: S.activation(out=m_b.f32(), in_=m_b.f32(), func=AF.Exp, scale=0.5), reads=m_b.keys(), writes=m_b.keys())
            T.op('dve', lambda: V.scalar_tensor_tensor(out=x_b.f32(), in0=x_b.f32(), scalar=1.0, in1=m_b.f32(), op0=ALU.add, op1=ALU.mult),
                 reads=x_b.keys() + m_b.keys(), writes=x_b.keys())

        def rev(ap2d):
            n = ap2d.shape[1]
            return bass.AP(ap2d.tensor, ap2d.offset + (n - 1), [list(ap2d.ap[0]), [-1, n]])

        def conv5(n, xlp, o0, u, nt):
            w5 = pcol(P_W5 + n * 5, 5)
            xf = xlp.f32()
            T.op('dve', lambda: V.tensor_scalar(out=u.f32()[:, 0:nt], in0=xf[:, o0:o0 + nt], scalar1=w5[:, 0:1], scalar2=pcol(P_CB + n),
                                                op0=ALU.mult, op1=ALU.add),
                 reads=xlp.keys() + B_prm.keys(), writes=u.keys())
            for o in range(1, 5):
                T.op('dve', lambda o=o: V.scalar_tensor_tensor(out=u.f32()[:, 0:nt], in0=xf[:, o0 + o:o0 + o + nt], scalar=w5[:, o:o + 1],
                                                               in1=u.f32()[:, 0:nt], op0=ALU.mult, op1=ALU.add),
                     reads=xlp.keys() + u.keys() + B_prm.keys(), writes=u.keys())

        hT1 = Buf(ar, BASE, T1 * KC * 2)
        RP1 = Ring(ar, BASE + hT1.n, BASE + hT1.n + 49152)
        groups = []
        xrow = lambda r0: xs_d[r0:r0 + 128, :]
        groups.append([(xrow(NA - NH), False, 0, 0), (xrow(NA), False, 1, 128)])
        for i in range(3):
            groups.append([(xrow(NA + 128 * (2 * i + 1)), False, 2 + 2 * i, 128 * (2 * i + 2)),
                           (xrow(NA + 128 * (2 * i + 2)), False, 3 + 2 * i, 128 * (2 * i + 3))])
        groups.append([(xrow(NA + 128 * 7), False, 8, 128 * 8)])
        groups.append([(cx_d[0:128, :], True, 9, 1152), (cx_d[128:256, :], True, 10, 1280)])
        w_issue()
        w_issue()
        prep(groups, hT1, RP1)
        if dbg:
            dump("hT1", hT1.bf(), [128, KC * T1], BF16, reads=hT1.keys())
        if stop_after == 1:
            finish()
            return nc, dbg_outs
        T.barrier()
        R1 = Ring(ar, BASE + hT1.n, KV0)
        def pipelined(units):
            prev = None
            for (pf, qf) in units:
                pf()
                if prev is not None:
                    prev()
                prev = qf
            if prev is not None:
                prev()

        kunits = []
        for g in range(4):
            wtb = {}
            for ui, (t0, n, k0, roped) in enumerate(((128, 512, NA, True), (640, 512, NA + 512, True), (1152, 256, NA + NB, False))):
                stt = {}

                def pf(g=g, ui=ui, t0=t0, n=n, wtb=wtb, stt=stt):
                    if ui == 0:
                        wtb['wt'] = w_next(C_K + g * 128)
                    stt['bank'] = accb.next()
                    proj(wtb['wt'], hT1, T1, t0, n, stt['bank'])

                def qf(g=g, n=n, k0=k0, roped=roped, stt=stt):
                    normrope(stt['bank'], n, pcol(P_KW), B_prm.keys(), (k0 if roped else None), kTv[:, g, k0:k0 + n], kkeys(g, k0, n), R1)
                kunits.append((pf, qf))
        pipelined(kunits)
        for j in range(4):
            wt = w_next(C_V + j * 128)
            for tg in range(3):
                tl = list(range(tg * 4, min(tg * 4 + 4, 10)))
                bank = accb.next()
                for i, t in enumerate(tl):
                    proj_tm(wt, hT1, T1, 128 + t * 128, bank, i * 128)
                nt = len(tl)
                kt0 = 8 + tl[0]
                dst = vv[:, kt0:kt0 + nt, j * 128:(j + 1) * 128]
                dk = []
                for t in tl:
                    dk += vkeys(8 + t, j * 128, 128)
                T.op('act', lambda dst=dst, bank=bank, nt=nt: S.activation(out=dst, in_=PS(bank)[:, 0:nt * 128].rearrange("p (t c) -> p t c", t=nt), func=AF.Copy),
                     reads=PSK(bank), writes=dk)
        if dbg:
            dump("kT", B_kT.bf(), [128, 4 * NKEY], BF16, reads=B_kT.keys())
            dump("v", B_v.bf(), [128, 18 * 512], BF16, reads=B_v.keys())
        R1B = BASE + hT1.n
        XLP1 = 5376
        xlp1 = [Buf(ar, R1B + i * XLP1, XLP1) for i in range(2)]
        ub1 = [Buf(ar, R1B + 2 * XLP1 + i * 2560, 2560) for i in range(2)]
        o_ = R1B + 2 * XLP1 + 5120
        u1 = Buf(ar, o_, 5120)
        a1 = Buf(ar, o_ + 5120, 6144)
        x1 = Buf(ar, o_ + 5120 + 6144, 6144)
        m1 = Buf(ar, o_ + 5120 + 2 * 6144, 6144)
        assert o_ + 5120 + 3 * 6144 <= KV0

        def p1_A(n):
            wt = w_next(C_XL + n * 128)
            wgb = wg_load(n)
            xlp = xlp1[n % 2]
            xf = xlp.f32()
            T.op('dve', lambda: V.memset(xf[:, 1026:1030], 0.0), writes=xlp.keys())
            T.op('dve', lambda: V.memset(xf[:, 1286:1288], 0.0), writes=xlp.keys())
            for (t0, nn) in ((0, 512), (512, 512), (1024, 384)):
                bank = accb4.next()
                proj(wt, hT1, T1, t0, nn, bank)
                if t0 == 0:
                    T.op('act', lambda bank=bank: S.activation(out=xf[:, 0:386], in_=PS(bank)[:, 126:512], func=AF.Copy),
                         reads=PSK(bank), writes=xlp.keys())
                elif t0 == 512:
                    T.op('act', lambda bank=bank: S.activation(out=xf[:, 386:898], in_=PS(bank)[:, 0:512], func=AF.Copy),
                         reads=PSK(bank), writes=xlp.keys())
                else:
                    T.op('act', lambda bank=bank: S.activation(out=xf[:, 898:1026], in_=PS(bank)[:, 0:128], func=AF.Copy),
                         reads=PSK(bank), writes=xlp.keys())
                    T.op('act', lambda bank=bank: S.activation(out=xf[:, 1030:1286], in_=PS(bank)[:, 128:384], func=AF.Copy),
                         reads=PSK(bank), writes=xlp.keys())
            T.op('dve', lambda n=n: V.tensor_copy(out=b_xlB2.f32()[:, 2 * n:2 * n + 2], in_=xf[:, 2:4]),
                 reads=xlp.keys(), writes=b_xlB2.keys())
            conv5(n, xlp, 0, u1.sub(0, 4096), 1024)
            conv5(n, xlp, 1028, u1.sub(4096, 1024), 256)
            ub = ub1[n % 2]
            T.op('act', lambda: S.activation(out=ub.bf(), in_=u1.f32(), func=AF.Copy), reads=u1.keys(), writes=ub.keys())
            return wgb, ub

        def p1_B(n, wgb, ub):
            ubB = ub.bf()[:, 0:1024]
            ubC = ub.bf()[:, 1024:1280]
            af, xf_ = a1.f32(), x1.f32()
            gate_mm_tanh(wgb, n, 0, ubC, ub.keys(), af[:, 0:256], a1.sub(0, 1024).keys(), xf_[:, 0:256], x1.sub(0, 1024).keys(), 256)
            gate_mm_tanh(wgb, n, 1, ubC, ub.keys(), af[:, 256:512], a1.sub(1024, 1024).keys(), xf_[:, 256:512], x1.sub(1024, 1024).keys(), 256)
            for c in range(2):
                lo = 512 + c * 512
                gate_mm_tanh(wgb, n, 1, ubB[:, c * 512:(c + 1) * 512], ub.keys(), af[:, lo:lo + 512], a1.sub(lo * 4, 2048).keys(),
                             xf_[:, lo:lo + 512], x1.sub(lo * 4, 2048).keys(), 512)
            gate_exp(n, 0, af[:, 0:256], a1.sub(0, 1024).keys())
            gate_exp(n, 1, af[:, 256:1536], a1.sub(1024, 5120).keys())
            gate_mult(a1, x1, m1)
            for (lo, nn, uu) in ((0, 256, ubC), (256, 256, ubC), (512, 1024, ubB)):
                T.op('dve', lambda lo=lo, nn=nn, uu=uu: V.scalar_tensor_tensor(out=xf_[:, lo:lo + nn], in0=xf_[:, lo:lo + nn], scalar=0.5, in1=uu,
                                                                          op0=ALU.mult, op1=ALU.mult),
                     reads=x1.sub(lo * 4, nn * 4).keys() + ub.keys(), writes=x1.sub(lo * 4, nn * 4).keys())
            mf = m1.f32()
            T.op('dve', lambda: V.tensor_tensor_scan(out=mf[:, 0:256], data0=af[:, 0:256], data1=xf_[:, 0:256], initial=0.0, op0=ALU.mult, op1=ALU.add),
                 reads=a1.sub(0, 1024).keys() + x1.sub(0, 1024).keys(), writes=m1.sub(0, 1024).keys())
            T.op('dve', lambda n=n: V.tensor_copy(out=b_stF.f32()[:, n:n + 1], in_=mf[:, 255:256]), reads=m1.sub(0, 1024).keys(), writes=b_stF.keys())
            T.op('dve', lambda: V.tensor_tensor_scan(out=rev(mf[:, 256:512]), data0=rev(af[:, 256:512]), data1=rev(xf_[:, 256:512]), initial=0.0,
                                                     op0=ALU.mult, op1=ALU.add),
                 reads=a1.sub(1024, 1024).keys() + x1.sub(1024, 1024).keys(), writes=m1.sub(1024, 1024).keys())
            T.op('dve', lambda: V.tensor_tensor_scan(out=rev(mf[:, 512:1536]), data0=rev(af[:, 512:1536]), data1=rev(xf_[:, 512:1536]),
                                                     initial=mf[:, 256:257], op0=ALU.mult, op1=ALU.add),
                 reads=a1.sub(2048, 4096).keys() + x1.sub(2048, 4096).keys() + m1.sub(1024, 1024).keys(), writes=m1.sub(2048, 4096).keys())
            T.op('dve', lambda n=n: V.tensor_copy(out=b_stB.f32()[:, n:n + 1], in_=mf[:, 512:513]), reads=m1.sub(2048, 4096).keys(), writes=b_stB.keys())

        GBANK = 5

        def gate_tile(j):
            wt = w_next(2 * D + j * 128, 'ada')
            wv = wt.bf().rearrange("p (kc n) -> p kc n", kc=KC)
            q = j % 4
            for kc in range(KC):
                T.op('pe', lambda kc=kc: PE.matmul(PS(GBANK)[0:2, q * 128:(q + 1) * 128], lhsT=csb[:, 2 * kc:2 * kc + 2], rhs=wv[:, kc, :],
                                                   start=(kc == 0), stop=(kc == KC - 1)),
                     reads=b_csb.keys() + wt.sub(kc * 256, 256).keys(), writes=PSK(GBANK), inc=(kc == KC - 1))
            if q == 3:
                grp = j // 4
                T.dma('sp', b_gb.f32()[0:2, :], bada_d[:, 2 * D + grp * 512:2 * D + (grp + 1) * 512], writes=b_gb.keys())
                T.op('dve', lambda: V.tensor_tensor(out=b_grow.f32()[0:2, :], in0=PS(GBANK)[0:2, :], in1=b_gb.f32()[0:2, :], op=ALU.add),
                     reads=PSK(GBANK) + b_gb.keys(), writes=b_grow.keys())
                T.dma('sp', gate_d[0:1, grp * 512:(grp + 1) * 512], b_grow.f32()[0:1, :], reads=b_grow.keys(), writes=["gate_row"])

        pend = p1_A(0)
        for n in range(16):
            nxt = p1_A(n + 1) if n + 1 < 16 else None
            p1_B(n, *pend)
            gate_tile(2 * n)
            gate_tile(2 * n + 1)
            pend = nxt
        if dbg:
            dump("stF", b_stF.f32(), [128, 16], reads=b_stF.keys())
            dump("stB", b_stB.f32(), [128, 16], reads=b_stB.keys())
        if stop_after == 2:
            finish()
            return nc, dbg_outs
        T.barrier()

        hTA = Buf(ar, BASE, NA * KC * 2)
        RPA = Ring(ar, BASE + hTA.n, BASE + hTA.n + 49152)
        groups = []
        for i in range(4):
            groups.append([(xrow(256 * i), False, 11 + 2 * i, 256 * i), (xrow(256 * i + 128), False, 12 + 2 * i, 256 * i + 128)])
        prep(groups, hTA, RPA)
        T.barrier()
        B_Ua = Buf(ar, BASE + hTA.n, 32768)
        Uav = B_Ua.bf().rearrange("p (h t) -> p h t", h=16)
        R2BASE = BASE + hTA.n + 32768
        b_ssa = Buf(ar, R2BASE, 4096)
        QTB = [Buf(ar, R2BASE + 4096 + i * 1024, 1024) for i in range(4)]
        GWB = [Buf(ar, R2BASE + 8192 + i * 1024, 1024) for i in range(4)]
        R2 = Ring(ar, R2BASE + 12288 + 6144, KV0)
        kunits = []
        for g in range(4):
            wtb = {}
            for qb in range(2):
                stt = {}

                def pf(g=g, qb=qb, wtb=wtb, stt=stt):
                    if qb == 0:
                        wtb['wt'] = w_next(C_K + g * 128)
                    stt['bank'] = accb.next()
                    proj(wtb['wt'], hTA, NA, qb * 512, 512, stt['bank'])

                def qf(g=g, qb=qb, stt=stt):
                    normrope(stt['bank'], 512, pcol(P_KW), B_prm.keys(), qb * 512, kTv[:, g, qb * 512:(qb + 1) * 512], kkeys(g, qb * 512, 512), R2)
                kunits.append((pf, qf))
        pipelined(kunits)
        for j in range(4):
            wt = w_next(C_V + j * 128)
            for tg in range(2):
                bank = accb.next()
                for i in range(4):
                    proj_tm(wt, hTA, NA, (tg * 4 + i) * 128, bank, i * 128)
                dst = vv[:, tg * 4:tg * 4 + 4, j * 128:(j + 1) * 128]
                dk = []
                for t in range(tg * 4, tg * 4 + 4):
                    dk += vkeys(t, j * 128, 128)
                T.op('act', lambda dst=dst, bank=bank: S.activation(out=dst, in_=PS(bank)[:, 0:512].rearrange("p (t c) -> p t c", t=4), func=AF.Copy),
                     reads=PSK(bank), writes=dk)
        if dbg:
            dump("kT2", B_kT.bf(), [128, 4 * NKEY], BF16, reads=B_kT.keys())
            dump("v2", B_v.bf(), [128, 18 * 512], BF16, reads=B_v.keys())
        sb_rot = Rot([2, 3, 6])
        pv_rot = Rot([4])
        auxb.items = [7]
        NRL = [Buf(ar, R2BASE + 12288 + i * 3072, 3072) for i in range(2)]

        def nr_split(bank, n, w_ap, w_keys, cs0, dst_b, lnb, qn):
            sqb = dst_b

            def p1():
                T.op('act', lambda: S.activation(out=sqb.bf(), in_=PS(bank)[:, 0:n], func=AF.Square), reads=PSK(bank), writes=sqb.keys())

            def p2():
                ab = auxb.next()
                T.op('pe', lambda: PE.matmul(PS(ab)[:, 0:n], lhsT=ones, rhs=sqb.bf(), start=True, stop=True),
                     reads=sqb.keys() + B_cst.keys(), writes=PSK(ab))
                T.op('act', lambda: S.activation(out=lnb.f32(), in_=PS(ab)[:, 0:n], func=AF.Ln, scale=1.0 / 128, bias=b_eps.f32()),
                     reads=PSK(ab) + b_eps.keys(), writes=lnb.keys())
                T.op('act', lambda: S.activation(out=lnb.f32(), in_=lnb.f32(), func=AF.Exp, scale=-0.5), reads=lnb.keys(), writes=lnb.keys())
                T.op('dve', lambda: V.scalar_tensor_tensor(out=qn.bf(), in0=PS(bank)[:, 0:n], scalar=w_ap, in1=lnb.f32(), op0=ALU.mult, op1=ALU.mult),
                     reads=PSK(bank) + lnb.keys() + w_keys, writes=qn.keys())

            def p3():
                ab2 = auxb.next()
                T.op('pe', lambda: PE.matmul(PS(ab2)[:, 0:n], lhsT=rmat, rhs=qn.bf(), start=True, stop=True),
                     reads=qn.keys() + B_cst.keys(), writes=PSK(ab2))
                t1 = R2.alloc(n * 4)
                rope_mul(t1, lambda lo: qn.bf()[lo:lo + 64, 0:n], qn.keys(), P_COS, cs0, n)
                t2 = R2.alloc(n * 4)
                rope_mul(t2, lambda lo: PS(ab2)[lo:lo + 64, 0:n], PSK(ab2), P_SIN, cs0, n)
                T.op('dve', lambda: V.tensor_tensor(out=dst_b.bf(), in0=t1.f32(), in1=t2.f32(), op=ALU.add),
                     reads=t1.keys() + t2.keys(), writes=dst_b.keys())
            return p1, p2, p3

        def att_A_stages(h):
            st = {}
            qTs = [QTB[(h % 2) * 2 + qb] for qb in range(2)]
            gws = [GWB[(h % 2) * 2 + qb] for qb in range(2)]

            def a1():
                wq = w_next(C_Q + h * 128)
                st['pieces'] = []
                for qb in range(2):
                    bank = accb.next()
                    proj(wq, hTA, NA, qb * 512, 512, bank)
                    pcs = nr_split(bank, 512, b_qws.f32(), b_qws.keys(), qb * 512, qTs[qb], NRL[qb].sub(0, 2048), NRL[qb].sub(2048, 1024))
                    st['pieces'].append(pcs)
                for pcs in st['pieces']:
                    pcs[0]()

            def a2():
                for pcs in st['pieces']:
                    pcs[1]()

            def a3():
                for pcs in st['pieces']:
                    pcs[2]()

            def a4():
                wga = w_next(C_GA + h * 128)
                for qb in range(2):
                    bank = accb.next()
                    proj(wga, hTA, NA, qb * 512, 512, bank)
                    th = R2.alloc(2048)
                    T.op('act', lambda th=th, bank=bank: S.activation(out=th.f32(), in_=PS(bank)[:, :], func=AF.Tanh, scale=0.5), reads=PSK(bank), writes=th.keys())
                    gw = gws[qb]
                    T.op('dve', lambda th=th, gw=gw, bank=bank: V.scalar_tensor_tensor(out=gw.bf(), in0=th.f32(), scalar=1.0, in1=PS(bank)[:, :], op0=ALU.add, op1=ALU.mult),
                         reads=th.keys() + PSK(bank), writes=gw.keys())
            return (qTs, gws), {1: a1, 5: a2, 10: a3, 14: a4}

        def att_B(h, qTs, gws, hooks):
            g = h // 4
            for qb in range(2):
                qT = qTs[qb]
                pvb = pv_rot.next()
                smb = 5
                sbanks = [None] * 18

                def s_mm(kt):
                    sb = sb_rot.next()
                    sbanks[kt] = sb
                    T.op('pe', lambda: PE.matmul(PS(sb)[:, :], lhsT=kTv[:, g, kt * 128:(kt + 1) * 128], rhs=qT.bf(), start=True, stop=True),
                         reads=kkeys(g, kt * 128, 128) + qT.keys(), writes=PSK(sb))
                s_mm(0)
                s_mm(1)
                for kt in range(18):
                    if kt + 2 < 18:
                        s_mm(kt + 2)
                    if qb == 0 and kt in hooks:
                        hooks[kt]()
                    sb = sbanks[kt]
                    pT = R2.alloc(1024)
                    T.op('act', lambda sb=sb, pT=pT: S.activation(out=pT.bf(), in_=PS(sb)[:, :], func=AF.Exp), reads=PSK(sb), writes=pT.keys())
                    T.op('pe', lambda kt=kt, pT=pT: PE.matmul(PS(pvb)[:, :], lhsT=vv[:, kt, g * 128:(g + 1) * 128], rhs=pT.bf(),
                                                              start=(kt == 0), stop=(kt == 17)),
                         reads=vkeys(kt, g * 128, 128) + pT.keys(), writes=PSK(pvb), inc=(kt == 17))
                    T.op('pe', lambda kt=kt, pT=pT: PE.matmul(PS(smb)[:, :], lhsT=ones, rhs=pT.bf(), start=(kt == 0), stop=(kt == 17)),
                         reads=pT.keys() + B_cst.keys(), writes=PSK(smb), inc=(kt == 17))
                smc = R2.alloc(2048)
                T.op('act', lambda smc=smc: S.activation(out=smc.f32(), in_=PS(smb)[:, :], func=AF.Copy), reads=PSK(smb), writes=smc.keys())
                pvc = R2.alloc(2048)
                T.op('act', lambda pvc=pvc: S.activation(out=pvc.f32(), in_=PS(pvb)[:, :], func=AF.Copy), reads=PSK(pvb), writes=pvc.keys())
                rs = R2.alloc(2048)
                T.op('dve', lambda rs=rs, smc=smc: V.reciprocal(out=rs.f32(), in_=smc.f32()), reads=smc.keys(), writes=rs.keys())
                att = R2.alloc(2048)
                T.op('dve', lambda rs=rs, att=att, pvc=pvc: V.tensor_tensor(out=att.f32(), in0=pvc.f32(), in1=rs.f32(), op=ALU.mult),
                     reads=pvc.keys() + rs.keys(), writes=att.keys())
                ssl_ = b_ssa.sub(qb * 2048, 2048)
                if h == 0:
                    T.op('dve', lambda att=att, ssl_=ssl_: V.tensor_tensor(out=ssl_.f32(), in0=att.f32(), in1=att.f32(), op=ALU.mult),
                         reads=att.keys(), writes=ssl_.keys())
                else:
                    sqa = R2.alloc(2048)
                    T.op('dve', lambda att=att, sqa=sqa: V.tensor_tensor(out=sqa.f32(), in0=att.f32(), in1=att.f32(), op=ALU.mult),
                         reads=att.keys(), writes=sqa.keys())
                    T.op('dve', lambda sqa=sqa, ssl_=ssl_: V.tensor_tensor(out=ssl_.f32(), in0=ssl_.f32(), in1=sqa.f32(), op=ALU.add),
                         reads=sqa.keys() + ssl_.keys(), writes=ssl_.keys())
                gw = gws[qb]
                T.op('dve', lambda att=att, gw=gw, qb=qb, h=h: V.scalar_tensor_tensor(out=Uav[:, h, qb * 512:(qb + 1) * 512], in0=att.f32(),
                                                                                    scalar=b_wa05.f32()[:, h:h + 1], in1=gw.bf(),
                                                                                    op0=ALU.mult, op1=ALU.mult),
                     reads=att.keys() + gw.keys() + b_wa05.keys(), writes=B_Ua.sub((h * NA + qb * 512) * 2, 1024).keys())

        cur, hk = att_A_stages(0)
        for k_ in (1, 5, 10, 14):
            hk[k_]()
        for h in range(16):
            if h + 1 < 16:
                nxt, hk = att_A_stages(h + 1)
            else:
                nxt, hk = None, {}
            att_B(h, cur[0], cur[1], hk)
            cur = nxt
        auxb.items = [6, 7]
        rsa = R2.alloc(4096)
        for qb in range(2):
            sqb = R2.alloc(1024)
            T.op('dve', lambda qb=qb, sqb=sqb: V.tensor_copy(out=sqb.bf(), in_=b_ssa.f32()[:, qb * 512:(qb + 1) * 512]),
                 reads=b_ssa.keys(), writes=sqb.keys())
            ab = auxb.next()
            T.op('pe', lambda sqb=sqb, ab=ab: PE.matmul(PS(ab)[:, :], lhsT=ones, rhs=sqb.bf(), start=True, stop=True),
                 reads=sqb.keys() + B_cst.keys(), writes=PSK(ab))
            T.op('act', lambda qb=qb, ab=ab: S.activation(out=rsa.f32()[:, qb * 512:(qb + 1) * 512], in_=PS(ab)[:, :], func=AF.Ln, scale=1.0 / 2048, bias=b_eps.f32()),
                 reads=PSK(ab) + b_eps.keys(), writes=rsa.keys())
        T.op('act', lambda: S.activation(out=rsa.f32(), in_=rsa.f32(), func=AF.Exp, scale=-0.5), reads=rsa.keys(), writes=rsa.keys())
        for h in range(16):
            T.op('dve', lambda h=h: V.tensor_tensor(out=Uav[:, h, :], in0=Uav[:, h, :], in1=rsa.f32(), op=ALU.mult),
                 reads=rsa.keys() + B_Ua.sub(h * 2048, 2048).keys(), writes=B_Ua.sub(h * 2048, 2048).keys())
        if dbg:
            dump("Ua", B_Ua.bf(), [128, 16 * NA], BF16, reads=B_Ua.keys())
        if stop_after == 3:
            finish()
            return nc, dbg_outs
        T.barrier()

        T.dma('sp', ua_d[:, :], B_Ua.bf(), reads=B_Ua.keys(), writes=["ua_scr"])
        T.barrier()
        B_Ul = Buf(ar, KV0, 32768)
        Ulv = B_Ul.bf().rearrange("p (h t) -> p h t", h=16)
        R2A = BASE + hTA.n
        b_ssl = Buf(ar, R2A, 4096)
        o_ = R2A + 4096
        xlpA = [Buf(ar, o_ + i * 4352, 4352) for i in range(2)]
        o_ += 8704
        uA = [Buf(ar, o_ + i * 4096, 4096) for i in range(2)]
        o_ += 8192
        ubA = [Buf(ar, o_ + i * 2048, 2048) for i in range(2)]
        o_ += 4096
        aA = Buf(ar, o_, 8192)
        xA = Buf(ar, o_ + 8192, 8192)
        mA = Buf(ar, o_ + 16384, 8192)
        o_ += 24576
        RS = Ring(ar, o_, o_ + 8192)
        GWL = [Buf(ar, o_ + 8192 + i * 1024, 1024) for i in range(4)]
        assert o_ + 8192 + 4096 <= KV0

        def p2_A(n):
            wxl = w_next(C_XL + n * 128)
            wgb = wg_load(n)
            xlp = xlpA[n % 2]
            xf = xlp.f32()
            T.op('dve', lambda: V.memset(xf[:, 0:2], 0.0), writes=xlp.keys())
            T.op('dve', lambda n=n: V.tensor_copy(out=xf[:, 1026:1028], in_=b_xlB2.f32()[:, 2 * n:2 * n + 2]), reads=b_xlB2.keys(), writes=xlp.keys())
            for qb in range(2):
                bank = accb4.next()
                proj(wxl, hTA, NA, qb * 512, 512, bank)
                T.op('act', lambda qb=qb, bank=bank: S.activation(out=xf[:, 2 + qb * 512:2 + (qb + 1) * 512], in_=PS(bank)[:, :], func=AF.Copy),
                     reads=PSK(bank), writes=xlp.keys())
            u = uA[n % 2]
            conv5(n, xlp, 0, u, 1024)
            ub = ubA[n % 2]
            T.op('act', lambda: S.activation(out=ub.bf(), in_=u.f32(), func=AF.Copy), reads=u.keys(), writes=ub.keys())
            wgl = w_next(C_GL + n * 128)
            gws = []
            for qb in range(2):
                bank = accb4.next()
                proj(wgl, hTA, NA, qb * 512, 512, bank)
                th = RS.alloc(2048)
                T.op('act', lambda th=th, bank=bank: S.activation(out=th.f32(), in_=PS(bank)[:, :], func=AF.Tanh, scale=0.5), reads=PSK(bank), writes=th.keys())
                gw = GWL[(n % 2) * 2 + qb]
                T.op('dve', lambda th=th, gw=gw, bank=bank: V.scalar_tensor_tensor(out=gw.bf(), in0=th.f32(), scalar=1.0, in1=PS(bank)[:, :], op0=ALU.add, op1=ALU.mult),
                     reads=th.keys() + PSK(bank), writes=gw.keys())
                gws.append(gw)
            return wgb, u, ub, gws

        def p2_B(n, wgb, u, ub, gws):
            af, xf_, mf = aA.f32(), xA.f32(), mA.f32()
            for d in range(2):
                for c in range(2):
                    lo = d * 1024 + c * 512
                    gate_mm_tanh(wgb, n, d, ub.bf()[:, c * 512:(c + 1) * 512], ub.keys(), af[:, lo:lo + 512], aA.sub(lo * 4, 2048).keys(),
                                 xf_[:, lo:lo + 512], xA.sub(lo * 4, 2048).keys(), 512)
            for d in range(2):
                gate_exp(n, d, af[:, d * 1024:(d + 1) * 1024], aA.sub(d * 4096, 4096).keys())
            gate_mult(aA, xA, mA)
            for d in range(2):
                T.op('dve', lambda d=d: V.scalar_tensor_tensor(out=xf_[:, d * 1024:(d + 1) * 1024], in0=xf_[:, d * 1024:(d + 1) * 1024], scalar=0.5,
                                                            in1=u.f32(), op0=ALU.mult, op1=ALU.mult),
                     reads=xA.sub(d * 4096, 4096).keys() + u.keys(), writes=xA.sub(d * 4096, 4096).keys())
            T.op('dve', lambda: V.tensor_tensor_scan(out=mf[:, 0:1024], data0=af[:, 0:1024], data1=xf_[:, 0:1024], initial=b_stF.f32()[:, n:n + 1],
                                                     op0=ALU.mult, op1=ALU.add),
                 reads=aA.sub(0, 4096).keys() + xA.sub(0, 4096).keys() + b_stF.keys(), writes=mA.sub(0, 4096).keys())
            T.op('dve', lambda: V.tensor_tensor_scan(out=rev(mf[:, 1024:2048]), data0=rev(af[:, 1024:2048]), data1=rev(xf_[:, 1024:2048]),
                                                     initial=b_stB.f32()[:, n:n + 1], op0=ALU.mult, op1=ALU.add),
                 reads=aA.sub(4096, 4096).keys() + xA.sub(4096, 4096).keys() + b_stB.keys(), writes=mA.sub(4096, 4096).keys())
            lru = mA.sub(0, 4096)
            T.op('dve', lambda: V.tensor_tensor(out=mf[:, 0:1024], in0=mf[:, 0:1024], in1=mf[:, 1024:2048], op=ALU.add),
                 reads=mA.keys(), writes=lru.keys())
            for qb in range(2):
                lr = lru.sub(qb * 2048, 2048)
                ssl_ = b_ssl.sub(qb * 2048, 2048)
                if n == 0:
                    T.op('dve', lambda: V.tensor_tensor(out=ssl_.f32(), in0=lr.f32(), in1=lr.f32(), op=ALU.mult), reads=lr.keys(), writes=ssl_.keys())
                else:
                    sql = RS.alloc(2048)
                    T.op('dve', lambda: V.tensor_tensor(out=sql.f32(), in0=lr.f32(), in1=lr.f32(), op=ALU.mult), reads=lr.keys(), writes=sql.keys())
                    T.op('dve', lambda: V.tensor_tensor(out=ssl_.f32(), in0=ssl_.f32(), in1=sql.f32(), op=ALU.add),
                         reads=sql.keys() + ssl_.keys(), writes=ssl_.keys())
                gw = gws[qb]
                T.op('dve', lambda: V.scalar_tensor_tensor(out=Ulv[:, n, qb * 512:(qb + 1) * 512], in0=lr.f32(), scalar=b_wl05.f32()[:, n:n + 1],
                                                           in1=gw.bf(), op0=ALU.mult, op1=ALU.mult),
                     reads=lr.keys() + gw.keys() + b_wl05.keys(), writes=B_Ul.sub((n * NA + qb * 512) * 2, 1024).keys())

        pend = p2_A(0)
        for n in range(16):
            nxt = p2_A(n + 1) if n + 1 < 16 else None
            p2_B(n, *pend)
            pend = nxt
        rsl = Buf(ar, o_ - 24576, 4096)
        for qb in range(2):
            sqb = RS.alloc(1024)
            T.op('dve', lambda qb=qb, sqb=sqb: V.tensor_copy(out=sqb.bf(), in_=b_ssl.f32()[:, qb * 512:(qb + 1) * 512]),
                 reads=b_ssl.keys(), writes=sqb.keys())
            ab = auxb.next()
            T.op('pe', lambda sqb=sqb, ab=ab: PE.matmul(PS(ab)[:, :], lhsT=ones, rhs=sqb.bf(), start=True, stop=True),
                 reads=sqb.keys() + B_cst.keys(), writes=PSK(ab))
            T.op('act', lambda qb=qb, ab=ab: S.activation(out=rsl.f32()[:, qb * 512:(qb + 1) * 512], in_=PS(ab)[:, :], func=AF.Ln, scale=1.0 / 2048, bias=b_eps.f32()),
                 reads=PSK(ab) + b_eps.keys(), writes=rsl.keys())
        T.op('act', lambda: S.activation(out=rsl.f32(), in_=rsl.f32(), func=AF.Exp, scale=-0.5), reads=rsl.keys(), writes=rsl.keys())
        for n in range(16):
            T.op('dve', lambda n=n: V.tensor_tensor(out=Ulv[:, n, :], in0=Ulv[:, n, :], in1=rsl.f32(), op=ALU.mult),
                 reads=rsl.keys() + B_Ul.sub(n * 2048, 2048).keys(), writes=B_Ul.sub(n * 2048, 2048).keys())
        if dbg:
            dump("Ul", B_Ul.bf(), [128, 16 * NA], BF16, reads=B_Ul.keys())
        if stop_after == 4:
            finish()
            return nc, dbg_outs
        T.barrier()

        WO = Ring(ar, BASE, BASE + 65536)
        B_gate = Buf(ar, BASE + 65536 + 32768, 16384)
        R3 = Ring(ar, B_gate.off + B_gate.n, KV0)
        T.dma('sp', B_Ua.bf(), ua_d[:, :], reads=["ua_scr"], writes=B_Ua.keys())
        T.dma('sp', B_gate.f32(), gate_d[0:1, :].partition_broadcast(128), reads=["gate_row"], writes=B_gate.keys())
        out_toks = []

        def wo_load(cg):
            wt = WO.alloc(32768)
            wv = wt.bf().rearrange("p (kc n) -> p kc n", kc=KC)
            for q4 in range(4):
                src = wout_d[q4 * 1024:(q4 + 1) * 1024, cg * 512:(cg + 1) * 512].rearrange("(kc p) n -> p kc n", p=128)
                T.dma('pool', wv[:, q4 * 8:(q4 + 1) * 8, :], src, writes=wt.sub(q4 * 8192, 8192).keys())
            return wt
        wos = {0: wo_load(0)}
        acc3 = Rot([0, 1, 2, 3])
        for cg in range(8):
            if cg + 1 < 8:
                wos[cg + 1] = wo_load(cg + 1)
            wt = wos.pop(cg)
            wv = wt.bf().rearrange("p (kc n) -> p kc n", kc=KC)
            for tt in range(8):
                xr = R3.alloc(2048)
                T.dma('sp', xr.f32(), xs_d[tt * 128:(tt + 1) * 128, cg * 512:(cg + 1) * 512], writes=xr.keys())
                bank = acc3.next()
                for kc in range(KC):
                    if kc < 16:
                        lhs = Uav[:, kc, tt * 128:(tt + 1) * 128]
                        lk = B_Ua.sub((kc * NA + tt * 128) * 2, 256).keys()
                    else:
                        lhs = Ulv[:, kc - 16, tt * 128:(tt + 1) * 128]
                        lk = B_Ul.sub(((kc - 16) * NA + tt * 128) * 2, 256).keys()
                    T.op('pe', lambda kc=kc, lhs=lhs: PE.matmul(PS(bank)[:, :], lhsT=lhs, rhs=wv[:, kc, :], start=(kc == 0), stop=(kc == KC - 1)),
                         reads=lk + wt.sub(kc * 1024, 1024).keys(), writes=PSK(bank), inc=(kc == KC - 1))
                t_ = R3.alloc(2048)
                T.op('dve', lambda: V.tensor_tensor(out=t_.f32(), in0=PS(bank)[:, :], in1=B_gate.f32()[:, cg * 512:(cg + 1) * 512], op=ALU.mult),
                     reads=PSK(bank) + B_gate.sub(cg * 2048, 2048).keys(), writes=t_.keys())
                T.op('pool', lambda: G.tensor_tensor(out=t_.f32(), in0=t_.f32(), in1=xr.f32(), op=ALU.add),
                     reads=t_.keys() + xr.keys(), writes=t_.keys())
                T.dma('sp', out_d[tt * 128:(tt + 1) * 128, cg * 512:(cg + 1) * 512], t_.f32(), reads=t_.keys(), writes=["out"])
        finish()
    return nc, dbg_outs


def _rope_tables(h):
    pos = np.arange(2048)
    if h == 1:
        pos = 2047 - pos
    row = (pos // 64).astype(np.float32)
    col = (pos % 64).astype(np.float32)
    n_freq = 32
    freqs = (np.float32(10000.0) ** (-np.arange(n_freq, dtype=np.float32) / np.float32(n_freq))).astype(np.float32)
    cos = np.zeros((128, 2048), np.float32)
    sin = np.zeros((128, 2048), np.float32)
    for d in range(128):
        axis = d // 64
        j = d % 32
        ang = (row if axis == 0 else col) * freqs[j]
        cos[d] = np.cos(ang)
        sin[d] = np.sin(ang)
    return cos, sin


def _rope_small(h):
    n_freq = 32
    freqs = (np.float32(10000.0) ** (-np.arange(n_freq, dtype=np.float32) / np.float32(n_freq))).astype(np.float32)
    ctab = np.zeros((128, 64), np.float32)
    stab = np.zeros((128, 64), np.float32)
    for d in range(128):
        j = d % 32
        if d < 64:
            idx = np.arange(32, dtype=np.float32)
            val = idx if h == 0 else (31 - idx)
            ang = (val * freqs[j]).astype(np.float32)
            ctab[d, 0:32] = np.cos(ang)
            stab[d, 0:32] = np.sin(ang)
        else:
            idx = np.arange(64, dtype=np.float32)
            val = idx if h == 0 else (63 - idx)
            ang = (val * freqs[j]).astype(np.float32)
            ctab[d] = np.cos(ang)
            stab[d] = np.sin(ang)
    return ctab, stab


def _consts():
    ident = np.eye(128, dtype=np.float32)
    ones = np.ones((128, 128), np.float32)
    rm = np.zeros((128, 128), np.float32)
    for m in range(128):
        half = (m % 64) // 32
        if half == 0:
            rm[m + 32, m] = -1.0
        else:
            rm[m - 32, m] = 1.0
    return np.concatenate([ident, ones, rm], axis=1).astype(ml_dtypes.bfloat16)


def make_in_maps(x, c, ctx, c_ctx, w_ada, b_ada, norm_w, w_in, q_norm_w, k_norm_w, conv_w, conv_b,
                 lru_wa, lru_ba, lru_wx, lru_bx, lru_lambda, out_norm_att, out_norm_lru, w_out):
    f = np.float32
    w_ada0 = np.ascontiguousarray(np.asarray(w_ada, f)[0])
    w_in0 = np.ascontiguousarray(np.asarray(w_in, f)[0])
    w_out0 = np.ascontiguousarray(np.asarray(w_out, f)[0])
    bada2 = np.ascontiguousarray(np.broadcast_to(np.asarray(b_ada, f)[0][None, :], (2, 3 * D)))
    cst = _consts()
    x = np.asarray(x, f)
    ctx = np.asarray(ctx, f)
    c = np.asarray(c, f)
    c_ctx = np.asarray(c_ctx, f)

    def pl(v, n):
        return np.asarray(v, f).reshape(n, 128).T
    in_maps = []
    for core in range(8):
        b, h = core // 2, core % 2
        xs = x[b]
        cx = ctx[b]
        if h == 1:
            xs = xs[::-1]
            cx = cx[::-1]
        prm = np.zeros((128, NPRM), f)
        prm[:, P_NW:P_NW + 32] = pl(norm_w[0], 32)
        cc = np.zeros((128, 32, 2), f)
        cc[:, :, 0] = pl(c[b], 32)
        cc[:, :, 1] = pl(c_ctx, 32)
        prm[:, P_CC:P_CC + 64] = cc.reshape(128, 64)
        prm[:, P_QW] = np.asarray(q_norm_w, f)[0]
        prm[:, P_KW] = np.asarray(k_norm_w, f)[0]
        cw = np.asarray(conv_w, f)[0]
        w5 = np.zeros((5, 2048), f)
        if h == 0:
            w5[0:4] = cw
        else:
            w5[1:5] = cw[::-1]
        prm[:, P_W5:P_W5 + 80] = w5.reshape(5, 16, 128).transpose(2, 1, 0).reshape(128, 80)
        prm[:, P_CB:P_CB + 16] = pl(np.asarray(conv_b, f)[0], 16)
        dirs = (0, 1) if h == 0 else (1, 0)
        for dd, d in enumerate(dirs):
            prm[:, P_BA + dd * 16:P_BA + dd * 16 + 16] = pl(np.asarray(lru_ba, f)[0, d], 16)
            prm[:, P_BX + dd * 16:P_BX + dd * 16 + 16] = pl(np.asarray(lru_bx, f)[0, d], 16)
            prm[:, P_LAM + dd * 16:P_LAM + dd * 16 + 16] = pl(np.asarray(lru_lambda, f)[0, d], 16)
        prm[:, P_WNA:P_WNA + 16] = pl(np.asarray(out_norm_att, f)[0], 16)
        prm[:, P_WNL:P_WNL + 16] = pl(np.asarray(out_norm_lru, f)[0], 16)
        ctab, stab = _rope_small(h)
        prm[:, P_COS:P_COS + 64] = ctab
        prm[:, P_SIN:P_SIN + 64] = stab
        prm[0, P_ID2] = 1.0
        prm[1, P_ID2 + 1] = 1.0
        wa = np.asarray(lru_wa, f)[0]
        wx = np.asarray(lru_wx, f)[0]
        wg = np.zeros((128, 16, 2, 2, 128), f)
        for dd, d in enumerate(dirs):
            wg[:, :, dd, 0, :] = wa[d].transpose(1, 0, 2)
            wg[:, :, dd, 1, :] = wx[d].transpose(1, 0, 2)
        cos, sin = _rope_tables(h)
        cs = np.concatenate([cos, sin], axis=1).astype(ml_dtypes.bfloat16)
        in_maps.append(dict(xs=np.ascontiguousarray(xs), cx=np.ascontiguousarray(cx), w_ada=w_ada0, w_in=w_in0, w_out=w_out0,
                            prm=prm, bada2=bada2, cst=cst, cs=cs, idf=np.eye(128, dtype=np.float32), wg=np.ascontiguousarray(wg.reshape(128, 16 * 512))))
    return in_maps


_NC_CACHE = {}


def kernel(x, c, ctx, c_ctx, w_ada, b_ada, norm_w, w_in, q_norm_w, k_norm_w, conv_w, conv_b,
           lru_wa, lru_ba, lru_wx, lru_bx, lru_lambda, out_norm_att, out_norm_lru, w_out):
    in_maps = make_in_maps(x, c, ctx, c_ctx, w_ada, b_ada, norm_w, w_in, q_norm_w, k_norm_w, conv_w, conv_b,
                           lru_wa, lru_ba, lru_wx, lru_bx, lru_lambda, out_norm_att, out_norm_lru, w_out)
    if 'nc' not in _NC_CACHE:
        _NC_CACHE['nc'] = build_nc()[0]
    nc = _NC_CACHE['nc']
    res = run_bass_kernel_spmd(nc, in_maps, core_ids=list(range(8)))
    out = np.zeros((4, 2048, D), np.float32)
    for core in range(8):
        b, h = core // 2, core % 2
        o = np.asarray(res.results[core]["out"], np.float32)
        if h == 0:
            out[b, 0:1024] = o
        else:
            out[b, 1024:2048] = o[::-1]
    return out
```
